# Optimizing a Trainium2 kernel written in Bass

```python
import math
import jax, jax.numpy as jnp
from jax import lax
import numpy as np

D_MODEL = 1024
BATCH = 32
SEQ = 2048
DEPTH = 2

GRID_W = 64
CTX_LEN = 256
N_HEADS = 8
N_KV_HEADS = 2
HEAD_DIM = D_MODEL // N_HEADS
Q_PER_KV = N_HEADS // N_KV_HEADS
ROPE_THETA = 10000.0
Q_BLOCK = 128
CONV_WIDTH = D_MODEL // 2
CONV_TAPS = 3
SGU_WIDTH = D_MODEL // 2
SGU_GROUPS = 8
SGU_GROUP_DIM = SGU_WIDTH // SGU_GROUPS
CHUNK = 128
MIX_IN = 3 * CONV_WIDTH + 2 * SGU_WIDTH
MIX_OUT = CONV_WIDTH + SGU_WIDTH
FFN_HIDDEN = -(-8 * D_MODEL // (3 * 256)) * 256
ALPHA = (2.0 * DEPTH) ** 0.25
BETA = (8.0 * DEPTH) ** -0.25
N_EVEN = (DEPTH + 1) // 2
N_ODD = DEPTH // 2
LN_EPS = 1e-5
RMS_EPS = 1e-6

kernel_name = "hybrid_conv_sgu_gqa_prefix_deepnorm"


def _layernorm(x, g, b):
    xf = x.astype(jnp.float32)
    mu = jnp.mean(xf, axis=-1, keepdims=True)
    var = jnp.mean(jnp.square(xf - mu), axis=-1, keepdims=True)
    y = (xf - mu) * lax.rsqrt(var + LN_EPS) * g.astype(jnp.float32) + b.astype(jnp.float32)
    return y.astype(x.dtype)


def _rmsnorm(x, g):
    xf = x.astype(jnp.float32)
    y = xf * lax.rsqrt(jnp.mean(jnp.square(xf), axis=-1, keepdims=True) + RMS_EPS) * g.astype(jnp.float32)
    return y.astype(x.dtype)


def _modulate(h, shift, scale):
    return h * (1.0 + scale) + shift


def _post_norm(x, y, gate, g, b):
    return _layernorm(ALPHA * x + gate * y, g, b)


def _swiglu(h, w_in, w_out):
    gt, up = jnp.split(h @ w_in, 2, axis=-1)
    return (jax.nn.silu(gt) * up) @ w_out


def _sgu(u, v, ln_g, ln_b, w_s, b_s):
    bsz, n, _ = v.shape
    v = _layernorm(v, ln_g, ln_b).reshape(bsz, n // CHUNK, CHUNK, SGU_GROUPS, SGU_GROUP_DIM)
    s = jnp.einsum('gpq,bcqgd->bcpgd', w_s, v) + b_s.T[None, None, :, :, None]
    return u * s.reshape(bsz, n, SGU_WIDTH)


def _conv_chunk_mixer(h, w_in, conv_w, sgu_ln_g, sgu_ln_b, sgu_w, sgu_b, w_out):
    p = h @ w_in
    g_b, g_c, hv, u, v = jnp.split(
        p, [CONV_WIDTH, 2 * CONV_WIDTH, 3 * CONV_WIDTH, 3 * CONV_WIDTH + SGU_WIDTH], axis=-1)
    z = g_c * hv
    zp = jnp.pad(z, ((0, 0), (1, 1), (0, 0)))
    zc = conv_w[0] * zp[:, :-2] + conv_w[1] * zp[:, 1:-1] + conv_w[2] * zp[:, 2:]
    y_a = g_b * zc
    y_b = _sgu(jax.nn.gelu(u, approximate=False), jax.nn.gelu(v, approximate=False),
               sgu_ln_g, sgu_ln_b, sgu_w, sgu_b)
    return jnp.concatenate([y_a, y_b], axis=-1) @ w_out


def _axial_tables(n):
    rows = n // GRID_W
    row = jnp.repeat(jnp.arange(rows, dtype=jnp.float32), GRID_W)
    col = jnp.tile(jnp.arange(GRID_W, dtype=jnp.float32), rows)
    n_freq = HEAD_DIM // 4
    inv = ROPE_THETA ** (-jnp.arange(n_freq, dtype=jnp.float32) / n_freq)
    ang_r = row[:, None] * inv
    ang_c = col[:, None] * inv
    return (jnp.cos(ang_r)[None, :, None, :], jnp.sin(ang_r)[None, :, None, :],
            jnp.cos(ang_c)[None, :, None, :], jnp.sin(ang_c)[None, :, None, :])


def _rotate(x, cos, sin):
    x1, x2 = jnp.split(x, 2, axis=-1)
    return jnp.concatenate([x1 * cos - x2 * sin, x2 * cos + x1 * sin], axis=-1)


def _axial_rope(x, tables):
    cos_r, sin_r, cos_c, sin_c = tables
    xf = x.astype(jnp.float32)
    xr, xc = jnp.split(xf, 2, axis=-1)
    return jnp.concatenate([_rotate(xr, cos_r, sin_r), _rotate(xc, cos_c, sin_c)], axis=-1).astype(x.dtype)


def _proj_q(h, w_q, q_g):
    bsz, n, _ = h.shape
    return _rmsnorm((h @ w_q).reshape(bsz, n, N_HEADS, HEAD_DIM), q_g)


def _proj_kv(h, w_kv, k_g):
    bsz, n, _ = h.shape
    k, v = jnp.split((h @ w_kv).reshape(bsz, n, 2 * N_KV_HEADS, HEAD_DIM), 2, axis=2)
    return _rmsnorm(k, k_g), v


def _attend(q, k, v):
    bsz, n, _, _ = q.shape
    nb = n // Q_BLOCK
    qb = jnp.moveaxis(q.reshape(bsz, nb, Q_BLOCK, N_KV_HEADS, Q_PER_KV, HEAD_DIM), 1, 0)
    scale = HEAD_DIM ** -0.5

    def block(qi):
        s = jnp.einsum('bqkgd,bskd->bkgqs', qi, k).astype(jnp.float32) * scale
        p = jax.nn.softmax(s, axis=-1).astype(v.dtype)
        return jnp.einsum('bkgqs,bskd->bqkgd', p, v)

    o = lax.map(block, qb)
    return jnp.moveaxis(o, 0, 1).reshape(bsz, n, N_HEADS * HEAD_DIM)


def _attention_mixer(h_lat, h_ctx, w_qkv, q_g, k_g, w_out, with_ctx_out):
    w_q = w_qkv[:, :N_HEADS * HEAD_DIM]
    w_kv = w_qkv[:, N_HEADS * HEAD_DIM:]
    tables = _axial_tables(h_lat.shape[1])
    q_l = _axial_rope(_proj_q(h_lat, w_q, q_g), tables)
    k_l, v_l = _proj_kv(h_lat, w_kv, k_g)
    k_l = _axial_rope(k_l, tables)
    k_c, v_c = _proj_kv(h_ctx, w_kv, k_g)
    k_all = jnp.concatenate([k_l, k_c], axis=1)
    v_all = jnp.concatenate([v_l, v_c], axis=1)
    y_lat = _attend(q_l, k_all, v_all) @ w_out
    y_ctx = _attend(_proj_q(h_ctx, w_q, q_g), k_c, v_c) @ w_out if with_ctx_out else None
    return y_lat, y_ctx


def setup_inputs(seed: int = 0) -> dict:
    key = jax.random.key(seed)
    ks = jax.random.split(key, 21)
    f32 = jnp.float32
    nrm = lambda k, shape, s: jax.random.normal(k, shape, f32) * s
    D = D_MODEL
    return {
        "x": nrm(ks[0], (BATCH, SEQ, D), 1.0),
        "c": nrm(ks[1], (BATCH, D), 1.0),
        "ctx": nrm(ks[2], (BATCH, CTX_LEN, D), 1.0),
        "c_ctx": nrm(ks[3], (D,), 1.0),
        "ada_w": nrm(ks[4], (DEPTH, D, 6 * D), 0.5 * D ** -0.5),
        "ada_b": nrm(ks[5], (DEPTH, 6 * D), 0.01),
        "ln_g": 1.0 + nrm(ks[6], (DEPTH, 2, D), 0.05),
        "ln_b": nrm(ks[7], (DEPTH, 2, D), 0.01),
        "ffn_w_in": nrm(ks[8], (DEPTH, D, 2 * FFN_HIDDEN), D ** -0.5),
        "ffn_w_out": nrm(ks[9], (DEPTH, FFN_HIDDEN, D), BETA * FFN_HIDDEN ** -0.5),
        "mix_w_in": nrm(ks[10], (N_EVEN, D, MIX_IN), D ** -0.5),
        "conv_w": nrm(ks[11], (N_EVEN, CONV_TAPS, CONV_WIDTH), CONV_TAPS ** -0.5),
        "sgu_ln_g": 1.0 + nrm(ks[12], (N_EVEN, SGU_WIDTH), 0.05),
        "sgu_ln_b": nrm(ks[13], (N_EVEN, SGU_WIDTH), 0.01),
        "sgu_w": nrm(ks[14], (N_EVEN, SGU_GROUPS, CHUNK, CHUNK), CHUNK ** -0.5),
        "sgu_b": 1.0 + nrm(ks[15], (N_EVEN, SGU_GROUPS, CHUNK), 0.01),
        "mix_w_out": nrm(ks[16], (N_EVEN, MIX_OUT, D), BETA * MIX_OUT ** -0.5),
        "attn_w_qkv": nrm(ks[17], (N_ODD, D, (N_HEADS + 2 * N_KV_HEADS) * HEAD_DIM), D ** -0.5),
        "q_norm_g": 1.0 + nrm(ks[18], (N_ODD, HEAD_DIM), 0.05),
        "k_norm_g": 1.0 + nrm(ks[19], (N_ODD, HEAD_DIM), 0.05),
        "attn_w_out": nrm(ks[20], (N_ODD, N_HEADS * HEAD_DIM, D), BETA * (N_HEADS * HEAD_DIM) ** -0.5),
    }


def reference(x, c, ctx, c_ctx, ada_w, ada_b, ln_g, ln_b, ffn_w_in, ffn_w_out,
              mix_w_in, conv_w, sgu_ln_g, sgu_ln_b, sgu_w, sgu_b, mix_w_out,
              attn_w_qkv, q_norm_g, k_norm_g, attn_w_out):
    s_lat = jax.nn.silu(c)
    s_ctx = jax.nn.silu(c_ctx)
    for l in range(DEPTH):
        last = l == DEPTH - 1
        m_lat = jnp.split((s_lat @ ada_w[l] + ada_b[l])[:, None, :], 6, axis=-1)
        m_ctx = jnp.split(s_ctx @ ada_w[l] + ada_b[l], 6, axis=-1)
        h_lat = _modulate(x, m_lat[0], m_lat[1])
        h_ctx = _modulate(ctx, m_ctx[0], m_ctx[1])
        if l % 2 == 0:
            e = l // 2
            params = (mix_w_in[e], conv_w[e], sgu_ln_g[e], sgu_ln_b[e], sgu_w[e], sgu_b[e], mix_w_out[e])
            y_lat = _conv_chunk_mixer(h_lat, *params)
            y_ctx = None if last else _conv_chunk_mixer(h_ctx, *params)
        else:
            o = l // 2
            y_lat, y_ctx = _attention_mixer(h_lat, h_ctx, attn_w_qkv[o], q_norm_g[o], k_norm_g[o],
                                            attn_w_out[o], not last)
        x = _post_norm(x, y_lat, m_lat[2], ln_g[l, 0], ln_b[l, 0])
        x = _post_norm(x, _swiglu(_modulate(x, m_lat[3], m_lat[4]), ffn_w_in[l], ffn_w_out[l]),
                       m_lat[5], ln_g[l, 1], ln_b[l, 1])
        if not last:
            ctx = _post_norm(ctx, y_ctx, m_ctx[2], ln_g[l, 0], ln_b[l, 0])
            ctx = _post_norm(ctx, _swiglu(_modulate(ctx, m_ctx[3], m_ctx[4]), ffn_w_in[l], ffn_w_out[l]),
                             m_ctx[5], ln_g[l, 1], ln_b[l, 1])
    return x
```

```python
import numpy as np
import concourse.bass as bass
import concourse.mybir as mybir
from concourse.bass_utils import run_bass_kernel_spmd

F32 = mybir.dt.float32
BF16 = mybir.dt.bfloat16
AF = mybir.ActivationFunctionType
ALU = mybir.AluOpType

NCORES = 8
D = 1024
KC = 8
S = 2048
CTX = 256
NTOK = S + CTX
FF = 2816
FC = 22
NH = 8
HD = 128
ALPHA = float((2.0 * 2) ** 0.25)
LN_EPS = 1e-5
RMS_EPS = 1e-6
GRID_W = 64
NSLOT = 4
SLOTW = 2816
NFP = 12
NBP = 10
FPW = 516


class Tok:
    __slots__ = ("w", "r", "const")

    def __init__(self, const=False):
        self.w = {}
        self.r = {}
        self.const = const


class Prog:
    def __init__(self):
        self.nc = bass.Bass("TRN2", target_bir_lowering=False)
        nc = self.nc
        self.eng = {"pe": nc.tensor, "act": nc.scalar, "dve": nc.vector, "pool": nc.gpsimd, "sp": nc.sync}
        self.ops = []
        self.last_dma = {}
        self.label = "setup"
        self.pe_labels = []
        self.dbg = None

    def op(self, eng, fn, reads=(), writes=(), dma=None, nmm=0):
        deps = {}
        if nmm:
            self.pe_labels.extend([self.label] * nmm)

        def add(d):
            for k, i in d.items():
                if deps.get(k, -1) < i:
                    deps[k] = i

        for t in reads:
            add(t.w)
        for t in writes:
            add(t.w)
            add(t.r)
        i = len(self.ops)
        key = dma if dma is not None else eng
        if dma is not None:
            if key in self.last_dma:
                add({key: self.last_dma[key]})
            self.last_dma[key] = i
        self.ops.append([eng, fn, deps, key, dma is not None, None, False, self.label])
        for t in reads:
            if not t.const:
                if t.r.get(key, -1) < i:
                    t.r[key] = i
        for t in writes:
            t.w = {key: i}
            t.r = {}
        return i

    def emit(self):
        nc = self.nc
        ops = self.ops
        needed = set()
        for o in ops:
            needed.update(o[2].values())
        keys = []
        for o in ops:
            if o[3] not in keys:
                keys.append(o[3])
        import contextlib
        with contextlib.ExitStack() as es:
            sems = {k: es.enter_context(nc.semaphore("s_" + k)) for k in keys}
            cnt = {k: 0 for k in keys}
            for i, o in enumerate(ops):
                if o[4]:
                    cnt[o[3]] += 16
                    o[5] = cnt[o[3]]
                elif i in needed:
                    cnt[o[3]] += 1
                    o[5] = cnt[o[3]]
                    o[6] = True
            seen = {e: {} for e in self.eng}
            nwait = 0
            for i, o in enumerate(ops):
                e = o[0]
                h = self.eng[e]
                for k, d in o[2].items():
                    p = ops[d]
                    if e == "pe" and p[0] == "pe" and not p[4]:
                        continue
                    v = p[5]
                    if seen[e].get(k, 0) < v:
                        h.wait_ge(sems[k], v)
                        seen[e][k] = v
                        nwait += 1
                ins = o[1](h)
                if self.dbg is not None:
                    try:
                        self.dbg.append((e, int(ins.ins.name.split("-")[1]), o[7]))
                    except Exception:
                        pass
                if o[4]:
                    ins.then_inc(sems[o[3]], 16)
                elif o[6]:
                    ins.then_inc(sems[e], 1)
            for k in keys:
                if cnt[k] > 0 and seen["sp"].get(k, 0) < cnt[k]:
                    nc.sync.wait_ge(sems[k], cnt[k])
            self.nwait = nwait
        return nc


def _piece_lhsT(W, ocs):
    K, N = W.shape
    kc, no = K // 128, N // 128
    A = W.reshape(kc, 128, no, 128).transpose(2, 1, 0, 3)
    A = A.reshape(no // ocs, ocs, 128, kc, 128).transpose(0, 2, 1, 3, 4)
    return np.ascontiguousarray(A.reshape(no // ocs, 128, ocs * kc * 128))


def _piece_moving(W, kcs):
    K, N = W.shape
    kc = K // 128
    A = W.reshape(kc, 128, N).transpose(1, 0, 2)
    A = A.reshape(128, kc // kcs, kcs * N).transpose(1, 0, 2)
    return np.ascontiguousarray(A)


def _vec_pc(v):
    v = np.asarray(v)
    lead = v.shape[:-1]
    n = v.shape[-1] // 128
    A = v.reshape(lead + (n, 128))
    A = np.moveaxis(A, -1, 0)
    return np.ascontiguousarray(A)


W_MI = 0
W_MO = 10
W_FI0 = 14
W_Q = 36
W_K = 40
W_V = 41
W_O = 42
W_FI1 = 46
NW1 = 68
NW2 = 16


def _rope_tables():
    f32 = np.float32
    t = np.arange(S)
    row = (t // GRID_W).astype(f32)
    col = (t % GRID_W).astype(f32)
    nf = HD // 4
    inv = (np.float32(10000.0) ** (-(np.arange(nf, dtype=f32) / f32(nf)))).astype(f32)
    ang_r = (row[:, None] * inv[None, :]).astype(f32)
    ang_c = (col[:, None] * inv[None, :]).astype(f32)
    cr, sr = np.cos(ang_r).astype(f32), np.sin(ang_r).astype(f32)
    cc, sc = np.cos(ang_c).astype(f32), np.sin(ang_c).astype(f32)
    COS = np.concatenate([cr, cr, cc, cc], axis=1).T
    SIN = np.concatenate([-sr, sr, -sc, sc], axis=1).T
    perm = np.arange(128)
    perm = np.where((perm // 32) % 2 == 0, perm + 32, perm - 32)
    PM = np.zeros((128, 128), f32)
    PM[perm, np.arange(128)] = 1.0
    return np.ascontiguousarray(COS, dtype=f32), np.ascontiguousarray(SIN, dtype=f32), PM


def prepare_shared(inp):
    f = lambda k: np.asarray(inp[k], dtype=np.float32)
    mix_in, mix_out = f("mix_w_in")[0], f("mix_w_out")[0]
    fin, fout = f("ffn_w_in"), f("ffn_w_out")
    qkv, wo = f("attn_w_qkv")[0], f("attn_w_out")[0]
    w1 = np.empty((NW1, 128, 2048), np.float32)
    w1[W_MI + 0:W_MI + 2] = _piece_lhsT(mix_in[:, 512:1024], 2)
    w1[W_MI + 2:W_MI + 4] = _piece_lhsT(mix_in[:, 1024:1536], 2)
    w1[W_MI + 4:W_MI + 6] = _piece_lhsT(mix_in[:, 0:512], 2)
    w1[W_MI + 6:W_MI + 8] = _piece_lhsT(mix_in[:, 1536:2048], 2)
    w1[W_MI + 8:W_MI + 10] = _piece_moving(mix_in[:, 2048:2560], 4)
    w1[W_MO:W_MO + 4] = _piece_lhsT(mix_out, 2)
    for l, base in ((0, W_FI0), (1, W_FI1)):
        g = _piece_lhsT(fin[l][:, :FF], 1)
        u = _piece_lhsT(fin[l][:, FF:], 1)
        w1[base:base + FC] = np.concatenate([g, u], axis=2)
    w1[W_Q:W_Q + 4] = _piece_lhsT(qkv[:, 0:1024], 2)
    w1[W_K:W_K + 1] = _piece_lhsT(qkv[:, 1024:1280], 2)
    w1[W_V:W_V + 1] = _piece_moving(qkv[:, 1280:1536], 8)
    w1[W_O:W_O + 4] = _piece_lhsT(wo, 2)
    w2 = np.empty((NW2, 128, SLOTW), np.float32)
    w2[0:8] = _piece_lhsT(fout[0], 1)
    w2[8:16] = _piece_lhsT(fout[1], 1)
    ada = f("ada_w")
    A = ada.reshape(2, 2, 4, 128, 48, 128)
    A = A.transpose(0, 4, 1, 3, 2, 5)
    adaw = np.ascontiguousarray(A.reshape(2 * 48 * 2, 128, 512))
    adab = _vec_pc(f("ada_b")).reshape(128, 96)
    lng = _vec_pc(f("ln_g")).reshape(128, 32)
    lnb = _vec_pc(f("ln_b")).reshape(128, 32)
    cw = np.ascontiguousarray(_vec_pc(f("conv_w")[0]).transpose(0, 2, 1)).reshape(128, 12)
    wsT = np.ascontiguousarray(f("sgu_w")[0].transpose(2, 0, 1)).reshape(128, 1024)
    sb = f("sgu_b")[0]
    bsr = np.ascontiguousarray(
        np.repeat(sb.reshape(4, 2, 1, 128), 64, axis=2).reshape(4, 128, 128).transpose(1, 0, 2)).reshape(128, 512)
    gam = _vec_pc(f("sgu_ln_g")[0]).reshape(128, 4)
    betb = np.ascontiguousarray(np.broadcast_to(f("sgu_ln_b")[0][None, :], (128, 512)))
    qg = f("q_norm_g")[0].reshape(128, 1)
    kg = f("k_norm_g")[0].reshape(128, 1)
    COS, SIN, PM = _rope_tables()
    small = np.concatenate([adab, lng, lnb, cw, gam, qg, kg], axis=1)
    return dict(w1=w1, w2=w2, adaw=adaw, small=np.ascontiguousarray(small), wsT=wsT, bsr=bsr, betb=betb,
                cosT=COS, sinT=SIN, pm=PM)


SM_ADAB, SM_LNG, SM_LNB, SM_CW, SM_GAM, SM_QG, SM_KG, SM_W = 0, 96, 128, 160, 172, 176, 177, 178


def prepare_core(inp, core, nb):
    x = np.asarray(inp["x"], dtype=np.float32)[core * nb:(core + 1) * nb]
    ctx = np.asarray(inp["ctx"], dtype=np.float32)[core * nb:(core + 1) * nb]
    c = np.asarray(inp["c"], dtype=np.float32)[core * nb:(core + 1) * nb]
    cctx = np.asarray(inp["c_ctx"], dtype=np.float32)
    xT = np.ascontiguousarray(x.reshape(nb, S, KC, 128).transpose(0, 3, 2, 1))
    ctxT = np.ascontiguousarray(ctx.reshape(nb, CTX, KC, 128).transpose(0, 3, 2, 1))
    cc = np.zeros((5, D), np.float32)
    cc[:nb] = c
    cc[4] = cctx
    ccT = np.ascontiguousarray(cc.reshape(5, KC, 128).transpose(2, 1, 0))
    return dict(xT=xT, ctxT=ctxT, cc=ccT.reshape(128, 40))


def build(nb=4, stop_after_l0=False):
    P = Prog()
    nc = P.nc
    op = P.op

    def dram(name, shape, dt, kind):
        return nc.dram_tensor(name, list(shape), dt, kind=kind).ap()

    xT = dram("xT", [nb, 128, KC, S], F32, "ExternalInput")
    ctxT = dram("ctxT", [nb, 128, KC, CTX], F32, "ExternalInput")
    ccD = dram("cc", [128, 40], F32, "ExternalInput")
    w1D = dram("w1", [NW1, 128, 2048], F32, "ExternalInput")
    w2D = dram("w2", [NW2, 128, SLOTW], F32, "ExternalInput")
    adawD = dram("adaw", [192, 128, 512], F32, "ExternalInput")
    smallD = dram("small", [128, SM_W], F32, "ExternalInput")
    wsTD = dram("wsT", [128, 1024], F32, "ExternalInput")
    bsrD = dram("bsr", [128, 512], F32, "ExternalInput")
    betbD = dram("betb", [128, 512], F32, "ExternalInput")
    cosD = dram("cosT", [128, S], F32, "ExternalInput")
    sinD = dram("sinT", [128, S], F32, "ExternalInput")
    pmD = dram("pm", [128, 128], F32, "ExternalInput")
    outT = dram("outT", [nb, 128, KC, S], F32, "ExternalOutput")
    w1B = dram("w1b", [NW1, 128, 2048], BF16, "Internal")
    w2B = dram("w2b", [NW2, 128, SLOTW], BF16, "Internal")

    def sb(name, shape, dt):
        return nc.alloc_sbuf_tensor(name, list(shape), dt).ap()

    XA = sb("XA", [128, KC, NTOK], F32)
    OP8 = [sb(f"OP8_{i}", [128, KC, 512], BF16) for i in range(3)]
    WR = [sb(f"WR{i}", [128, SLOTW], BF16) for i in range(NSLOT)]
    HID = sb("HID", [128, FC, 512], BF16)
    FPB = [sb(f"FP{i}", [128, FPW], F32) for i in range(NFP)]
    BPB = [sb(f"BP{i}", [128, 512], BF16) for i in range(NBP)]
    ZH = sb("ZH", [128, 4, 6], F32)
    GH = sb("GH", [128, 4, 6], F32)
    HE = sb("HE", [128, KC, 6], BF16)
    KT = sb("KT", [128, 2, NTOK], BF16)
    RG = sb("RG", [128, (NTOK // 128) * 256], BF16)
    VT = RG.rearrange("p (a b) -> p a b", b=256)
    Z = RG[:, 0:4 * 514 * 2].bitcast(F32).rearrange("p (a b) -> p a b", b=514)
    SMALL = sb("SMALL", [128, SM_W], F32)
    CC = sb("CC", [128, KC, 5], F32)
    SIL = sb("SIL", [128, KC, 5], F32)
    MOD = sb("MOD", [128, 96, 5], F32)
    SCH = sb("SCH", [128, 4, KC, 5], F32)
    GA = sb("GA", [128, 32], F32)
    BA = sb("BA", [128, 32], F32)
    QG = sb("QG", [128, 1], F32)
    WST = sb("WST", [128, 8, 128], BF16)
    WSTF = FPB
    BS2 = sb("BS2", [128, 4, 128], F32)
    ONES_LN = sb("ONES_LN", [128, 128], BF16)
    ONES_HD = sb("ONES_HD", [128, 128], BF16)
    ONES_1 = sb("ONES_1", [128, 128], BF16)
    PMS = sb("PMS", [128, 128], F32)
    STAT = [sb(f"STAT{i}", [128, 16], F32) for i in range(6)]
    PS = [nc.alloc_psum_tensor(f"PS{i}", [128, 512], F32).ap() for i in range(8)]

    tXA = [[Tok() for _ in range(KC)] for _ in range(5)]
    tOP8 = [[Tok() for _ in range(KC)] for _ in range(3)]
    tWR = [Tok() for _ in range(NSLOT)]
    tHID = [Tok() for _ in range(FC)]
    tFP = [Tok() for _ in range(NFP)]
    tBP = [Tok() for _ in range(NBP)]
    tZ = [Tok() for _ in range(4)]
    tZH, tGH, tHE = Tok(), Tok(), Tok()
    tKT = [[Tok() for _ in range(5)] for _ in range(2)]
    tVT = [Tok() for _ in range(NTOK // 128)]
    tSTAT = [Tok() for _ in range(6)]
    tPS = [Tok() for _ in range(8)]
    tCONST = Tok(const=True)
    tW1 = [Tok(const=True) for _ in range(NW1)]
    tW2 = [Tok(const=True) for _ in range(NW2)]

    st = dict(wr=0, fp=0, bp=0, ps=0, stat=0, op8=0, ndma=0)

    held = {"fp": set(), "bp": set()}

    def _pool(kind, n, hold):
        i = st[kind]
        k = 0
        while i in held[kind]:
            i = (i + 1) % n
            k += 1
            assert k <= n, "pool exhausted " + kind
        st[kind] = (i + 1) % n
        if hold:
            held[kind].add(i)
        return i

    def fpool(hold=False):
        i = _pool("fp", NFP, hold)
        return FPB[i], tFP[i]

    def bpool(hold=False):
        i = _pool("bp", NBP, hold)
        return BPB[i], tBP[i]

    def frelease(buf):
        held["fp"].discard([k for k in range(NFP) if FPB[k] is buf][0])

    def brelease(buf):
        held["bp"].discard([k for k in range(NBP) if BPB[k] is buf][0])

    def spool():
        i = st["stat"]
        st["stat"] = (i + 1) % 6
        return STAT[i], tSTAT[i]

    ps_set = {"banks": [0, 1, 2, 3, 4, 5]}
    ps_held = set()

    def psum(hold=False):
        banks = ps_set["banks"]
        k = 0
        while True:
            i = banks[st["ps"] % len(banks)]
            st["ps"] += 1
            if i not in ps_held:
                break
            k += 1
            assert k <= len(banks), "psum exhausted"
        if hold:
            ps_held.add(i)
        return PS[i], tPS[i]

    def prelease(ps_):
        ps_held.discard([k for k in range(8) if PS[k] is ps_][0])

    op8_held = set()

    def op8():
        i = st["op8"]
        k = 0
        while i in op8_held:
            i = (i + 1) % 3
            k += 1
            assert k <= 3, "op8 exhausted"
        st["op8"] = (i + 1) % 3
        op8_held.add(i)
        return OP8[i], tOP8[i]

    def orelease(buf):
        op8_held.discard([k for k in range(3) if OP8[k] is buf][0])

    def dma(eng, out, in_, reads, writes, stream):
        if stream in ("misc", "cast"):
            k = st["ndma"]
            st["ndma"] = k + 1
            stream = stream + str(k % 4)
        return op(eng, lambda h: h.dma_start(out=out, in_=in_), reads, writes, dma=stream)

    def wload(which, idx):
        s = st["wr"]
        st["wr"] = (s + 1) % NSLOT
        if which == 1:
            src, tk, n = w1B[idx], tW1[idx], 2048
        else:
            src, tk, n = w2B[idx], tW2[idx], SLOTW
        dma("sp", WR[s][:, 0:n], src, [tk], [tWR[s]], f"w{s}")
        return WR[s], tWR[s]

    def mm(out, pairs, reads, wtok):
        def fn(h):
            n = len(pairs)
            ins = None
            for i, (l, r) in enumerate(pairs):
                ins = h.matmul(out, lhsT=l, rhs=r, start=(i == 0), stop=(i == n - 1))
            return ins
        return op("pe", fn, reads, [wtok], nmm=len(pairs))

    def mm1(out, l, r, start, stop, reads, wtok):
        return op("pe", lambda h: h.matmul(out, lhsT=l, rhs=r, start=start, stop=stop), reads, [wtok], nmm=1)

    def tt(e, out, a, b, o, reads, writes):
        return op(e, lambda h: h.tensor_tensor(out=out, in0=a, in1=b, op=o), reads, writes)

    def ts(e, out, a, s1, s2, o1, o2, reads, writes):
        if s2 is None:
            return op(e, lambda h: h.tensor_scalar(out=out, in0=a, scalar1=s1, scalar2=None, op0=o1), reads, writes)
        return op(e, lambda h: h.tensor_scalar(out=out, in0=a, scalar1=s1, scalar2=s2, op0=o1, op1=o2), reads, writes)

    def stt(out, a, s, b, o1, o2, reads, writes):
        return op("dve", lambda h: h.scalar_tensor_tensor(out=out, in0=a, scalar=s, in1=b, op0=o1, op1=o2),
                  reads, writes)

    def act(out, in_, func, reads, writes, bias=None, scale=None):
        kw = {}
        if bias is not None:
            kw["bias"] = bias
        if scale is not None:
            kw["scale"] = scale
        return op("act", lambda h: h.activation(out=out, in_=in_, func=func, **kw), reads, writes)

    order1 = list(range(W_MI, W_MI + 10)) + list(range(W_MO, W_MO + 4)) + list(range(W_FI0, W_FI0 + FC))
    order1b = list(range(W_Q, NW1))

    def cast1(lo, hi):
        dma("pool", w1B[lo:hi], w1D[lo:hi], [], [tW1[i] for i in range(lo, hi)], "cast")

    def cast2(lo, hi):
        dma("pool", w2B[lo:hi], w2D[lo:hi], [], [tW2[i] for i in range(lo, hi)], "cast")

    dma("sp", SMALL, smallD, [], [tCONST], "misc")
    dma("sp", CC.rearrange("p a b -> p (a b)"), ccD, [], [tCONST], "misc")
    dma("sp", PMS, pmD, [], [tCONST], "misc")
    dma("sp", BS2.rearrange("p a b -> p (a b)"), bsrD, [], [tCONST], "misc")
    for lo in range(0, 36, 4):
        cast1(lo, lo + 4)
    cast2(0, 8)
    for lo in range(36, NW1, 4):
        cast1(lo, lo + 4)
    cast2(8, 16)

    op("dve", lambda h: h.memset(ONES_LN, 1.0 / D), [], [tCONST])
    op("dve", lambda h: h.memset(ONES_HD, 1.0 / HD), [], [tCONST])
    op("dve", lambda h: h.memset(ONES_1, 1.0), [], [tCONST])
    act(SIL, CC, AF.Silu, [tCONST], [tCONST])
    ts("dve", GA[:, 0:24], SMALL[:, SM_LNG:SM_LNG + 24], ALPHA, None, ALU.mult, None, [tCONST], [tCONST])
    ts("dve", GA[:, 24:32], SMALL[:, SM_LNG + 24:SM_LNG + 32], 1.0, None, ALU.mult, None, [tCONST], [tCONST])
    ts("dve", BA[:, 0:24], SMALL[:, SM_LNB:SM_LNB + 24], ALPHA, None, ALU.mult, None, [tCONST], [tCONST])
    ts("dve", BA[:, 24:32], SMALL[:, SM_LNB + 24:SM_LNB + 32], 1.0, None, ALU.mult, None, [tCONST], [tCONST])
    ts("dve", QG, SMALL[:, SM_QG:SM_QG + 1], float(HD ** -0.5), None, ALU.mult, None, [tCONST], [tCONST])

    wf = []
    for hlf in range(2):
        b_, t_ = fpool(hold=True)
        dma("sp", b_[:, 0:512], wsTD[:, hlf * 512:(hlf + 1) * 512], [], [t_], "misc")
        wf.append((b_, t_))
        op("dve", (lambda b_=b_, hlf=hlf: lambda h: h.tensor_copy(
            out=WST[:, hlf * 4:(hlf + 1) * 4, :].rearrange("p a b -> p (a b)"), in_=b_[:, 0:512]))(),
           [t_], [tCONST])
    bb, tbb = fpool(hold=True)
    dma("sp", bb[:, 0:512], betbD, [], [tbb], "misc")
    for jc in range(4):
        ps_, tp_ = psum()
        for hh in range(2):
            g = 2 * jc + hh
            wsrc, twsrc = wf[g // 4]
            mm1(ps_[64 * hh:64 * hh + 64, 0:128], bb[:, jc * 128 + 64 * hh:jc * 128 + 64 * hh + 64],
                wsrc[:, (g % 4) * 128:(g % 4) * 128 + 128], True, True, [tbb, twsrc], tp_)
        tt("dve", BS2[:, jc, :], ps_[:, 0:128], BS2[:, jc, :], ALU.add, [tp_, tCONST], [tCONST])
    frelease(bb)
    frelease(wf[0][0])
    frelease(wf[1][0])

    for l in range(2):
        for oc in range(48):
            bufs = []
            for hlf in range(2):
                b_, t_ = fpool()
                dma("sp", b_[:, 0:512], adawD[(l * 48 + oc) * 2 + hlf], [], [t_], "misc")
                bufs.append((b_, t_))
            ps_, tp_ = psum()
            pairs = []
            for kc in range(KC):
                b_, t_ = bufs[kc // 4]
                pairs.append((b_[:, (kc % 4) * 128:(kc % 4) * 128 + 128], SIL[:, kc, :]))
            mm(ps_[:, 0:5], pairs, [bufs[0][1], bufs[1][1], tCONST], tp_)
            ts("dve", MOD[:, l * 48 + oc, :], ps_[:, 0:5], SMALL[:, SM_ADAB + l * 48 + oc:SM_ADAB + l * 48 + oc + 1],
               None, ALU.add, None, [tp_, tCONST], [tCONST])
    for l in range(2):
        for s_ in range(2):
            base = l * 48 + (3 * s_ + 1) * 8
            ts("dve", SCH[:, l * 2 + s_].rearrange("p a b -> p (a b)"),
               MOD[:, base:base + 8, :].rearrange("p a b -> p (a b)"), 1.0, 1.0 / ALPHA, ALU.add, ALU.mult,
               [tCONST], [tCONST])

    def shift_ap(l, s_, c, j):
        return MOD[:, l * 48 + (3 * s_) * 8 + c, j:j + 1]

    def gate_ap(l, s_, c, j):
        return MOD[:, l * 48 + (3 * s_ + 2) * 8 + c, j:j + 1]

    def sch_ap(l, s_, c, j):
        return SCH[:, l * 2 + s_, c, j:j + 1]

    TILES = [(0, 0, 512, False), (1, 512, 512, False), (2, 1024, 512, False), (3, 1536, 512, False),
             (4, 2048, 256, True)]
    deferred = []

    def flush_deferred():
        for f_ in deferred:
            f_()
        deferred.clear()

    def make_h(l, ti, t0, T, j):
        buf, tk = op8()
        for c in range(KC):
            e = "dve" if c % 2 == 0 else "pool"
            ts(e, buf[:, c, 0:T], XA[:, c, t0:t0 + T], sch_ap(l, 0, c, j), shift_ap(l, 0, c, j), ALU.mult, ALU.add,
               [tXA[ti][c], tCONST], [tk[c]])
        return buf, tk

    def run_par(*gens):
        gens = [g for g in gens if g is not None]
        while gens:
            for g in list(gens):
                try:
                    next(g)
                except StopIteration:
                    gens.remove(g)

    def layer_norm(l, s_, ti, t0, T, j, res):
        lab = f"ln{l}{s_}"
        P.label = lab
        mean_ps, tm = PS[6], tPS[6]
        ex2_ps, te = PS[7], tPS[7]
        for c in range(KC):
            P.label = lab
            rb, trb = bpool()
            op("pool", (lambda rb=rb, c=c: lambda h: h.tensor_copy(out=rb[:, 0:T], in_=XA[:, c, t0:t0 + T]))(),
               [tXA[ti][c]], [trb])
            sq, tsq = bpool()
            e = "pool" if c % 4 != 3 else "dve"
            tt(e, sq[:, 0:T], XA[:, c, t0:t0 + T], XA[:, c, t0:t0 + T], ALU.mult, [tXA[ti][c]], [tsq])
            mm1(mean_ps[:, 0:T], ONES_LN, rb[:, 0:T], c == 0, c == KC - 1, [trb, tCONST], tm)
            mm1(ex2_ps[:, 0:T], ONES_LN, sq[:, 0:T], c == 0, c == KC - 1, [tsq, tCONST], te)
            if c % 2 == 1:
                yield
        P.label = lab
        msb, tmsb = fpool(hold=True)
        op("dve", lambda h: h.tensor_copy(out=msb[:, 0:T], in_=mean_ps[:, 0:T]), [tm], [tmsb])
        var, tvar = fpool(hold=True)
        tt("pool", var[:, 0:T], msb[:, 0:T], msb[:, 0:T], ALU.mult, [tmsb], [tvar])
        tt("dve", var[:, 0:T], ex2_ps[:, 0:T], var[:, 0:T], ALU.subtract, [te, tvar], [tvar])
        A_, tA = fpool(hold=True)
        act(A_[:, 0:T], var[:, 0:T], AF.Sqrt, [tvar], [tA], bias=LN_EPS)
        op("dve", lambda h: h.reciprocal(out=A_[:, 0:T], in_=A_[:, 0:T]), [tA], [tA])
        frelease(var)
        B_, tB = fpool(hold=True)
        stt(B_[:, 0:T], msb[:, 0:T], -1.0, A_[:, 0:T], ALU.mult, ALU.mult, [tmsb, tA], [tB])
        hf = None
        if s_ == 0:
            hf = op8()
            res["hf"] = hf
        yield
        gi = (l * 2 + s_) * 8
        for c in range(KC):
            P.label = lab
            xa = XA[:, c, t0:t0 + T]
            tx = tXA[ti][c]
            on_dve = c in (1, 3, 5)
            if on_dve:
                stt(xa, xa, GA[:, gi + c:gi + c + 1], A_[:, 0:T], ALU.mult, ALU.mult, [tx, tA, tCONST], [tx])
                stt(xa, B_[:, 0:T], GA[:, gi + c:gi + c + 1], xa, ALU.mult, ALU.add, [tx, tB, tCONST], [tx])
                ts("dve", xa, xa, BA[:, gi + c:gi + c + 1], None, ALU.add, None, [tx, tCONST], [tx])
            else:
                tt("pool", xa, xa, A_[:, 0:T], ALU.mult, [tx, tA], [tx])
                tt("pool", xa, xa, B_[:, 0:T], ALU.add, [tx, tB], [tx])
                ts("pool", xa, xa, GA[:, gi + c:gi + c + 1], BA[:, gi + c:gi + c + 1], ALU.mult, ALU.add,
                   [tx, tCONST], [tx])
            if s_ == 0:
                e = "dve" if on_dve else "pool"
                ts(e, hf[0][:, c, 0:T], xa, sch_ap(l, 1, c, j), shift_ap(l, 1, c, j), ALU.mult, ALU.add,
                   [tx, tCONST], [hf[1][c]])
            if c % 2 == 1:
                yield
        frelease(msb)
        frelease(A_)
        frelease(B_)

    def ffn(l, ti, t0, T, j, res):
        hb, thb = res["hf"]
        base = W_FI0 if l == 0 else W_FI1
        for jj in range(FC):
            P.label = f"ffn_in{l}"
            w, tw = wload(1, base + jj)
            pg, tpg = psum()
            mm(pg[:, 0:T], [(w[:, kc * 128:(kc + 1) * 128], hb[:, kc, 0:T]) for kc in range(KC)], [tw] + thb, tpg)
            pu, tpu = psum()
            mm(pu[:, 0:T], [(w[:, (8 + kc) * 128:(9 + kc) * 128], hb[:, kc, 0:T]) for kc in range(KC)],
               [tw] + thb, tpu)
            sg, tsg = fpool()
            act(sg[:, 0:T], pg[:, 0:T], AF.Silu, [tpg], [tsg])
            tt("dve", HID[:, jj, 0:T], pu[:, 0:T], sg[:, 0:T], ALU.mult, [tpu, tsg], [tHID[jj]])
            if jj == 10:
                flush_deferred()
            yield
        orelease(hb)
        for oc in range(KC):
            P.label = f"ffn_out{l}"
            w, tw = wload(2, l * 8 + oc)
            po, tpo = psum()
            mm(po[:, 0:T], [(w[:, kc * 128:(kc + 1) * 128], HID[:, kc, 0:T]) for kc in range(FC)], [tw] + tHID, tpo)
            xa = XA[:, oc, t0:t0 + T]
            stt(xa, po[:, 0:T], gate_ap(l, 1, oc, j), xa, ALU.mult, ALU.add, [tpo, tXA[ti][oc], tCONST],
                [tXA[ti][oc]])
            yield

    def l0_mixer(b, ti, t0, T, is_ctx, hm):
        j = 4 if is_ctx else b
        hb, thb = hm
        ntc = T // 128
        do_halo = (ti == 0)
        P.label = "l0.v"
        wv0, twv0 = wload(1, W_MI + 8)
        wv1, twv1 = wload(1, W_MI + 9)
        VN = []
        for tc in range(ntc):
            P.label = "l0.v"
            pv, tpv = psum()
            pairs = []
            for kc in range(KC):
                wv = wv0 if kc < 4 else wv1
                pairs.append((hb[:, kc, tc * 128:(tc + 1) * 128], wv[:, (kc % 4) * 512:(kc % 4 + 1) * 512]))
            mm(pv, pairs, [twv0, twv1] + thb, tpv)
            vg, tvg = fpool()
            act(vg[:, 0:512], pv, AF.Gelu, [tpv], [tvg])
            sst, tst = spool()
            op("dve", (lambda sst=sst, vg=vg: lambda h: h.bn_stats(out=sst[:, 0:6], in_=vg[:, 0:512]))(),
               [tvg], [tst])
            op("dve", (lambda sst=sst: lambda h: h.bn_aggr(out=sst[:, 8:10], in_=sst[:, 0:6]))(), [tst], [tst])
            act(sst[:, 10:11], sst[:, 9:10], AF.Sqrt, [tst], [tst], bias=LN_EPS)
            op("dve", (lambda sst=sst: lambda h: h.reciprocal(out=sst[:, 11:12], in_=sst[:, 10:11]))(), [tst], [tst])
            vn, tvn = bpool(hold=True)
            ts("dve", vn, vg[:, 0:512], sst[:, 8:9], sst[:, 11:12], ALU.subtract, ALU.mult, [tvg, tst], [tvn])
            VN.append((vn, tvn))
            yield
        for h2 in range(2):
            P.label = "l0.mixA"
            wg, twg = wload(1, W_MI + 0 + h2)
            wh, twh = wload(1, W_MI + 2 + h2)
            for o2 in range(2):
                P.label = "l0.mixA"
                oc = h2 * 2 + o2
                pg, tpg = psum()
                mm(pg[:, 0:T], [(wg[:, (o2 * 8 + kc) * 128:(o2 * 8 + kc + 1) * 128], hb[:, kc, 0:T])
                                for kc in range(KC)], [twg] + thb, tpg)
                gt, tgt = fpool()
                act(gt[:, 0:T], pg[:, 0:T], AF.Copy, [tpg], [tgt])
                ph, tph = psum()
                mm(ph[:, 0:T], [(wh[:, (o2 * 8 + kc) * 128:(o2 * 8 + kc + 1) * 128], hb[:, kc, 0:T])
                                for kc in range(KC)], [twh] + thb, tph)
                tt("dve", Z[:, oc, 1:T + 1], ph[:, 0:T], gt[:, 0:T], ALU.mult, [tph, tgt], [tZ[oc]] + tVT)
                if do_halo:
                    pg2, tpg2 = psum()
                    mm(pg2[:, 0:6], [(wg[:, (o2 * 8 + kc) * 128:(o2 * 8 + kc + 1) * 128], HE[:, kc, :])
                                     for kc in range(KC)], [twg, tHE], tpg2)
                    act(GH[:, oc, :], pg2[:, 0:6], AF.Copy, [tpg2], [tGH])
                    ph2, tph2 = psum()
                    mm(ph2[:, 0:6], [(wh[:, (o2 * 8 + kc) * 128:(o2 * 8 + kc + 1) * 128], HE[:, kc, :])
                                     for kc in range(KC)], [twh, tHE], tph2)
                    tt("dve", ZH[:, oc, :], ph2[:, 0:6], GH[:, oc, :], ALU.mult, [tph2, tGH], [tZH])
                yield
        P.label = "l0.mixA"
        if is_ctx or ti == 0:
            op("pool", lambda h: h.memset(Z[:, :, 0:1], 0.0), [], tZ + tVT)
        else:
            op("pool", lambda h: h.tensor_copy(out=Z[:, :, 0:1], in_=ZH[:, :, 2 * (ti - 1):2 * (ti - 1) + 1]),
               [tZH], tZ + tVT)
        if is_ctx or ti == 3:
            op("pool", lambda h: h.memset(Z[:, :, T + 1:T + 2], 0.0), [], tZ + tVT)
        else:
            op("pool", lambda h: h.tensor_copy(out=Z[:, :, T + 1:T + 2], in_=ZH[:, :, 2 * ti + 1:2 * ti + 2]),
               [tZH], tZ + tVT)
        yab, tyab = op8()
        ZC = []
        for oc in range(4):
            zc, tzc = fpool(hold=True)
            cwa = lambda tap, oc=oc: SMALL[:, SM_CW + oc * 3 + tap:SM_CW + oc * 3 + tap + 1]
            ts("pool", zc[:, 0:T], Z[:, oc, 1:T + 1], cwa(1), None, ALU.mult, None, [tZ[oc], tCONST], [tzc])
            stt(zc[:, 0:T], Z[:, oc, 0:T], cwa(0), zc[:, 0:T], ALU.mult, ALU.add, [tZ[oc], tzc, tCONST], [tzc])
            stt(zc[:, 0:T], Z[:, oc, 2:T + 2], cwa(2), zc[:, 0:T], ALU.mult, ALU.add, [tZ[oc], tzc, tCONST], [tzc])
            ZC.append((zc, tzc))
        yield
        for h2 in range(2):
            P.label = "l0.gb"
            wb_, twb = wload(1, W_MI + 4 + h2)
            for o2 in range(2):
                P.label = "l0.gb"
                oc = h2 * 2 + o2
                pb, tpb = psum()
                mm(pb[:, 0:T], [(wb_[:, (o2 * 8 + kc) * 128:(o2 * 8 + kc + 1) * 128], hb[:, kc, 0:T])
                                for kc in range(KC)], [twb] + thb, tpb)
                zc, tzc = ZC[oc]
                tt("dve", yab[:, oc, 0:T], pb[:, 0:T], zc[:, 0:T], ALU.mult, [tpb, tzc], [tyab[oc]])
                frelease(zc)
                yield
        for h2 in range(2):
            P.label = "l0.u"
            wu, twu = wload(1, W_MI + 6 + h2)
            for o2 in range(2):
                P.label = "l0.u"
                jc = h2 * 2 + o2
                pu, tpu = psum()
                mm(pu[:, 0:T], [(wu[:, (o2 * 8 + kc) * 128:(o2 * 8 + kc + 1) * 128], hb[:, kc, 0:T])
                                for kc in range(KC)], [twu] + thb, tpu)
                ub, tub = fpool()
                act(ub[:, 0:T], pu[:, 0:T], AF.Gelu, [tpu], [tub])
                P.label = "l0.sgu"
                ps_, tp_ = psum()

                def fn(h, ps_=ps_, jc=jc):
                    ins = None
                    for tc in range(ntc):
                        for hh in range(2):
                            g = 2 * jc + hh
                            ins = h.matmul(ps_[64 * hh:64 * hh + 64, tc * 128:(tc + 1) * 128],
                                           lhsT=VN[tc][0][:, g * 64:(g + 1) * 64], rhs=WST[:, g, :],
                                           start=True, stop=True)
                    return ins
                op("pe", fn, [v[1] for v in VN] + [tCONST], [tp_], nmm=2 * ntc)
                tmp, ttmp = fpool()
                stt(tmp[:, 0:T].rearrange("p (a b) -> p a b", b=128),
                    ps_[:, 0:T].rearrange("p (a b) -> p a b", b=128),
                    SMALL[:, SM_GAM + jc:SM_GAM + jc + 1], BS2[:, jc:jc + 1, :].broadcast_to([128, ntc, 128]),
                    ALU.mult, ALU.add, [tp_, tCONST], [ttmp])
                tt("dve", yab[:, 4 + jc, 0:T], tmp[:, 0:T], ub[:, 0:T], ALU.mult, [ttmp, tub], [tyab[4 + jc]])
                yield
        for v_ in VN:
            brelease(v_[0])
        for pz in range(4):
            P.label = "l0.mixout"
            wo_, two = wload(1, W_MO + pz)
            for o2 in range(2):
                P.label = "l0.mixout"
                oc = pz * 2 + o2
                po, tpo = psum()
                mm(po[:, 0:T], [(wo_[:, (o2 * 8 + kc) * 128:(o2 * 8 + kc + 1) * 128], yab[:, kc, 0:T])
                                for kc in range(KC)], [two] + tyab, tpo)
                xa = XA[:, oc, t0:t0 + T]
                stt(xa, po[:, 0:T], gate_ap(0, 0, oc, j), xa, ALU.mult, ALU.add, [tpo, tXA[ti][oc], tCONST],
                    [tXA[ti][oc]])
                yield
        orelease(hb)
        orelease(yab)

    def qk_norm_rope(ps_, tp_, T, gvec, rope, dst, tdst, lab):
        P.label = lab
        sq, tsq = bpool()
        act(sq[:, 0:T], ps_[:, 0:T], AF.Square, [tp_], [tsq])
        yield
        P.label = lab
        pm_, tpm = psum()
        mm1(pm_[:, 0:T], ONES_HD, sq[:, 0:T], True, True, [tsq, tCONST], tpm)
        A_, tA = fpool()
        act(A_[:, 0:T], pm_[:, 0:T], AF.Sqrt, [tpm], [tA], bias=RMS_EPS)
        op("dve", lambda h: h.reciprocal(out=A_[:, 0:T], in_=A_[:, 0:T]), [tA], [tA])
        kn, tkn = fpool(hold=True)
        stt(kn[:, 0:T], ps_[:, 0:T], gvec, A_[:, 0:T], ALU.mult, ALU.mult, [tp_, tA, tCONST], [tkn])
        prelease(ps_)
        if rope is None:
            act(dst, kn[:, 0:T], AF.Copy, [tkn], tdst)
            frelease(kn)
            return
        yield
        P.label = lab
        (cs, tcs), (sn, tsn) = rope
        pp, tpp = psum()
        mm1(pp[:, 0:T], PMS, kn[:, 0:T], True, True, [tkn, tCONST], tpp)
        t1, tt1 = fpool()
        tt("pool", t1[:, 0:T], kn[:, 0:T], cs[:, 0:T], ALU.mult, [tkn, tcs], [tt1])
        tt("dve", kn[:, 0:T], pp[:, 0:T], sn[:, 0:T], ALU.mult, [tpp, tsn, tkn], [tkn])
        tt("pool", dst, t1[:, 0:T], kn[:, 0:T], ALU.add, [tt1, tkn], tdst)
        frelease(kn)

    def load_rope(t0, T):
        cs, tcs = fpool(hold=True)
        dma("sp", cs[:, 0:T], cosD[:, t0:t0 + T], [], [tcs], "misc")
        sn, tsn = fpool(hold=True)
        dma("sp", sn[:, 0:T], sinD[:, t0:t0 + T], [], [tsn], "misc")
        return (cs, tcs), (sn, tsn)

    def l1_kv_tile(b, ti, t0, T, is_ctx):
        P.label = "l1.kv"
        j = 4 if is_ctx else b
        hb, thb = make_h(1, ti, t0, T, j)
        rope = None if is_ctx else load_rope(t0, T)
        wk, twk = wload(1, W_K)
        wv, twv = wload(1, W_V)
        chains = []
        for kvh in range(2):
            P.label = "l1.kv"
            pk, tpk = psum(hold=True)
            mm(pk[:, 0:T], [(wk[:, (kvh * 8 + kc) * 128:(kvh * 8 + kc + 1) * 128], hb[:, kc, 0:T])
                            for kc in range(KC)], [twk] + thb, tpk)
            chains.append(qk_norm_rope(pk, tpk, T, SMALL[:, SM_KG:SM_KG + 1], rope, KT[:, kvh, t0:t0 + T],
                                       [tKT[kvh][ti]], "l1.kv"))
        vsteps = []
        for tc in range(T // 128):
            def vstep(tc=tc):
                P.label = "l1.kv"
                pv, tpv = psum()
                mm(pv[:, 0:256], [(hb[:, kc, tc * 128:(tc + 1) * 128], wv[:, kc * 256:(kc + 1) * 256])
                                  for kc in range(KC)], [twv] + thb, tpv)
                g = t0 // 128 + tc
                act(VT[:, g, :], pv[:, 0:256], AF.Copy, [tpv], [tVT[g]] + tZ)
            vsteps.append(vstep)

        def vgen():
            for f_ in vsteps:
                f_()
                yield
        for _ in kv_driver(chains, vgen()):
            yield
        orelease(hb)
        if rope is not None:
            frelease(rope[0][0])
            frelease(rope[1][0])

    def kv_driver(chains, vg):
        gens = list(chains) + [vg]
        while gens:
            for g in list(gens):
                try:
                    next(g)
                except StopIteration:
                    gens.remove(g)
            yield

    def l1_att(b, ti, t0, T, hm):
        j = b
        hb, thb = hm
        rope = load_rope(t0, T)
        at, tat = op8()
        NKC = NTOK // 128

        def qproj(pz):
            P.label = "l1.qproj"
            wq, twq = wload(1, W_Q + pz)
            for o2 in range(2):
                P.label = "l1.qproj"
                pq, tpq = psum(hold=True)
                mm(pq[:, 0:T], [(wq[:, (o2 * 8 + kc) * 128:(o2 * 8 + kc + 1) * 128], hb[:, kc, 0:T])
                                for kc in range(KC)], [twq] + thb, tpq)
                qt, tqt = bpool(hold=True)
                qres[pz].append((qt, tqt))
                for _ in qk_norm_rope(pq, tpq, T, QG[:, 0:1], rope, qt[:, 0:T], [tqt], "l1.qproj"):
                    yield
                yield

        def attend(hd_, qt, tqt):
            P.label = "l1.attn"
            kv = hd_ // 4
            po, tpo = PS[4], tPS[4]
            pd, tpd = PS[5], tPS[5]

            def smm(kc):
                ps_, tp_ = psum(hold=True)
                ktile = min(kc // 4, 4)
                mm1(ps_[:, 0:T], KT[:, kv, kc * 128:(kc + 1) * 128], qt[:, 0:T], True, True,
                    [tKT[kv][ktile], tqt], tp_)
                return ps_, tp_
            nxt = smm(0)
            prev_pt = None
            for kc in range(NKC):
                P.label = "l1.attn"
                cur = nxt
                if kc + 1 < NKC:
                    nxt = smm(kc + 1)
                pt, tpt = bpool(hold=True)
                act(pt[:, 0:T], cur[0][:, 0:T], AF.Exp, [cur[1]], [tpt])
                prelease(cur[0])
                mm1(po[:, 0:T], VT[:, kc, kv * 128:(kv + 1) * 128], pt[:, 0:T], kc == 0, kc == NKC - 1,
                    [tVT[kc], tpt], tpo)
                if kc % 2 == 0:
                    prev_pt = (pt, tpt)
                else:
                    p2, tp2 = bpool()
                    tt("dve", p2[:, 0:T], prev_pt[0][:, 0:T], pt[:, 0:T], ALU.add, [prev_pt[1], tpt], [tp2])
                    mm1(pd[:, 0:T], ONES_1, p2[:, 0:T], kc == 1, kc == NKC - 1, [tp2, tCONST], tpd)
                    brelease(prev_pt[0])
                    brelease(pt)
                    yield
            P.label = "l1.attn"
            rd, trd = fpool()
            op("dve", lambda h: h.reciprocal(out=rd[:, 0:T], in_=pd[:, 0:T]), [tpd], [trd])
            tt("dve", at[:, hd_, 0:T], po[:, 0:T], rd[:, 0:T], ALU.mult, [tpo, trd], [tat[hd_]])
            brelease(qt)

        qres = [[] for _ in range(4)]
        for _ in qproj(0):
            yield
        for pz in range(4):
            nq = qproj(pz + 1) if pz + 1 < 4 else None
            for o2 in range(2):
                qt, tqt = qres[pz][o2]
                for _ in attend(pz * 2 + o2, qt, tqt):
                    if nq is not None:
                        try:
                            next(nq)
                        except StopIteration:
                            nq = None
                    yield
            if nq is not None:
                for _ in nq:
                    yield
        orelease(hb)
        frelease(rope[0][0])
        frelease(rope[1][0])
        for pz in range(4):
            P.label = "l1.out"
            wo_, two = wload(1, W_O + pz)
            for o2 in range(2):
                P.label = "l1.out"
                oc = pz * 2 + o2
                po, tpo = psum()
                mm(po[:, 0:T], [(wo_[:, (o2 * 8 + kc) * 128:(o2 * 8 + kc + 1) * 128], at[:, kc, 0:T])
                                for kc in range(KC)], [two] + tat, tpo)
                xa = XA[:, oc, t0:t0 + T]
                stt(xa, po[:, 0:T], gate_ap(1, 0, oc, j), xa, ALU.mult, ALU.add, [tpo, tXA[ti][oc], tCONST],
                    [tXA[ti][oc]])
                yield
        orelease(at)

    def load_tile(b, ti, t0, T, is_ctx):
        src = ctxT[b] if is_ctx else xT[b][:, :, t0:t0 + T]
        dma("sp", XA[:, :, t0:t0 + T], src, [], tXA[ti], f"xin{ti}")
        for c in range(KC):
            e = "pool" if c % 2 == 0 else "dve"
            ts(e, XA[:, c, t0:t0 + T], XA[:, c, t0:t0 + T], ALPHA, None, ALU.mult, None, [tXA[ti][c]], [tXA[ti][c]])

    def store_tile(b, ti, t0, T):
        dma("sp", outT[b][:, :, t0:t0 + T], XA[:, :, t0:t0 + T], tXA[ti], [], f"out{ti}")

    def pipeline(n, front, ln1, back, ln2):
        res = [dict() for _ in range(n)]
        run_par(front(0, res[0]))
        if n > 1:
            run_par(ln1(0, res[0]), front(1, res[1]))
        else:
            run_par(ln1(0, res[0]))
        for k in range(n):
            nxt_ln1 = ln1(k + 1, res[k + 1]) if k + 1 < n else None
            run_par(nxt_ln1, back(k, res[k]))
            nxt_front = front(k + 2, res[k + 2]) if k + 2 < n else None
            run_par(ln2(k, res[k]), nxt_front)

    for (ti, t0, T, is_ctx) in TILES:
        load_tile(0, ti, t0, T, is_ctx)
    for b in range(nb):
        flush_deferred()
        P.label = "l0.pre"
        ps_set["banks"] = [0, 1, 2, 3, 4, 5]
        for c in range(KC):
            ts("dve", HE[:, c, :].rearrange("p (a b) -> p a b", b=2),
               XA[:, c, 511:2047].rearrange("p (a b) -> p a b", b=512)[:, :, 0:2],
               sch_ap(0, 0, c, b), shift_ap(0, 0, c, b), ALU.mult, ALU.add, [tXA[0][c], tXA[1][c], tXA[2][c],
                                                                                 tXA[3][c], tCONST], [tHE])

        def l0_front(k, res, b=b):
            (ti, t0, T, is_ctx) = TILES[k]
            hm = make_h(0, ti, t0, T, 4 if is_ctx else b)
            return l0_mixer(b, ti, t0, T, is_ctx, hm)

        def l0_ln1(k, res, b=b):
            (ti, t0, T, is_ctx) = TILES[k]
            return layer_norm(0, 0, ti, t0, T, 4 if is_ctx else b, res)

        def l0_back(k, res, b=b):
            (ti, t0, T, is_ctx) = TILES[k]
            return ffn(0, ti, t0, T, 4 if is_ctx else b, res)

        def l0_ln2(k, res, b=b):
            (ti, t0, T, is_ctx) = TILES[k]
            return layer_norm(0, 1, ti, t0, T, 4 if is_ctx else b, res)

        pipeline(5, l0_front, l0_ln1, l0_back, l0_ln2)
        if stop_after_l0:
            for (ti, t0, T, is_ctx) in TILES[:4]:
                store_tile(b, ti, t0, T)
            continue
        ps_set["banks"] = [0, 1, 2, 3]
        for (ti, t0, T, is_ctx) in TILES:
            run_par(l1_kv_tile(b, ti, t0, T, is_ctx))
        if b + 1 < nb:
            deferred.append((lambda b=b: load_tile(b + 1, 4, 2048, 256, True)))

        def l1_front(k, res, b=b):
            (ti, t0, T, is_ctx) = TILES[k]
            hm = make_h(1, ti, t0, T, b)
            return l1_att(b, ti, t0, T, hm)

        def l1_ln1(k, res, b=b):
            (ti, t0, T, is_ctx) = TILES[k]
            return layer_norm(1, 0, ti, t0, T, b, res)

        def l1_back(k, res, b=b):
            (ti, t0, T, is_ctx) = TILES[k]
            return ffn(1, ti, t0, T, b, res)

        def l1_ln2(k, res, b=b):
            (ti, t0, T, is_ctx) = TILES[k]

            def g():
                for _ in layer_norm(1, 1, ti, t0, T, b, res):
                    yield

                def fin(b=b, ti=ti, t0=t0, T=T):
                    store_tile(b, ti, t0, T)
                    if b + 1 < nb:
                        load_tile(b + 1, ti, t0, T, False)
                deferred.append(fin)
            return g()

        pipeline(4, l1_front, l1_ln1, l1_back, l1_ln2)
    flush_deferred()
    import os
    if os.environ.get("KDBG_LABELS"):
        P.dbg = []
    P.emit()
    if os.environ.get("KDBG_LABELS"):
        import json
        json.dump(P.dbg, open(os.environ["KDBG_LABELS"], "w"))
    return P


_CACHE = {}


def kernel(**inputs):
    nb = 4
    if "P" not in _CACHE:
        _CACHE["P"] = build(nb)
    P = _CACHE["P"]
    shared = prepare_shared(inputs)
    in_maps = []
    for core in range(NCORES):
        m = dict(shared)
        m.update(prepare_core(inputs, core, nb))
        in_maps.append(m)
    res = run_bass_kernel_spmd(P.nc, in_maps, core_ids=list(range(NCORES)))
    outs = []
    for core in range(NCORES):
        o = np.asarray(res.results[core]["outT"])
        outs.append(o.transpose(0, 3, 2, 1).reshape(nb, S, D))
    return np.ascontiguousarray(np.concatenate(outs, axis=0), dtype=np.float32)
```

```python
import numpy as np
import concourse.bass as bass
import concourse.mybir as mybir
from concourse.bass_utils import run_bass_kernel_spmd

F32 = mybir.dt.float32
BF16 = mybir.dt.bfloat16
AF = mybir.ActivationFunctionType
ALU = mybir.AluOpType

NCORES = 8
D = 1024
KC = 8
S = 2048
CTX = 256
NTOK = S + CTX
FF = 2816
FC = 22
NH = 8
HD = 128
ALPHA = float((2.0 * 2) ** 0.25)
LN_EPS = 1e-5
RMS_EPS = 1e-6
GRID_W = 64
NSLOT = 4
SLOTW = 2816
NFP = 11
NBP = 12
FPW = 516


class Tok:
    __slots__ = ("w", "r", "const")

    def __init__(self, const=False):
        self.w = {}
        self.r = {}
        self.const = const


class Prog:
    def __init__(self):
        self.nc = bass.Bass("TRN2", target_bir_lowering=False)
        nc = self.nc
        self.eng = {"pe": nc.tensor, "act": nc.scalar, "dve": nc.vector, "pool": nc.gpsimd, "sp": nc.sync}
        self.ops = []
        self.last_dma = {}
        self.label = "setup"
        self.pe_labels = []
        self.dbg = None

    def op(self, eng, fn, reads=(), writes=(), dma=None, nmm=0):
        deps = {}
        if nmm:
            self.pe_labels.extend([self.label] * nmm)

        def add(d):
            for k, i in d.items():
                if deps.get(k, -1) < i:
                    deps[k] = i

        for t in reads:
            add(t.w)
        for t in writes:
            add(t.w)
            add(t.r)
        i = len(self.ops)
        key = dma if dma is not None else eng
        if dma is not None:
            if key in self.last_dma:
                add({key: self.last_dma[key]})
            self.last_dma[key] = i
        self.ops.append([eng, fn, deps, key, dma is not None, None, False, self.label])
        for t in reads:
            if not t.const:
                if t.r.get(key, -1) < i:
                    t.r[key] = i
        for t in writes:
            t.w = {key: i}
            t.r = {}
        return i

    def emit(self):
        nc = self.nc
        ops = self.ops
        needed = set()
        for o in ops:
            needed.update(o[2].values())
        keys = []
        for o in ops:
            if o[3] not in keys:
                keys.append(o[3])
        import contextlib
        with contextlib.ExitStack() as es:
            sems = {k: es.enter_context(nc.semaphore("s_" + k)) for k in keys}
            cnt = {k: 0 for k in keys}
            for i, o in enumerate(ops):
                if o[4]:
                    cnt[o[3]] += 16
                    o[5] = cnt[o[3]]
                elif i in needed:
                    cnt[o[3]] += 1
                    o[5] = cnt[o[3]]
                    o[6] = True
            seen = {e: {} for e in self.eng}
            nwait = 0
            for i, o in enumerate(ops):
                e = o[0]
                h = self.eng[e]
                for k, d in o[2].items():
                    p = ops[d]
                    if e == "pe" and p[0] == "pe" and not p[4]:
                        continue
                    v = p[5]
                    if seen[e].get(k, 0) < v:
                        h.wait_ge(sems[k], v)
                        seen[e][k] = v
                        nwait += 1
                ins = o[1](h)
                if self.dbg is not None:
                    try:
                        self.dbg.append((e, int(ins.ins.name.split("-")[1]), o[7]))
                    except Exception:
                        pass
                if o[4]:
                    ins.then_inc(sems[o[3]], 16)
                elif o[6]:
                    ins.then_inc(sems[e], 1)
            for k in keys:
                if cnt[k] > 0 and seen["sp"].get(k, 0) < cnt[k]:
                    nc.sync.wait_ge(sems[k], cnt[k])
            self.nwait = nwait
        return nc


def _piece_lhsT(W, ocs):
    K, N = W.shape
    kc, no = K // 128, N // 128
    A = W.reshape(kc, 128, no, 128).transpose(2, 1, 0, 3)
    A = A.reshape(no // ocs, ocs, 128, kc, 128).transpose(0, 2, 1, 3, 4)
    return np.ascontiguousarray(A.reshape(no // ocs, 128, ocs * kc * 128))


def _piece_moving(W, kcs):
    K, N = W.shape
    kc = K // 128
    A = W.reshape(kc, 128, N).transpose(1, 0, 2)
    A = A.reshape(128, kc // kcs, kcs * N).transpose(1, 0, 2)
    return np.ascontiguousarray(A)


def _vec_pc(v):
    v = np.asarray(v)
    lead = v.shape[:-1]
    n = v.shape[-1] // 128
    A = v.reshape(lead + (n, 128))
    A = np.moveaxis(A, -1, 0)
    return np.ascontiguousarray(A)


W_MI = 0
W_MO = 10
W_FI0 = 14
W_Q = 36
W_K = 40
W_V = 41
W_O = 42
W_FI1 = 46
NW1 = 68
NW2 = 16


def _rope_tables():
    f32 = np.float32
    t = np.arange(S)
    row = (t // GRID_W).astype(f32)
    col = (t % GRID_W).astype(f32)
    nf = HD // 4
    inv = (np.float32(10000.0) ** (-(np.arange(nf, dtype=f32) / f32(nf)))).astype(f32)
    ang_r = (row[:, None] * inv[None, :]).astype(f32)
    ang_c = (col[:, None] * inv[None, :]).astype(f32)
    cr, sr = np.cos(ang_r).astype(f32), np.sin(ang_r).astype(f32)
    cc, sc = np.cos(ang_c).astype(f32), np.sin(ang_c).astype(f32)
    COS = np.concatenate([cr, cr, cc, cc], axis=1).T
    SIN = np.concatenate([-sr, sr, -sc, sc], axis=1).T
    perm = np.arange(128)
    perm = np.where((perm // 32) % 2 == 0, perm + 32, perm - 32)
    PM = np.zeros((128, 128), f32)
    PM[perm, np.arange(128)] = 1.0
    return np.ascontiguousarray(COS, dtype=f32), np.ascontiguousarray(SIN, dtype=f32), PM


def prepare_shared(inp):
    f = lambda k: np.asarray(inp[k], dtype=np.float32)
    mix_in, mix_out = f("mix_w_in")[0], f("mix_w_out")[0]
    fin, fout = f("ffn_w_in"), f("ffn_w_out")
    qkv, wo = f("attn_w_qkv")[0], f("attn_w_out")[0]
    w1 = np.empty((NW1, 128, 2048), np.float32)
    w1[W_MI + 0:W_MI + 2] = _piece_lhsT(mix_in[:, 512:1024], 2)
    w1[W_MI + 2:W_MI + 4] = _piece_lhsT(mix_in[:, 1024:1536], 2)
    w1[W_MI + 4:W_MI + 6] = _piece_lhsT(mix_in[:, 0:512], 2)
    w1[W_MI + 6:W_MI + 8] = _piece_lhsT(mix_in[:, 1536:2048], 2)
    w1[W_MI + 8:W_MI + 10] = _piece_moving(mix_in[:, 2048:2560], 4)
    w1[W_MO:W_MO + 4] = _piece_lhsT(mix_out, 2)
    for l, base in ((0, W_FI0), (1, W_FI1)):
        g = _piece_lhsT(fin[l][:, :FF], 1)
        u = _piece_lhsT(fin[l][:, FF:], 1)
        w1[base:base + FC] = np.concatenate([g, u], axis=2)
    w1[W_Q:W_Q + 4] = _piece_lhsT(qkv[:, 0:1024], 2)
    w1[W_K:W_K + 1] = _piece_lhsT(qkv[:, 1024:1280], 2)
    w1[W_V:W_V + 1] = _piece_moving(qkv[:, 1280:1536], 8)
    w1[W_O:W_O + 4] = _piece_lhsT(wo, 2)
    w2 = np.empty((NW2, 128, SLOTW), np.float32)
    w2[0:8] = _piece_lhsT(fout[0], 1)
    w2[8:16] = _piece_lhsT(fout[1], 1)
    ada = f("ada_w")
    A = ada.reshape(2, 2, 4, 128, 48, 128)
    A = A.transpose(0, 4, 1, 3, 2, 5)
    adaw = np.ascontiguousarray(A.reshape(2 * 48 * 2, 128, 512))
    adab = _vec_pc(f("ada_b")).reshape(128, 96)
    lng = _vec_pc(f("ln_g")).reshape(128, 32)
    lnb = _vec_pc(f("ln_b")).reshape(128, 32)
    cw = np.ascontiguousarray(_vec_pc(f("conv_w")[0]).transpose(0, 2, 1)).reshape(128, 12)
    wsT = np.ascontiguousarray(f("sgu_w")[0].transpose(2, 0, 1)).reshape(128, 1024)
    sb = f("sgu_b")[0]
    bsr = np.ascontiguousarray(
        np.repeat(sb.reshape(4, 2, 1, 128), 64, axis=2).reshape(4, 128, 128).transpose(1, 0, 2)).reshape(128, 512)
    gam = _vec_pc(f("sgu_ln_g")[0]).reshape(128, 4)
    betb = np.ascontiguousarray(np.broadcast_to(f("sgu_ln_b")[0][None, :], (128, 512)))
    qg = f("q_norm_g")[0].reshape(128, 1)
    kg = f("k_norm_g")[0].reshape(128, 1)
    COS, SIN, PM = _rope_tables()
    small = np.concatenate([adab, lng, lnb, cw, gam, qg, kg], axis=1)
    return dict(w1=w1, w2=w2, adaw=adaw, small=np.ascontiguousarray(small), wsT=wsT, bsr=bsr, betb=betb,
                cosT=COS, sinT=SIN, pm=PM)


SM_ADAB, SM_LNG, SM_LNB, SM_CW, SM_GAM, SM_QG, SM_KG, SM_W = 0, 96, 128, 160, 172, 176, 177, 178


def prepare_core(inp, core, nb):
    x = np.asarray(inp["x"], dtype=np.float32)[core * nb:(core + 1) * nb]
    ctx = np.asarray(inp["ctx"], dtype=np.float32)[core * nb:(core + 1) * nb]
    c = np.asarray(inp["c"], dtype=np.float32)[core * nb:(core + 1) * nb]
    cctx = np.asarray(inp["c_ctx"], dtype=np.float32)
    xT = np.ascontiguousarray(x.reshape(nb, S, KC, 128).transpose(0, 3, 2, 1))
    ctxT = np.ascontiguousarray(ctx.reshape(nb, CTX, KC, 128).transpose(0, 3, 2, 1))
    cc = np.zeros((5, D), np.float32)
    cc[:nb] = c
    cc[4] = cctx
    ccT = np.ascontiguousarray(cc.reshape(5, KC, 128).transpose(2, 1, 0))
    return dict(xT=xT, ctxT=ctxT, cc=ccT.reshape(128, 40))


def build(nb=4, stop_after_l0=False):
    P = Prog()
    nc = P.nc
    op = P.op

    def dram(name, shape, dt, kind):
        return nc.dram_tensor(name, list(shape), dt, kind=kind).ap()

    xT = dram("xT", [nb, 128, KC, S], F32, "ExternalInput")
    ctxT = dram("ctxT", [nb, 128, KC, CTX], F32, "ExternalInput")
    ccD = dram("cc", [128, 40], F32, "ExternalInput")
    w1D = dram("w1", [NW1, 128, 2048], F32, "ExternalInput")
    w2D = dram("w2", [NW2, 128, SLOTW], F32, "ExternalInput")
    adawD = dram("adaw", [192, 128, 512], F32, "ExternalInput")
    smallD = dram("small", [128, SM_W], F32, "ExternalInput")
    wsTD = dram("wsT", [128, 1024], F32, "ExternalInput")
    bsrD = dram("bsr", [128, 512], F32, "ExternalInput")
    betbD = dram("betb", [128, 512], F32, "ExternalInput")
    cosD = dram("cosT", [128, S], F32, "ExternalInput")
    sinD = dram("sinT", [128, S], F32, "ExternalInput")
    pmD = dram("pm", [128, 128], F32, "ExternalInput")
    outT = dram("outT", [nb, 128, KC, S], F32, "ExternalOutput")
    w1B = dram("w1b", [NW1, 128, 2048], BF16, "Internal")
    w2B = dram("w2b", [NW2, 128, SLOTW], BF16, "Internal")

    def sb(name, shape, dt):
        return nc.alloc_sbuf_tensor(name, list(shape), dt).ap()

    XA = sb("XA", [128, KC, NTOK], F32)
    OP8 = [sb(f"OP8_{i}", [128, KC, 512], BF16) for i in range(3)]
    WR = [sb(f"WR{i}", [128, SLOTW], BF16) for i in range(NSLOT)]
    HID = sb("HID", [128, FC, 512], BF16)
    FPB = [sb(f"FP{i}", [128, FPW], F32) for i in range(NFP)]
    BPB = [sb(f"BP{i}", [128, 512], BF16) for i in range(NBP)]
    ZH = sb("ZH", [128, 4, 6], F32)
    GH = sb("GH", [128, 4, 6], F32)
    HE = sb("HE", [128, KC, 6], BF16)
    KT = sb("KT", [128, 2, NTOK], BF16)
    RG = sb("RG", [128, (NTOK // 128) * 256], BF16)
    VT = RG.rearrange("p (a b) -> p a b", b=256)
    Z = RG[:, 0:4 * 514 * 2].bitcast(F32).rearrange("p (a b) -> p a b", b=514)
    SMALL = sb("SMALL", [128, SM_W], F32)
    CC = sb("CC", [128, KC, 5], F32)
    SIL = sb("SIL", [128, KC, 5], F32)
    MOD = sb("MOD", [128, 96, 5], F32)
    SCH = sb("SCH", [128, 4, KC, 5], F32)
    GA = sb("GA", [128, 32], F32)
    BA = sb("BA", [128, 32], F32)
    QG = sb("QG", [128, 1], F32)
    WST = sb("WST", [128, 8, 128], BF16)
    WSTF = FPB
    BS2 = sb("BS2", [128, 4, 128], F32)
    ONES_LN = sb("ONES_LN", [128, 128], BF16)
    ONES_HD = sb("ONES_HD", [128, 128], BF16)
    ONES_1 = sb("ONES_1", [128, 128], BF16)
    PMS = sb("PMS", [128, 128], F32)
    PMSB = sb("PMSB", [128, 128], BF16)
    STAT = [sb(f"STAT{i}", [128, 16], F32) for i in range(6)]
    PS = [nc.alloc_psum_tensor(f"PS{i}", [128, 512], F32).ap() for i in range(8)]

    tXA = [[Tok() for _ in range(KC)] for _ in range(5)]
    tOP8 = [[Tok() for _ in range(KC)] for _ in range(3)]
    tWR = [Tok() for _ in range(NSLOT)]
    tHID = [Tok() for _ in range(FC)]
    tFP = [Tok() for _ in range(NFP)]
    tBP = [Tok() for _ in range(NBP)]
    tZ = [Tok() for _ in range(4)]
    tZH, tGH, tHE = Tok(), Tok(), Tok()
    tKT = [[Tok() for _ in range(5)] for _ in range(2)]
    tVT = [Tok() for _ in range(NTOK // 128)]
    tSTAT = [Tok() for _ in range(6)]
    tPS = [Tok() for _ in range(8)]
    tCONST = Tok(const=True)
    tW1 = [Tok(const=True) for _ in range(NW1)]
    tW2 = [Tok(const=True) for _ in range(NW2)]

    st = dict(wr=0, fp=0, bp=0, ps=0, stat=0, op8=0, ndma=0)

    held = {"fp": set(), "bp": set()}

    def _pool(kind, n, hold):
        i = st[kind]
        k = 0
        while i in held[kind]:
            i = (i + 1) % n
            k += 1
            assert k <= n, "pool exhausted " + kind
        st[kind] = (i + 1) % n
        if hold:
            held[kind].add(i)
        return i

    def fpool(hold=False):
        i = _pool("fp", NFP, hold)
        return FPB[i], tFP[i]

    def bpool(hold=False):
        i = _pool("bp", NBP, hold)
        return BPB[i], tBP[i]

    def frelease(buf):
        held["fp"].discard([k for k in range(NFP) if FPB[k] is buf][0])

    def brelease(buf):
        held["bp"].discard([k for k in range(NBP) if BPB[k] is buf][0])

    def spool():
        i = st["stat"]
        st["stat"] = (i + 1) % 6
        return STAT[i], tSTAT[i]

    ps_set = {"banks": [0, 1, 2, 3, 4, 5]}
    ps_held = set()

    def psum(hold=False):
        banks = ps_set["banks"]
        k = 0
        while True:
            i = banks[st["ps"] % len(banks)]
            st["ps"] += 1
            if i not in ps_held:
                break
            k += 1
            assert k <= len(banks), "psum exhausted"
        if hold:
            ps_held.add(i)
        return PS[i], tPS[i]

    def prelease(ps_):
        ps_held.discard([k for k in range(8) if PS[k] is ps_][0])

    op8_held = set()

    def op8():
        i = st["op8"]
        k = 0
        while i in op8_held:
            i = (i + 1) % 3
            k += 1
            assert k <= 3, "op8 exhausted"
        st["op8"] = (i + 1) % 3
        op8_held.add(i)
        return OP8[i], tOP8[i]

    def orelease(buf):
        op8_held.discard([k for k in range(3) if OP8[k] is buf][0])

    def dma(eng, out, in_, reads, writes, stream):
        if stream in ("misc", "cast"):
            k = st["ndma"]
            st["ndma"] = k + 1
            stream = stream + str(k % 4)
        return op(eng, lambda h: h.dma_start(out=out, in_=in_), reads, writes, dma=stream)

    def wload(which, idx):
        s = st["wr"]
        st["wr"] = (s + 1) % NSLOT
        if which == 1:
            src, tk, n = w1B[idx], tW1[idx], 2048
        else:
            src, tk, n = w2B[idx], tW2[idx], SLOTW
        dma("sp", WR[s][:, 0:n], src, [tk], [tWR[s]], f"w{s}")
        return WR[s], tWR[s]

    def mm(out, pairs, reads, wtok):
        def fn(h):
            n = len(pairs)
            ins = None
            for i, (l, r) in enumerate(pairs):
                ins = h.matmul(out, lhsT=l, rhs=r, start=(i == 0), stop=(i == n - 1))
            return ins
        return op("pe", fn, reads, [wtok], nmm=len(pairs))

    def mm1(out, l, r, start, stop, reads, wtok):
        return op("pe", lambda h: h.matmul(out, lhsT=l, rhs=r, start=start, stop=stop), reads, [wtok], nmm=1)

    def tt(e, out, a, b, o, reads, writes):
        return op(e, lambda h: h.tensor_tensor(out=out, in0=a, in1=b, op=o), reads, writes)

    def ts(e, out, a, s1, s2, o1, o2, reads, writes):
        if s2 is None:
            return op(e, lambda h: h.tensor_scalar(out=out, in0=a, scalar1=s1, scalar2=None, op0=o1), reads, writes)
        return op(e, lambda h: h.tensor_scalar(out=out, in0=a, scalar1=s1, scalar2=s2, op0=o1, op1=o2), reads, writes)

    def stt(out, a, s, b, o1, o2, reads, writes):
        return op("dve", lambda h: h.scalar_tensor_tensor(out=out, in0=a, scalar=s, in1=b, op0=o1, op1=o2),
                  reads, writes)

    def act(out, in_, func, reads, writes, bias=None, scale=None):
        kw = {}
        if bias is not None:
            kw["bias"] = bias
        if scale is not None:
            kw["scale"] = scale
        return op("act", lambda h: h.activation(out=out, in_=in_, func=func, **kw), reads, writes)

    order1 = list(range(W_MI, W_MI + 10)) + list(range(W_MO, W_MO + 4)) + list(range(W_FI0, W_FI0 + FC))
    order1b = list(range(W_Q, NW1))

    def cast1(lo, hi):
        dma("pool", w1B[lo:hi], w1D[lo:hi], [], [tW1[i] for i in range(lo, hi)], "cast")

    def cast2(lo, hi):
        dma("pool", w2B[lo:hi], w2D[lo:hi], [], [tW2[i] for i in range(lo, hi)], "cast")

    dma("sp", SMALL, smallD, [], [tCONST], "misc")
    dma("sp", CC.rearrange("p a b -> p (a b)"), ccD, [], [tCONST], "misc")
    dma("sp", PMS, pmD, [], [tCONST], "misc")
    dma("sp", BS2.rearrange("p a b -> p (a b)"), bsrD, [], [tCONST], "misc")
    for lo in range(0, 36, 4):
        cast1(lo, lo + 4)
    cast2(0, 8)
    for lo in range(36, NW1, 4):
        cast1(lo, lo + 4)
    cast2(8, 16)

    op("dve", lambda h: h.memset(ONES_LN, 1.0 / D), [], [tCONST])
    op("dve", lambda h: h.memset(ONES_HD, 1.0 / HD), [], [tCONST])
    op("dve", lambda h: h.memset(ONES_1, 1.0), [], [tCONST])
    act(SIL, CC, AF.Silu, [tCONST], [tCONST])
    op("dve", lambda h: h.tensor_copy(out=PMSB, in_=PMS), [tCONST], [tCONST])
    ts("dve", GA[:, 0:24], SMALL[:, SM_LNG:SM_LNG + 24], ALPHA, None, ALU.mult, None, [tCONST], [tCONST])
    ts("dve", GA[:, 24:32], SMALL[:, SM_LNG + 24:SM_LNG + 32], 1.0, None, ALU.mult, None, [tCONST], [tCONST])
    ts("dve", BA[:, 0:24], SMALL[:, SM_LNB:SM_LNB + 24], ALPHA, None, ALU.mult, None, [tCONST], [tCONST])
    ts("dve", BA[:, 24:32], SMALL[:, SM_LNB + 24:SM_LNB + 32], 1.0, None, ALU.mult, None, [tCONST], [tCONST])
    ts("dve", QG, SMALL[:, SM_QG:SM_QG + 1], float(HD ** -0.5), None, ALU.mult, None, [tCONST], [tCONST])

    wf = []
    for hlf in range(2):
        b_, t_ = fpool(hold=True)
        dma("sp", b_[:, 0:512], wsTD[:, hlf * 512:(hlf + 1) * 512], [], [t_], "misc")
        wf.append((b_, t_))
        op("dve", (lambda b_=b_, hlf=hlf: lambda h: h.tensor_copy(
            out=WST[:, hlf * 4:(hlf + 1) * 4, :].rearrange("p a b -> p (a b)"), in_=b_[:, 0:512]))(),
           [t_], [tCONST])
    bb, tbb = fpool(hold=True)
    dma("sp", bb[:, 0:512], betbD, [], [tbb], "misc")
    for jc in range(4):
        ps_, tp_ = psum()
        for hh in range(2):
            g = 2 * jc + hh
            wsrc, twsrc = wf[g // 4]
            mm1(ps_[64 * hh:64 * hh + 64, 0:128], bb[:, jc * 128 + 64 * hh:jc * 128 + 64 * hh + 64],
                wsrc[:, (g % 4) * 128:(g % 4) * 128 + 128], True, True, [tbb, twsrc], tp_)
        tt("dve", BS2[:, jc, :], ps_[:, 0:128], BS2[:, jc, :], ALU.add, [tp_, tCONST], [tCONST])
    frelease(bb)
    frelease(wf[0][0])
    frelease(wf[1][0])

    for l in range(2):
        for oc in range(48):
            bufs = []
            for hlf in range(2):
                b_, t_ = fpool()
                dma("sp", b_[:, 0:512], adawD[(l * 48 + oc) * 2 + hlf], [], [t_], "misc")
                bufs.append((b_, t_))
            ps_, tp_ = psum()
            pairs = []
            for kc in range(KC):
                b_, t_ = bufs[kc // 4]
                pairs.append((b_[:, (kc % 4) * 128:(kc % 4) * 128 + 128], SIL[:, kc, :]))
            mm(ps_[:, 0:5], pairs, [bufs[0][1], bufs[1][1], tCONST], tp_)
            ts("dve", MOD[:, l * 48 + oc, :], ps_[:, 0:5], SMALL[:, SM_ADAB + l * 48 + oc:SM_ADAB + l * 48 + oc + 1],
               None, ALU.add, None, [tp_, tCONST], [tCONST])
    for l in range(2):
        for s_ in range(2):
            base = l * 48 + (3 * s_ + 1) * 8
            ts("dve", SCH[:, l * 2 + s_].rearrange("p a b -> p (a b)"),
               MOD[:, base:base + 8, :].rearrange("p a b -> p (a b)"), 1.0, 1.0 / ALPHA, ALU.add, ALU.mult,
               [tCONST], [tCONST])

    def shift_ap(l, s_, c, j):
        return MOD[:, l * 48 + (3 * s_) * 8 + c, j:j + 1]

    def gate_ap(l, s_, c, j):
        return MOD[:, l * 48 + (3 * s_ + 2) * 8 + c, j:j + 1]

    def sch_ap(l, s_, c, j):
        return SCH[:, l * 2 + s_, c, j:j + 1]

    TILES = [(0, 0, 512, False), (1, 512, 512, False), (2, 1024, 512, False), (3, 1536, 512, False),
             (4, 2048, 256, True)]
    deferred = []

    def flush_deferred():
        for f_ in deferred:
            f_()
        deferred.clear()

    def make_h(l, ti, t0, T, j):
        buf, tk = op8()
        for c in range(KC):
            e = "dve" if c % 2 == 0 else "pool"
            ts(e, buf[:, c, 0:T], XA[:, c, t0:t0 + T], sch_ap(l, 0, c, j), shift_ap(l, 0, c, j), ALU.mult, ALU.add,
               [tXA[ti][c], tCONST], [tk[c]])
        return buf, tk

    def run_par(*gens):
        gens = [g for g in gens if g is not None]
        while gens:
            for g in list(gens):
                try:
                    next(g)
                except StopIteration:
                    gens.remove(g)

    def layer_norm(l, s_, ti, t0, T, j, res):
        lab = f"ln{l}{s_}"
        P.label = lab
        mean_ps, tm = PS[6], tPS[6]
        ex2_ps, te = PS[7], tPS[7]
        for c in range(KC):
            P.label = lab
            rb, trb = bpool()
            op("pool", (lambda rb=rb, c=c: lambda h: h.tensor_copy(out=rb[:, 0:T], in_=XA[:, c, t0:t0 + T]))(),
               [tXA[ti][c]], [trb])
            sq, tsq = bpool()
            e = "pool" if c % 4 != 3 else "dve"
            tt(e, sq[:, 0:T], XA[:, c, t0:t0 + T], XA[:, c, t0:t0 + T], ALU.mult, [tXA[ti][c]], [tsq])
            mm1(mean_ps[:, 0:T], ONES_LN, rb[:, 0:T], c == 0, c == KC - 1, [trb, tCONST], tm)
            mm1(ex2_ps[:, 0:T], ONES_LN, sq[:, 0:T], c == 0, c == KC - 1, [tsq, tCONST], te)
            if c % 2 == 1:
                yield
        P.label = lab
        msb, tmsb = fpool(hold=True)
        op("dve", lambda h: h.tensor_copy(out=msb[:, 0:T], in_=mean_ps[:, 0:T]), [tm], [tmsb])
        var, tvar = fpool(hold=True)
        tt("pool", var[:, 0:T], msb[:, 0:T], msb[:, 0:T], ALU.mult, [tmsb], [tvar])
        tt("dve", var[:, 0:T], ex2_ps[:, 0:T], var[:, 0:T], ALU.subtract, [te, tvar], [tvar])
        A_, tA = fpool(hold=True)
        act(A_[:, 0:T], var[:, 0:T], AF.Ln, [tvar], [tA], bias=LN_EPS)
        act(A_[:, 0:T], A_[:, 0:T], AF.Exp, [tA], [tA], scale=-0.5)
        frelease(var)
        B_, tB = fpool(hold=True)
        stt(B_[:, 0:T], msb[:, 0:T], -1.0, A_[:, 0:T], ALU.mult, ALU.mult, [tmsb, tA], [tB])
        hf = None
        if s_ == 0:
            hf = op8()
            res["hf"] = hf
        yield
        gi = (l * 2 + s_) * 8
        for c in range(KC):
            P.label = lab
            xa = XA[:, c, t0:t0 + T]
            tx = tXA[ti][c]
            on_dve = c in (1, 3, 5)
            if on_dve:
                stt(xa, xa, GA[:, gi + c:gi + c + 1], A_[:, 0:T], ALU.mult, ALU.mult, [tx, tA, tCONST], [tx])
                stt(xa, B_[:, 0:T], GA[:, gi + c:gi + c + 1], xa, ALU.mult, ALU.add, [tx, tB, tCONST], [tx])
                ts("dve", xa, xa, BA[:, gi + c:gi + c + 1], None, ALU.add, None, [tx, tCONST], [tx])
            else:
                tt("pool", xa, xa, A_[:, 0:T], ALU.mult, [tx, tA], [tx])
                tt("pool", xa, xa, B_[:, 0:T], ALU.add, [tx, tB], [tx])
                ts("pool", xa, xa, GA[:, gi + c:gi + c + 1], BA[:, gi + c:gi + c + 1], ALU.mult, ALU.add,
                   [tx, tCONST], [tx])
            if s_ == 0:
                e = "dve" if on_dve else "pool"
                ts(e, hf[0][:, c, 0:T], xa, sch_ap(l, 1, c, j), shift_ap(l, 1, c, j), ALU.mult, ALU.add,
                   [tx, tCONST], [hf[1][c]])
            if c % 2 == 1:
                yield
        frelease(msb)
        frelease(A_)
        frelease(B_)

    def ffn(l, ti, t0, T, j, res):
        hb, thb = res["hf"]
        base = W_FI0 if l == 0 else W_FI1
        for jj in range(FC):
            P.label = f"ffn_in{l}"
            w, tw = wload(1, base + jj)
            pg, tpg = psum()
            mm(pg[:, 0:T], [(w[:, kc * 128:(kc + 1) * 128], hb[:, kc, 0:T]) for kc in range(KC)], [tw] + thb, tpg)
            pu, tpu = psum()
            mm(pu[:, 0:T], [(w[:, (8 + kc) * 128:(9 + kc) * 128], hb[:, kc, 0:T]) for kc in range(KC)],
               [tw] + thb, tpu)
            sg, tsg = fpool()
            act(sg[:, 0:T], pg[:, 0:T], AF.Silu, [tpg], [tsg])
            tt("dve", HID[:, jj, 0:T], pu[:, 0:T], sg[:, 0:T], ALU.mult, [tpu, tsg], [tHID[jj]])
            if jj == 10:
                flush_deferred()
            yield
        orelease(hb)
        for oc in range(KC):
            P.label = f"ffn_out{l}"
            w, tw = wload(2, l * 8 + oc)
            po, tpo = psum()
            mm(po[:, 0:T], [(w[:, kc * 128:(kc + 1) * 128], HID[:, kc, 0:T]) for kc in range(FC)], [tw] + tHID, tpo)
            xa = XA[:, oc, t0:t0 + T]
            stt(xa, po[:, 0:T], gate_ap(l, 1, oc, j), xa, ALU.mult, ALU.add, [tpo, tXA[ti][oc], tCONST],
                [tXA[ti][oc]])
            yield

    def l0_mixer(b, ti, t0, T, is_ctx, hm):
        j = 4 if is_ctx else b
        hb, thb = hm
        ntc = T // 128
        do_halo = (ti == 0)
        P.label = "l0.v"
        wv0, twv0 = wload(1, W_MI + 8)
        wv1, twv1 = wload(1, W_MI + 9)
        VN = []
        VG = []
        sst, tst = spool()
        for tc in range(ntc):
            P.label = "l0.v"
            pv, tpv = psum()
            pairs = []
            for kc in range(KC):
                wv = wv0 if kc < 4 else wv1
                pairs.append((hb[:, kc, tc * 128:(tc + 1) * 128], wv[:, (kc % 4) * 512:(kc % 4 + 1) * 512]))
            mm(pv, pairs, [twv0, twv1] + thb, tpv)
            vg, tvg = fpool(hold=True)
            act(vg[:, 0:512], pv, AF.Gelu, [tpv], [tvg])
            s6, ts6 = spool()
            op("dve", (lambda s6=s6, vg=vg: lambda h: h.bn_stats(out=s6[:, 0:6], in_=vg[:, 0:512]))(),
               [tvg], [ts6])
            op("dve", (lambda s6=s6, tc=tc: lambda h: h.bn_aggr(out=sst[:, 2 * tc:2 * tc + 2], in_=s6[:, 0:6]))(),
               [ts6, tst], [tst])
            VG.append((vg, tvg))
            yield
        P.label = "l0.v"
        varv = sst[:, 0:2 * ntc].rearrange("p (a b) -> p a b", b=2)[:, :, 1:2]
        rsv = sst[:, 8:8 + ntc].rearrange("p (a b) -> p a b", b=1)
        act(rsv, varv, AF.Ln, [tst], [tst], bias=LN_EPS)
        act(rsv, rsv, AF.Exp, [tst], [tst], scale=-0.5)
        for tc in range(ntc):
            vg, tvg = VG[tc]
            vn, tvn = bpool(hold=True)
            ts("dve", vn, vg[:, 0:512], sst[:, 2 * tc:2 * tc + 1], sst[:, 8 + tc:9 + tc], ALU.subtract, ALU.mult,
               [tvg, tst], [tvn])
            frelease(vg)
            VN.append((vn, tvn))
        yield
        for h2 in range(2):
            P.label = "l0.mixA"
            wg, twg = wload(1, W_MI + 0 + h2)
            wh, twh = wload(1, W_MI + 2 + h2)
            for o2 in range(2):
                P.label = "l0.mixA"
                oc = h2 * 2 + o2
                pg, tpg = psum()
                mm(pg[:, 0:T], [(wg[:, (o2 * 8 + kc) * 128:(o2 * 8 + kc + 1) * 128], hb[:, kc, 0:T])
                                for kc in range(KC)], [twg] + thb, tpg)
                gt, tgt = fpool()
                act(gt[:, 0:T], pg[:, 0:T], AF.Copy, [tpg], [tgt])
                ph, tph = psum()
                mm(ph[:, 0:T], [(wh[:, (o2 * 8 + kc) * 128:(o2 * 8 + kc + 1) * 128], hb[:, kc, 0:T])
                                for kc in range(KC)], [twh] + thb, tph)
                tt("dve", Z[:, oc, 1:T + 1], ph[:, 0:T], gt[:, 0:T], ALU.mult, [tph, tgt], [tZ[oc]] + tVT)
                if do_halo:
                    pg2, tpg2 = psum()
                    mm(pg2[:, 0:6], [(wg[:, (o2 * 8 + kc) * 128:(o2 * 8 + kc + 1) * 128], HE[:, kc, :])
                                     for kc in range(KC)], [twg, tHE], tpg2)
                    act(GH[:, oc, :], pg2[:, 0:6], AF.Copy, [tpg2], [tGH])
                    ph2, tph2 = psum()
                    mm(ph2[:, 0:6], [(wh[:, (o2 * 8 + kc) * 128:(o2 * 8 + kc + 1) * 128], HE[:, kc, :])
                                     for kc in range(KC)], [twh, tHE], tph2)
                    tt("dve", ZH[:, oc, :], ph2[:, 0:6], GH[:, oc, :], ALU.mult, [tph2, tGH], [tZH])
                yield
        P.label = "l0.mixA"
        if is_ctx or ti == 0:
            op("pool", lambda h: h.memset(Z[:, :, 0:1], 0.0), [], tZ + tVT)
        else:
            op("pool", lambda h: h.tensor_copy(out=Z[:, :, 0:1], in_=ZH[:, :, 2 * (ti - 1):2 * (ti - 1) + 1]),
               [tZH], tZ + tVT)
        if is_ctx or ti == 3:
            op("pool", lambda h: h.memset(Z[:, :, T + 1:T + 2], 0.0), [], tZ + tVT)
        else:
            op("pool", lambda h: h.tensor_copy(out=Z[:, :, T + 1:T + 2], in_=ZH[:, :, 2 * ti + 1:2 * ti + 2]),
               [tZH], tZ + tVT)
        yab, tyab = op8()
        ZC = []
        for oc in range(4):
            zc, tzc = fpool(hold=True)
            cwa = lambda tap, oc=oc: SMALL[:, SM_CW + oc * 3 + tap:SM_CW + oc * 3 + tap + 1]
            ts("pool", zc[:, 0:T], Z[:, oc, 1:T + 1], cwa(1), None, ALU.mult, None, [tZ[oc], tCONST], [tzc])
            stt(zc[:, 0:T], Z[:, oc, 0:T], cwa(0), zc[:, 0:T], ALU.mult, ALU.add, [tZ[oc], tzc, tCONST], [tzc])
            stt(zc[:, 0:T], Z[:, oc, 2:T + 2], cwa(2), zc[:, 0:T], ALU.mult, ALU.add, [tZ[oc], tzc, tCONST], [tzc])
            ZC.append((zc, tzc))
        yield
        for h2 in range(2):
            P.label = "l0.gb"
            wb_, twb = wload(1, W_MI + 4 + h2)
            for o2 in range(2):
                P.label = "l0.gb"
                oc = h2 * 2 + o2
                pb, tpb = psum()
                mm(pb[:, 0:T], [(wb_[:, (o2 * 8 + kc) * 128:(o2 * 8 + kc + 1) * 128], hb[:, kc, 0:T])
                                for kc in range(KC)], [twb] + thb, tpb)
                zc, tzc = ZC[oc]
                tt("dve", yab[:, oc, 0:T], pb[:, 0:T], zc[:, 0:T], ALU.mult, [tpb, tzc], [tyab[oc]])
                frelease(zc)
                yield
        for h2 in range(2):
            P.label = "l0.u"
            wu, twu = wload(1, W_MI + 6 + h2)
            for o2 in range(2):
                P.label = "l0.u"
                jc = h2 * 2 + o2
                pu, tpu = psum()
                mm(pu[:, 0:T], [(wu[:, (o2 * 8 + kc) * 128:(o2 * 8 + kc + 1) * 128], hb[:, kc, 0:T])
                                for kc in range(KC)], [twu] + thb, tpu)
                ub, tub = fpool()
                act(ub[:, 0:T], pu[:, 0:T], AF.Gelu, [tpu], [tub])
                P.label = "l0.sgu"
                ps_, tp_ = psum()

                def fn(h, ps_=ps_, jc=jc):
                    ins = None
                    for tc in range(ntc):
                        for hh in range(2):
                            g = 2 * jc + hh
                            ins = h.matmul(ps_[64 * hh:64 * hh + 64, tc * 128:(tc + 1) * 128],
                                           lhsT=VN[tc][0][:, g * 64:(g + 1) * 64], rhs=WST[:, g, :],
                                           start=True, stop=True)
                    return ins
                op("pe", fn, [v[1] for v in VN] + [tCONST], [tp_], nmm=2 * ntc)
                tmp, ttmp = fpool()
                stt(tmp[:, 0:T].rearrange("p (a b) -> p a b", b=128),
                    ps_[:, 0:T].rearrange("p (a b) -> p a b", b=128),
                    SMALL[:, SM_GAM + jc:SM_GAM + jc + 1], BS2[:, jc:jc + 1, :].broadcast_to([128, ntc, 128]),
                    ALU.mult, ALU.add, [tp_, tCONST], [ttmp])
                tt("dve", yab[:, 4 + jc, 0:T], tmp[:, 0:T], ub[:, 0:T], ALU.mult, [ttmp, tub], [tyab[4 + jc]])
                yield
        for v_ in VN:
            brelease(v_[0])
        for pz in range(4):
            P.label = "l0.mixout"
            wo_, two = wload(1, W_MO + pz)
            for o2 in range(2):
                P.label = "l0.mixout"
                oc = pz * 2 + o2
                po, tpo = psum()
                mm(po[:, 0:T], [(wo_[:, (o2 * 8 + kc) * 128:(o2 * 8 + kc + 1) * 128], yab[:, kc, 0:T])
                                for kc in range(KC)], [two] + tyab, tpo)
                xa = XA[:, oc, t0:t0 + T]
                stt(xa, po[:, 0:T], gate_ap(0, 0, oc, j), xa, ALU.mult, ALU.add, [tpo, tXA[ti][oc], tCONST],
                    [tXA[ti][oc]])
                yield
        orelease(hb)
        orelease(yab)

    def qk_norm_rope(ps_, tp_, T, gvec, rope, dst, tdst, lab):
        P.label = lab
        sq, tsq = bpool()
        act(sq[:, 0:T], ps_[:, 0:T], AF.Square, [tp_], [tsq])
        yield
        P.label = lab
        pm_, tpm = psum()
        mm1(pm_[:, 0:T], ONES_HD, sq[:, 0:T], True, True, [tsq, tCONST], tpm)
        A_, tA = fpool()
        act(A_[:, 0:T], pm_[:, 0:T], AF.Ln, [tpm], [tA], bias=RMS_EPS)
        act(A_[:, 0:T], A_[:, 0:T], AF.Exp, [tA], [tA], scale=-0.5)
        if rope is None:
            stt(dst, ps_[:, 0:T], gvec, A_[:, 0:T], ALU.mult, ALU.mult, [tp_, tA, tCONST], tdst)
            prelease(ps_)
            return
        kn, tkn = bpool(hold=True)
        stt(kn[:, 0:T], ps_[:, 0:T], gvec, A_[:, 0:T], ALU.mult, ALU.mult, [tp_, tA, tCONST], [tkn])
        prelease(ps_)
        yield
        P.label = lab
        (cs, tcs), (sn, tsn) = rope
        pp, tpp = psum()
        mm1(pp[:, 0:T], PMSB, kn[:, 0:T], True, True, [tkn, tCONST], tpp)
        t1, tt1 = fpool()
        tt("pool", t1[:, 0:T], kn[:, 0:T], cs[:, 0:T], ALU.mult, [tkn, tcs], [tt1])
        t2, tt2 = fpool()
        tt("dve", t2[:, 0:T], pp[:, 0:T], sn[:, 0:T], ALU.mult, [tpp, tsn], [tt2])
        tt("dve", dst, t1[:, 0:T], t2[:, 0:T], ALU.add, [tt1, tt2], tdst)
        brelease(kn)

    def load_rope(t0, T):
        cs, tcs = fpool(hold=True)
        dma("sp", cs[:, 0:T], cosD[:, t0:t0 + T], [], [tcs], "misc")
        sn, tsn = fpool(hold=True)
        dma("sp", sn[:, 0:T], sinD[:, t0:t0 + T], [], [tsn], "misc")
        return (cs, tcs), (sn, tsn)

    def l1_kv_tile(b, ti, t0, T, is_ctx):
        P.label = "l1.kv"
        j = 4 if is_ctx else b
        hb, thb = make_h(1, ti, t0, T, j)
        rope = None if is_ctx else load_rope(t0, T)
        wk, twk = wload(1, W_K)
        wv, twv = wload(1, W_V)
        chains = []
        for kvh in range(2):
            P.label = "l1.kv"
            pk, tpk = psum(hold=True)
            mm(pk[:, 0:T], [(wk[:, (kvh * 8 + kc) * 128:(kvh * 8 + kc + 1) * 128], hb[:, kc, 0:T])
                            for kc in range(KC)], [twk] + thb, tpk)
            chains.append(qk_norm_rope(pk, tpk, T, SMALL[:, SM_KG:SM_KG + 1], rope, KT[:, kvh, t0:t0 + T],
                                       [tKT[kvh][ti]], "l1.kv"))
        vsteps = []
        for tc in range(T // 128):
            def vstep(tc=tc):
                P.label = "l1.kv"
                pv, tpv = psum()
                mm(pv[:, 0:256], [(hb[:, kc, tc * 128:(tc + 1) * 128], wv[:, kc * 256:(kc + 1) * 256])
                                  for kc in range(KC)], [twv] + thb, tpv)
                g = t0 // 128 + tc
                act(VT[:, g, :], pv[:, 0:256], AF.Copy, [tpv], [tVT[g]] + tZ)
            vsteps.append(vstep)

        def vgen():
            for f_ in vsteps:
                f_()
                yield
        for _ in kv_driver(chains, vgen()):
            yield
        orelease(hb)
        if rope is not None:
            frelease(rope[0][0])
            frelease(rope[1][0])

    def kv_driver(chains, vg):
        gens = list(chains) + [vg]
        while gens:
            for g in list(gens):
                try:
                    next(g)
                except StopIteration:
                    gens.remove(g)
            yield

    def l1_att(b, ti, t0, T, hm):
        j = b
        hb, thb = hm
        rope = load_rope(t0, T)
        at, tat = op8()
        NKC = NTOK // 128

        def qproj(pz):
            P.label = "l1.qproj"
            wq, twq = wload(1, W_Q + pz)
            for o2 in range(2):
                P.label = "l1.qproj"
                pq, tpq = psum(hold=True)
                mm(pq[:, 0:T], [(wq[:, (o2 * 8 + kc) * 128:(o2 * 8 + kc + 1) * 128], hb[:, kc, 0:T])
                                for kc in range(KC)], [twq] + thb, tpq)
                qt, tqt = bpool(hold=True)
                qres[pz].append((qt, tqt))
                for _ in qk_norm_rope(pq, tpq, T, QG[:, 0:1], rope, qt[:, 0:T], [tqt], "l1.qproj"):
                    yield
                yield

        def attend(hd_, qt, tqt):
            P.label = "l1.attn"
            kv = hd_ // 4
            po, tpo = PS[4], tPS[4]
            pd, tpd = PS[5], tPS[5]

            def smm(kc):
                ps_, tp_ = psum(hold=True)
                ktile = min(kc // 4, 4)
                mm1(ps_[:, 0:T], KT[:, kv, kc * 128:(kc + 1) * 128], qt[:, 0:T], True, True,
                    [tKT[kv][ktile], tqt], tp_)
                return ps_, tp_
            nxt = smm(0)
            prev_pt = None
            for kc in range(NKC):
                P.label = "l1.attn"
                cur = nxt
                if kc + 1 < NKC:
                    nxt = smm(kc + 1)
                pt, tpt = bpool(hold=True)
                act(pt[:, 0:T], cur[0][:, 0:T], AF.Exp, [cur[1]], [tpt])
                prelease(cur[0])
                mm1(po[:, 0:T], VT[:, kc, kv * 128:(kv + 1) * 128], pt[:, 0:T], kc == 0, kc == NKC - 1,
                    [tVT[kc], tpt], tpo)
                if kc % 2 == 0:
                    prev_pt = (pt, tpt)
                else:
                    p2, tp2 = bpool()
                    tt("dve", p2[:, 0:T], prev_pt[0][:, 0:T], pt[:, 0:T], ALU.add, [prev_pt[1], tpt], [tp2])
                    mm1(pd[:, 0:T], ONES_1, p2[:, 0:T], kc == 1, kc == NKC - 1, [tp2, tCONST], tpd)
                    brelease(prev_pt[0])
                    brelease(pt)
                    yield
            P.label = "l1.attn"
            rd, trd = fpool()
            act(rd[:, 0:T], pd[:, 0:T], AF.Ln, [tpd], [trd])
            act(rd[:, 0:T], rd[:, 0:T], AF.Exp, [trd], [trd], scale=-1.0)
            tt("dve", at[:, hd_, 0:T], po[:, 0:T], rd[:, 0:T], ALU.mult, [tpo, trd], [tat[hd_]])
            brelease(qt)

        qres = [[] for _ in range(4)]
        for _ in qproj(0):
            yield
        for pz in range(4):
            nq = qproj(pz + 1) if pz + 1 < 4 else None
            for o2 in range(2):
                qt, tqt = qres[pz][o2]
                for _ in attend(pz * 2 + o2, qt, tqt):
                    if nq is not None:
                        try:
                            next(nq)
                        except StopIteration:
                            nq = None
                    yield
            if nq is not None:
                for _ in nq:
                    yield
        orelease(hb)
        frelease(rope[0][0])
        frelease(rope[1][0])
        for pz in range(4):
            P.label = "l1.out"
            wo_, two = wload(1, W_O + pz)
            for o2 in range(2):
                P.label = "l1.out"
                oc = pz * 2 + o2
                po, tpo = psum()
                mm(po[:, 0:T], [(wo_[:, (o2 * 8 + kc) * 128:(o2 * 8 + kc + 1) * 128], at[:, kc, 0:T])
                                for kc in range(KC)], [two] + tat, tpo)
                xa = XA[:, oc, t0:t0 + T]
                stt(xa, po[:, 0:T], gate_ap(1, 0, oc, j), xa, ALU.mult, ALU.add, [tpo, tXA[ti][oc], tCONST],
                    [tXA[ti][oc]])
                yield
        orelease(at)

    def load_tile(b, ti, t0, T, is_ctx):
        src = ctxT[b] if is_ctx else xT[b][:, :, t0:t0 + T]
        dma("sp", XA[:, :, t0:t0 + T], src, [], tXA[ti], f"xin{ti}")
        for c in range(KC):
            e = "pool" if c % 2 == 0 else "dve"
            ts(e, XA[:, c, t0:t0 + T], XA[:, c, t0:t0 + T], ALPHA, None, ALU.mult, None, [tXA[ti][c]], [tXA[ti][c]])

    def store_tile(b, ti, t0, T):
        dma("sp", outT[b][:, :, t0:t0 + T], XA[:, :, t0:t0 + T], tXA[ti], [], f"out{ti}")

    def pipeline(n, front, ln1, back, ln2):
        res = [dict() for _ in range(n)]
        run_par(front(0, res[0]))
        if n > 1:
            run_par(ln1(0, res[0]), front(1, res[1]))
        else:
            run_par(ln1(0, res[0]))
        for k in range(n):
            nxt_ln1 = ln1(k + 1, res[k + 1]) if k + 1 < n else None
            run_par(nxt_ln1, back(k, res[k]))
            nxt_front = front(k + 2, res[k + 2]) if k + 2 < n else None
            run_par(ln2(k, res[k]), nxt_front)

    for (ti, t0, T, is_ctx) in TILES:
        load_tile(0, ti, t0, T, is_ctx)
    for b in range(nb):
        flush_deferred()
        P.label = "l0.pre"
        ps_set["banks"] = [0, 1, 2, 3, 4, 5]
        for c in range(KC):
            ts("dve", HE[:, c, :].rearrange("p (a b) -> p a b", b=2),
               XA[:, c, 511:2047].rearrange("p (a b) -> p a b", b=512)[:, :, 0:2],
               sch_ap(0, 0, c, b), shift_ap(0, 0, c, b), ALU.mult, ALU.add, [tXA[0][c], tXA[1][c], tXA[2][c],
                                                                                 tXA[3][c], tCONST], [tHE])

        def l0_front(k, res, b=b):
            (ti, t0, T, is_ctx) = TILES[k]
            hm = make_h(0, ti, t0, T, 4 if is_ctx else b)
            return l0_mixer(b, ti, t0, T, is_ctx, hm)

        def l0_ln1(k, res, b=b):
            (ti, t0, T, is_ctx) = TILES[k]
            return layer_norm(0, 0, ti, t0, T, 4 if is_ctx else b, res)

        def l0_back(k, res, b=b):
            (ti, t0, T, is_ctx) = TILES[k]
            return ffn(0, ti, t0, T, 4 if is_ctx else b, res)

        def l0_ln2(k, res, b=b):
            (ti, t0, T, is_ctx) = TILES[k]
            return layer_norm(0, 1, ti, t0, T, 4 if is_ctx else b, res)

        pipeline(5, l0_front, l0_ln1, l0_back, l0_ln2)
        if stop_after_l0:
            for (ti, t0, T, is_ctx) in TILES[:4]:
                store_tile(b, ti, t0, T)
            continue
        ps_set["banks"] = [0, 1, 2, 3]
        for (ti, t0, T, is_ctx) in TILES:
            run_par(l1_kv_tile(b, ti, t0, T, is_ctx))
        if b + 1 < nb:
            deferred.append((lambda b=b: load_tile(b + 1, 4, 2048, 256, True)))

        def l1_front(k, res, b=b):
            (ti, t0, T, is_ctx) = TILES[k]
            hm = make_h(1, ti, t0, T, b)
            return l1_att(b, ti, t0, T, hm)

        def l1_ln1(k, res, b=b):
            (ti, t0, T, is_ctx) = TILES[k]
            return layer_norm(1, 0, ti, t0, T, b, res)

        def l1_back(k, res, b=b):
            (ti, t0, T, is_ctx) = TILES[k]
            return ffn(1, ti, t0, T, b, res)

        def l1_ln2(k, res, b=b):
            (ti, t0, T, is_ctx) = TILES[k]

            def g():
                for _ in layer_norm(1, 1, ti, t0, T, b, res):
                    yield

                def fin(b=b, ti=ti, t0=t0, T=T):
                    store_tile(b, ti, t0, T)
                    if b + 1 < nb:
                        load_tile(b + 1, ti, t0, T, False)
                deferred.append(fin)
            return g()

        pipeline(4, l1_front, l1_ln1, l1_back, l1_ln2)
    flush_deferred()
    import os
    if os.environ.get("KDBG_LABELS"):
        P.dbg = []
    P.emit()
    if os.environ.get("KDBG_LABELS"):
        import json
        json.dump(P.dbg, open(os.environ["KDBG_LABELS"], "w"))
    return P


_CACHE = {}


def kernel(**inputs):
    nb = 4
    if "P" not in _CACHE:
        _CACHE["P"] = build(nb)
    P = _CACHE["P"]
    shared = prepare_shared(inputs)
    in_maps = []
    for core in range(NCORES):
        m = dict(shared)
        m.update(prepare_core(inputs, core, nb))
        in_maps.append(m)
    res = run_bass_kernel_spmd(P.nc, in_maps, core_ids=list(range(NCORES)))
    outs = []
    for core in range(NCORES):
        o = np.asarray(res.results[core]["outT"])
        outs.append(o.transpose(0, 3, 2, 1).reshape(nb, S, D))
    return np.ascontiguousarray(np.concatenate(outs, axis=0), dtype=np.float32)
```

```python
import numpy as np
import concourse.bass as bass
import concourse.mybir as mybir
from concourse.bass_utils import run_bass_kernel_spmd

F32 = mybir.dt.float32
BF16 = mybir.dt.bfloat16
AF = mybir.ActivationFunctionType
ALU = mybir.AluOpType

NCORES = 8
D = 1024
KC = 8
S = 2048
CTX = 256
NTOK = S + CTX
FF = 2816
FC = 22
NH = 8
HD = 128
ALPHA = float((2.0 * 2) ** 0.25)
LN_EPS = 1e-5
RMS_EPS = 1e-6
GRID_W = 64
NSLOT = 4
SLOTW = 2816
NFP = 11
NBP = 12
FPW = 516


class Tok:
    __slots__ = ("w", "r", "const")

    def __init__(self, const=False):
        self.w = {}
        self.r = {}
        self.const = const


class Prog:
    def __init__(self):
        self.nc = bass.Bass("TRN2", target_bir_lowering=False)
        nc = self.nc
        self.eng = {"pe": nc.tensor, "act": nc.scalar, "dve": nc.vector, "pool": nc.gpsimd, "sp": nc.sync}
        self.ops = []
        self.last_dma = {}
        self.label = "setup"
        self.pe_labels = []
        self.dbg = None

    def op(self, eng, fn, reads=(), writes=(), dma=None, nmm=0):
        deps = {}
        if nmm:
            self.pe_labels.extend([self.label] * nmm)

        def add(d):
            for k, i in d.items():
                if deps.get(k, -1) < i:
                    deps[k] = i

        for t in reads:
            add(t.w)
        for t in writes:
            add(t.w)
            add(t.r)
        i = len(self.ops)
        key = dma if dma is not None else eng
        if dma is not None:
            if key in self.last_dma:
                add({key: self.last_dma[key]})
            self.last_dma[key] = i
        self.ops.append([eng, fn, deps, key, dma is not None, None, False, self.label])
        for t in reads:
            if not t.const:
                if t.r.get(key, -1) < i:
                    t.r[key] = i
        for t in writes:
            t.w = {key: i}
            t.r = {}
        return i

    def emit(self):
        nc = self.nc
        ops = self.ops
        needed = set()
        for o in ops:
            needed.update(o[2].values())
        keys = []
        for o in ops:
            if o[3] not in keys:
                keys.append(o[3])
        import contextlib
        with contextlib.ExitStack() as es:
            sems = {k: es.enter_context(nc.semaphore("s_" + k)) for k in keys}
            cnt = {k: 0 for k in keys}
            for i, o in enumerate(ops):
                if o[4]:
                    cnt[o[3]] += 16
                    o[5] = cnt[o[3]]
                elif i in needed:
                    cnt[o[3]] += 1
                    o[5] = cnt[o[3]]
                    o[6] = True
            seen = {e: {} for e in self.eng}
            nwait = 0
            for i, o in enumerate(ops):
                e = o[0]
                h = self.eng[e]
                for k, d in o[2].items():
                    p = ops[d]
                    if e == "pe" and p[0] == "pe" and not p[4]:
                        continue
                    v = p[5]
                    if seen[e].get(k, 0) < v:
                        h.wait_ge(sems[k], v)
                        seen[e][k] = v
                        nwait += 1
                ins = o[1](h)
                if self.dbg is not None:
                    try:
                        self.dbg.append((e, int(ins.ins.name.split("-")[1]), o[7]))
                    except Exception:
                        pass
                if o[4]:
                    ins.then_inc(sems[o[3]], 16)
                elif o[6]:
                    ins.then_inc(sems[e], 1)
            for k in keys:
                if cnt[k] > 0 and seen["sp"].get(k, 0) < cnt[k]:
                    nc.sync.wait_ge(sems[k], cnt[k])
            self.nwait = nwait
        return nc


def _piece_lhsT(W, ocs):
    K, N = W.shape
    kc, no = K // 128, N // 128
    A = W.reshape(kc, 128, no, 128).transpose(2, 1, 0, 3)
    A = A.reshape(no // ocs, ocs, 128, kc, 128).transpose(0, 2, 1, 3, 4)
    return np.ascontiguousarray(A.reshape(no // ocs, 128, ocs * kc * 128))


def _piece_moving(W, kcs):
    K, N = W.shape
    kc = K // 128
    A = W.reshape(kc, 128, N).transpose(1, 0, 2)
    A = A.reshape(128, kc // kcs, kcs * N).transpose(1, 0, 2)
    return np.ascontiguousarray(A)


def _vec_pc(v):
    v = np.asarray(v)
    lead = v.shape[:-1]
    n = v.shape[-1] // 128
    A = v.reshape(lead + (n, 128))
    A = np.moveaxis(A, -1, 0)
    return np.ascontiguousarray(A)


W_MI = 0
W_MO = 10
W_FI0 = 14
W_Q = 36
W_K = 40
W_V = 41
W_O = 42
W_FI1 = 46
NW1 = 68
NW2 = 16


def _rope_tables():
    f32 = np.float32
    t = np.arange(S)
    row = (t // GRID_W).astype(f32)
    col = (t % GRID_W).astype(f32)
    nf = HD // 4
    inv = (np.float32(10000.0) ** (-(np.arange(nf, dtype=f32) / f32(nf)))).astype(f32)
    ang_r = (row[:, None] * inv[None, :]).astype(f32)
    ang_c = (col[:, None] * inv[None, :]).astype(f32)
    cr, sr = np.cos(ang_r).astype(f32), np.sin(ang_r).astype(f32)
    cc, sc = np.cos(ang_c).astype(f32), np.sin(ang_c).astype(f32)
    COS = np.concatenate([cr, cr, cc, cc], axis=1).T
    SIN = np.concatenate([-sr, sr, -sc, sc], axis=1).T
    perm = np.arange(128)
    perm = np.where((perm // 32) % 2 == 0, perm + 32, perm - 32)
    PM = np.zeros((128, 128), f32)
    PM[perm, np.arange(128)] = 1.0
    return np.ascontiguousarray(COS, dtype=f32), np.ascontiguousarray(SIN, dtype=f32), PM


def prepare_shared(inp):
    f = lambda k: np.asarray(inp[k], dtype=np.float32)
    mix_in, mix_out = f("mix_w_in")[0], f("mix_w_out")[0]
    fin, fout = f("ffn_w_in"), f("ffn_w_out")
    qkv, wo = f("attn_w_qkv")[0], f("attn_w_out")[0]
    w1 = np.empty((NW1, 128, 2048), np.float32)
    w1[W_MI + 0:W_MI + 2] = _piece_lhsT(mix_in[:, 512:1024], 2)
    w1[W_MI + 2:W_MI + 4] = _piece_lhsT(mix_in[:, 1024:1536], 2)
    w1[W_MI + 4:W_MI + 6] = _piece_lhsT(mix_in[:, 0:512], 2)
    w1[W_MI + 6:W_MI + 8] = _piece_lhsT(mix_in[:, 1536:2048], 2)
    w1[W_MI + 8:W_MI + 10] = _piece_moving(mix_in[:, 2048:2560], 4)
    w1[W_MO:W_MO + 4] = _piece_lhsT(mix_out, 2)
    for l, base in ((0, W_FI0), (1, W_FI1)):
        g = _piece_lhsT(fin[l][:, :FF], 1)
        u = _piece_lhsT(fin[l][:, FF:], 1)
        w1[base:base + FC] = np.concatenate([g, u], axis=2)
    w1[W_Q:W_Q + 4] = _piece_lhsT(qkv[:, 0:1024], 2)
    w1[W_K:W_K + 1] = _piece_lhsT(qkv[:, 1024:1280], 2)
    w1[W_V:W_V + 1] = _piece_moving(qkv[:, 1280:1536], 8)
    w1[W_O:W_O + 4] = _piece_lhsT(wo, 2)
    w2 = np.empty((NW2, 128, SLOTW), np.float32)
    w2[0:8] = _piece_lhsT(fout[0], 1)
    w2[8:16] = _piece_lhsT(fout[1], 1)
    ada = f("ada_w")
    A = ada.reshape(2, 2, 4, 128, 48, 128)
    A = A.transpose(0, 4, 1, 3, 2, 5)
    adaw = np.ascontiguousarray(A.reshape(2 * 48 * 2, 128, 512))
    adab = _vec_pc(f("ada_b")).reshape(128, 96)
    lng = _vec_pc(f("ln_g")).reshape(128, 32)
    lnb = _vec_pc(f("ln_b")).reshape(128, 32)
    cw = np.ascontiguousarray(_vec_pc(f("conv_w")[0]).transpose(0, 2, 1)).reshape(128, 12)
    wsT = np.ascontiguousarray(f("sgu_w")[0].transpose(2, 0, 1)).reshape(128, 1024)
    sb = f("sgu_b")[0]
    bsr = np.ascontiguousarray(
        np.repeat(sb.reshape(4, 2, 1, 128), 64, axis=2).reshape(4, 128, 128).transpose(1, 0, 2)).reshape(128, 512)
    gam = _vec_pc(f("sgu_ln_g")[0]).reshape(128, 4)
    betb = np.ascontiguousarray(np.broadcast_to(f("sgu_ln_b")[0][None, :], (128, 512)))
    qg = f("q_norm_g")[0].reshape(128, 1)
    kg = f("k_norm_g")[0].reshape(128, 1)
    COS, SIN, PM = _rope_tables()
    small = np.concatenate([adab, lng, lnb, cw, gam, qg, kg], axis=1)
    return dict(w1=w1, w2=w2, adaw=adaw, small=np.ascontiguousarray(small), wsT=wsT, bsr=bsr, betb=betb,
                cosT=COS, sinT=SIN, pm=PM)


SM_ADAB, SM_LNG, SM_LNB, SM_CW, SM_GAM, SM_QG, SM_KG, SM_W = 0, 96, 128, 160, 172, 176, 177, 178


def prepare_core(inp, core, nb):
    x = np.asarray(inp["x"], dtype=np.float32)[core * nb:(core + 1) * nb]
    ctx = np.asarray(inp["ctx"], dtype=np.float32)[core * nb:(core + 1) * nb]
    c = np.asarray(inp["c"], dtype=np.float32)[core * nb:(core + 1) * nb]
    cctx = np.asarray(inp["c_ctx"], dtype=np.float32)
    xT = np.ascontiguousarray(x.reshape(nb, S, KC, 128).transpose(0, 3, 2, 1))
    ctxT = np.ascontiguousarray(ctx.reshape(nb, CTX, KC, 128).transpose(0, 3, 2, 1))
    cc = np.zeros((5, D), np.float32)
    cc[:nb] = c
    cc[4] = cctx
    ccT = np.ascontiguousarray(cc.reshape(5, KC, 128).transpose(2, 1, 0))
    xe = np.ascontiguousarray(xT[:, :, :, [511, 512, 1023, 1024, 1535, 1536]]).reshape(nb, 128, KC * 6)
    return dict(xT=xT, ctxT=ctxT, cc=ccT.reshape(128, 40), xe=xe)


def build(nb=4, stop_after_l0=False):
    P = Prog()
    nc = P.nc
    op = P.op

    def dram(name, shape, dt, kind):
        return nc.dram_tensor(name, list(shape), dt, kind=kind).ap()

    xT = dram("xT", [nb, 128, KC, S], F32, "ExternalInput")
    ctxT = dram("ctxT", [nb, 128, KC, CTX], F32, "ExternalInput")
    ccD = dram("cc", [128, 40], F32, "ExternalInput")
    xeD = dram("xe", [nb, 128, KC * 6], F32, "ExternalInput")
    w1D = dram("w1", [NW1, 128, 2048], F32, "ExternalInput")
    w2D = dram("w2", [NW2, 128, SLOTW], F32, "ExternalInput")
    adawD = dram("adaw", [192, 128, 512], F32, "ExternalInput")
    smallD = dram("small", [128, SM_W], F32, "ExternalInput")
    wsTD = dram("wsT", [128, 1024], F32, "ExternalInput")
    bsrD = dram("bsr", [128, 512], F32, "ExternalInput")
    betbD = dram("betb", [128, 512], F32, "ExternalInput")
    cosD = dram("cosT", [128, S], F32, "ExternalInput")
    sinD = dram("sinT", [128, S], F32, "ExternalInput")
    pmD = dram("pm", [128, 128], F32, "ExternalInput")
    outT = dram("outT", [nb, 128, KC, S], F32, "ExternalOutput")
    w1B = dram("w1b", [NW1, 128, 2048], BF16, "Internal")
    w2B = dram("w2b", [NW2, 128, SLOTW], BF16, "Internal")

    def sb(name, shape, dt):
        return nc.alloc_sbuf_tensor(name, list(shape), dt).ap()

    XA = sb("XA", [128, KC, NTOK], F32)
    OP8 = [sb(f"OP8_{i}", [128, KC, 512], BF16) for i in range(3)]
    WR = [sb(f"WR{i}", [128, SLOTW], BF16) for i in range(NSLOT)]
    HID = sb("HID", [128, FC, 512], BF16)
    FPB = [sb(f"FP{i}", [128, FPW], F32) for i in range(NFP)]
    BPB = [sb(f"BP{i}", [128, 512], BF16) for i in range(NBP)]
    ZH = sb("ZH", [128, 4, 6], F32)
    GH = sb("GH", [128, 4, 6], F32)
    HE = sb("HE", [128, KC, 6], BF16)
    XEB = sb("XEB", [128, KC, 6], F32)
    KT = sb("KT", [128, 2, NTOK], BF16)
    RG = sb("RG", [128, (NTOK // 128) * 256], BF16)
    VT = RG.rearrange("p (a b) -> p a b", b=256)
    Z = RG[:, 0:4 * 514 * 2].bitcast(F32).rearrange("p (a b) -> p a b", b=514)
    SMALL = sb("SMALL", [128, SM_W], F32)
    CC = sb("CC", [128, KC, 5], F32)
    SIL = sb("SIL", [128, KC, 5], F32)
    MOD = sb("MOD", [128, 96, 5], F32)
    SCH = sb("SCH", [128, 4, KC, 5], F32)
    GA = sb("GA", [128, 32], F32)
    BA = sb("BA", [128, 32], F32)
    QG = sb("QG", [128, 1], F32)
    WST = sb("WST", [128, 8, 128], BF16)
    WSTF = FPB
    BS2 = sb("BS2", [128, 4, 128], F32)
    ONES_LN = sb("ONES_LN", [128, 128], BF16)
    ONES_HD = sb("ONES_HD", [128, 128], BF16)
    ONES_1 = sb("ONES_1", [128, 128], BF16)
    PMS = sb("PMS", [128, 128], F32)
    PMSB = sb("PMSB", [128, 128], BF16)
    STAT = [sb(f"STAT{i}", [128, 16], F32) for i in range(6)]
    PS = [nc.alloc_psum_tensor(f"PS{i}", [128, 512], F32).ap() for i in range(8)]

    tXA = [[Tok() for _ in range(KC)] for _ in range(5)]
    tOP8 = [[Tok() for _ in range(KC)] for _ in range(3)]
    tWR = [Tok() for _ in range(NSLOT)]
    tHID = [Tok() for _ in range(FC)]
    tFP = [Tok() for _ in range(NFP)]
    tBP = [Tok() for _ in range(NBP)]
    tZ = [Tok() for _ in range(4)]
    tZH, tGH, tHE, tXEB = Tok(), Tok(), Tok(), Tok()
    tKT = [[Tok() for _ in range(5)] for _ in range(2)]
    tVT = [Tok() for _ in range(NTOK // 128)]
    tSTAT = [Tok() for _ in range(6)]
    tPS = [Tok() for _ in range(8)]
    tCONST = Tok(const=True)
    tW1 = [Tok(const=True) for _ in range(NW1)]
    tW2 = [Tok(const=True) for _ in range(NW2)]

    st = dict(wr=0, fp=0, bp=0, ps=0, stat=0, op8=0, ndma=0, ncast=0)

    held = {"fp": set(), "bp": set()}

    def _pool(kind, n, hold):
        i = st[kind]
        k = 0
        while i in held[kind]:
            i = (i + 1) % n
            k += 1
            assert k <= n, "pool exhausted " + kind
        st[kind] = (i + 1) % n
        if hold:
            held[kind].add(i)
        return i

    def fpool(hold=False):
        i = _pool("fp", NFP, hold)
        return FPB[i], tFP[i]

    def bpool(hold=False):
        i = _pool("bp", NBP, hold)
        return BPB[i], tBP[i]

    def frelease(buf):
        held["fp"].discard([k for k in range(NFP) if FPB[k] is buf][0])

    def brelease(buf):
        held["bp"].discard([k for k in range(NBP) if BPB[k] is buf][0])

    def spool():
        i = st["stat"]
        st["stat"] = (i + 1) % 6
        return STAT[i], tSTAT[i]

    ps_set = {"banks": [0, 1, 2, 3, 4, 5]}
    ps_held = set()

    def psum(hold=False):
        banks = ps_set["banks"]
        k = 0
        while True:
            i = banks[st["ps"] % len(banks)]
            st["ps"] += 1
            if i not in ps_held:
                break
            k += 1
            assert k <= len(banks), "psum exhausted"
        if hold:
            ps_held.add(i)
        return PS[i], tPS[i]

    def prelease(ps_):
        ps_held.discard([k for k in range(8) if PS[k] is ps_][0])

    op8_held = set()

    def op8():
        i = st["op8"]
        k = 0
        while i in op8_held:
            i = (i + 1) % 3
            k += 1
            assert k <= 3, "op8 exhausted"
        st["op8"] = (i + 1) % 3
        op8_held.add(i)
        return OP8[i], tOP8[i]

    def orelease(buf):
        op8_held.discard([k for k in range(3) if OP8[k] is buf][0])

    def dma(eng, out, in_, reads, writes, stream):
        if stream == "misc":
            k = st["ndma"]
            st["ndma"] = k + 1
            stream = stream + str(k % 4)
        elif stream == "cast":
            k = st["ncast"]
            st["ncast"] = k + 1
            stream = stream + str(k % 12)
        return op(eng, lambda h: h.dma_start(out=out, in_=in_), reads, writes, dma=stream)

    def wload(which, idx):
        s = st["wr"]
        st["wr"] = (s + 1) % NSLOT
        if which == 1:
            src, tk, n = w1B[idx], tW1[idx], 2048
        else:
            src, tk, n = w2B[idx], tW2[idx], SLOTW
        dma("sp", WR[s][:, 0:n], src, [tk], [tWR[s]], f"w{s}")
        return WR[s], tWR[s]

    def mm(out, pairs, reads, wtok):
        def fn(h):
            n = len(pairs)
            ins = None
            for i, (l, r) in enumerate(pairs):
                ins = h.matmul(out, lhsT=l, rhs=r, start=(i == 0), stop=(i == n - 1))
            return ins
        return op("pe", fn, reads, [wtok], nmm=len(pairs))

    def mm1(out, l, r, start, stop, reads, wtok):
        return op("pe", lambda h: h.matmul(out, lhsT=l, rhs=r, start=start, stop=stop), reads, [wtok], nmm=1)

    def tt(e, out, a, b, o, reads, writes):
        return op(e, lambda h: h.tensor_tensor(out=out, in0=a, in1=b, op=o), reads, writes)

    def ts(e, out, a, s1, s2, o1, o2, reads, writes):
        if s2 is None:
            s2 = 1.0 if o1 == ALU.add else 0.0
            o2 = ALU.mult if o1 == ALU.add else ALU.add
        return op(e, lambda h: h.tensor_scalar(out=out, in0=a, scalar1=s1, scalar2=s2, op0=o1, op1=o2), reads, writes)

    def stt(out, a, s, b, o1, o2, reads, writes):
        return op("dve", lambda h: h.scalar_tensor_tensor(out=out, in0=a, scalar=s, in1=b, op0=o1, op1=o2),
                  reads, writes)

    def act(out, in_, func, reads, writes, bias=None, scale=None):
        kw = {}
        if bias is not None:
            kw["bias"] = bias
        if scale is not None:
            kw["scale"] = scale
        return op("act", lambda h: h.activation(out=out, in_=in_, func=func, **kw), reads, writes)

    order1 = list(range(W_MI, W_MI + 10)) + list(range(W_MO, W_MO + 4)) + list(range(W_FI0, W_FI0 + FC))
    order1b = list(range(W_Q, NW1))

    def cast1(lo, hi):
        dma("pool", w1B[lo:hi], w1D[lo:hi], [], [tW1[i] for i in range(lo, hi)], "cast")

    def cast2(lo, hi):
        dma("pool", w2B[lo:hi], w2D[lo:hi], [], [tW2[i] for i in range(lo, hi)], "cast")

    dma("sp", SMALL, smallD, [], [tCONST], "misc")
    dma("sp", CC.rearrange("p a b -> p (a b)"), ccD, [], [tCONST], "misc")
    dma("sp", PMS, pmD, [], [tCONST], "misc")
    dma("sp", BS2.rearrange("p a b -> p (a b)"), bsrD, [], [tCONST], "misc")
    cast1(8, 12)
    cast1(0, 4)
    cast1(4, 8)
    for lo in range(12, 36, 4):
        cast1(lo, lo + 4)
    cast2(0, 4)
    cast2(4, 8)

    def late_casts():
        for lo in range(36, NW1, 4):
            cast1(lo, lo + 4)
        cast2(8, 12)
        cast2(12, 16)

    op("dve", lambda h: h.memset(ONES_LN, 1.0 / D), [], [tCONST])
    op("dve", lambda h: h.memset(ONES_HD, 1.0 / HD), [], [tCONST])
    op("dve", lambda h: h.memset(ONES_1, 1.0), [], [tCONST])
    act(SIL, CC, AF.Silu, [tCONST], [tCONST])
    op("dve", lambda h: h.tensor_copy(out=PMSB, in_=PMS), [tCONST], [tCONST])
    ts("dve", GA[:, 0:24], SMALL[:, SM_LNG:SM_LNG + 24], ALPHA, None, ALU.mult, None, [tCONST], [tCONST])
    ts("dve", GA[:, 24:32], SMALL[:, SM_LNG + 24:SM_LNG + 32], 1.0, None, ALU.mult, None, [tCONST], [tCONST])
    ts("dve", BA[:, 0:24], SMALL[:, SM_LNB:SM_LNB + 24], ALPHA, None, ALU.mult, None, [tCONST], [tCONST])
    ts("dve", BA[:, 24:32], SMALL[:, SM_LNB + 24:SM_LNB + 32], 1.0, None, ALU.mult, None, [tCONST], [tCONST])
    ts("dve", QG, SMALL[:, SM_QG:SM_QG + 1], float(HD ** -0.5), None, ALU.mult, None, [tCONST], [tCONST])

    wf = []
    for hlf in range(2):
        b_, t_ = fpool(hold=True)
        dma("sp", b_[:, 0:512], wsTD[:, hlf * 512:(hlf + 1) * 512], [], [t_], "misc")
        wf.append((b_, t_))
        op("dve", (lambda b_=b_, hlf=hlf: lambda h: h.tensor_copy(
            out=WST[:, hlf * 4:(hlf + 1) * 4, :].rearrange("p a b -> p (a b)"), in_=b_[:, 0:512]))(),
           [t_], [tCONST])
    bb, tbb = fpool(hold=True)
    dma("sp", bb[:, 0:512], betbD, [], [tbb], "misc")
    for jc in range(4):
        ps_, tp_ = psum()
        for hh in range(2):
            g = 2 * jc + hh
            wsrc, twsrc = wf[g // 4]
            mm1(ps_[64 * hh:64 * hh + 64, 0:128], bb[:, jc * 128 + 64 * hh:jc * 128 + 64 * hh + 64],
                wsrc[:, (g % 4) * 128:(g % 4) * 128 + 128], True, True, [tbb, twsrc], tp_)
        tt("dve", BS2[:, jc, :], ps_[:, 0:128], BS2[:, jc, :], ALU.add, [tp_, tCONST], [tCONST])
    frelease(bb)
    frelease(wf[0][0])
    frelease(wf[1][0])

    SCHX = sb("SCHX", [128, KC, 5], F32)

    def ada_gen(l):
        for oc in range(48):
            P.label = "setup"
            bufs = []
            for hlf in range(2):
                b_, t_ = fpool()
                dma("sp", b_[:, 0:512], adawD[(l * 48 + oc) * 2 + hlf], [], [t_], "misc")
                bufs.append((b_, t_))
            ps_, tp_ = psum()
            pairs = []
            for kc in range(KC):
                b_, t_ = bufs[kc // 4]
                pairs.append((b_[:, (kc % 4) * 128:(kc % 4) * 128 + 128], SIL[:, kc, :]))
            mm(ps_[:, 0:5], pairs, [bufs[0][1], bufs[1][1], tCONST], tp_)
            ts("dve", MOD[:, l * 48 + oc, :], ps_[:, 0:5],
               SMALL[:, SM_ADAB + l * 48 + oc:SM_ADAB + l * 48 + oc + 1], None, ALU.add, None, [tp_, tCONST],
               [tCONST])
            yield
        P.label = "setup"
        for s_ in range(2):
            base = l * 48 + (3 * s_ + 1) * 8
            ts("dve", SCH[:, l * 2 + s_].rearrange("p a b -> p (a b)"),
               MOD[:, base:base + 8, :].rearrange("p a b -> p (a b)"), 1.0, 1.0 / ALPHA, ALU.add, ALU.mult,
               [tCONST], [tCONST])
        if l == 0:
            ts("dve", SCHX.rearrange("p a b -> p (a b)"), MOD[:, 8:16, :].rearrange("p a b -> p (a b)"),
               1.0, 1.0, ALU.add, ALU.mult, [tCONST], [tCONST])

    for _ in ada_gen(0):
        pass

    def shift_ap(l, s_, c, j):
        return MOD[:, l * 48 + (3 * s_) * 8 + c, j:j + 1]

    def gate_ap(l, s_, c, j):
        return MOD[:, l * 48 + (3 * s_ + 2) * 8 + c, j:j + 1]

    def sch_ap(l, s_, c, j):
        return SCH[:, l * 2 + s_, c, j:j + 1]

    TILES = [(0, 0, 512, False), (1, 512, 512, False), (2, 1024, 512, False), (3, 1536, 512, False),
             (4, 2048, 256, True)]
    deferred = []

    def flush_deferred():
        for f_ in deferred:
            f_()
        deferred.clear()

    def make_h(l, ti, t0, T, j):
        buf, tk = op8()
        for c in range(KC):
            e = "dve" if c % 2 == 0 else "pool"
            ts(e, buf[:, c, 0:T], XA[:, c, t0:t0 + T], sch_ap(l, 0, c, j), shift_ap(l, 0, c, j), ALU.mult, ALU.add,
               [tXA[ti][c], tCONST], [tk[c]])
        return buf, tk

    def run_par(*gens):
        gens = [g for g in gens if g is not None]
        while gens:
            for g in list(gens):
                try:
                    next(g)
                except StopIteration:
                    gens.remove(g)

    def layer_norm(l, s_, ti, t0, T, j, res):
        lab = f"ln{l}{s_}"
        P.label = lab
        mean_ps, tm = PS[6], tPS[6]
        ex2_ps, te = PS[7], tPS[7]
        pend = []

        def stats_mm():
            for (c, rb, trb, sq, tsq) in pend:
                mm1(mean_ps[:, 0:T], ONES_LN, rb[:, 0:T], c == 0, c == KC - 1, [trb, tCONST], tm)
                mm1(ex2_ps[:, 0:T], ONES_LN, sq[:, 0:T], c == 0, c == KC - 1, [tsq, tCONST], te)
                brelease(rb)
                brelease(sq)
            pend.clear()
        for c in range(KC):
            P.label = lab
            if c % 2 == 0:
                stats_mm()
            rb, trb = bpool(hold=True)
            op("pool", (lambda rb=rb, c=c: lambda h: h.tensor_copy(out=rb[:, 0:T], in_=XA[:, c, t0:t0 + T]))(),
               [tXA[ti][c]], [trb])
            sq, tsq = bpool(hold=True)
            e = "pool" if c % 4 != 3 else "dve"
            tt(e, sq[:, 0:T], XA[:, c, t0:t0 + T], XA[:, c, t0:t0 + T], ALU.mult, [tXA[ti][c]], [tsq])
            pend.append((c, rb, trb, sq, tsq))
            if c % 2 == 1:
                yield
        P.label = lab
        stats_mm()
        P.label = lab
        msb, tmsb = fpool(hold=True)
        op("dve", lambda h: h.tensor_copy(out=msb[:, 0:T], in_=mean_ps[:, 0:T]), [tm], [tmsb])
        var, tvar = fpool(hold=True)
        tt("pool", var[:, 0:T], msb[:, 0:T], msb[:, 0:T], ALU.mult, [tmsb], [tvar])
        tt("dve", var[:, 0:T], ex2_ps[:, 0:T], var[:, 0:T], ALU.subtract, [te, tvar], [tvar])
        A_, tA = fpool(hold=True)
        act(A_[:, 0:T], var[:, 0:T], AF.Ln, [tvar], [tA], bias=LN_EPS)
        act(A_[:, 0:T], A_[:, 0:T], AF.Exp, [tA], [tA], scale=-0.5)
        frelease(var)
        B_, tB = fpool(hold=True)
        stt(B_[:, 0:T], msb[:, 0:T], -1.0, A_[:, 0:T], ALU.mult, ALU.mult, [tmsb, tA], [tB])
        hf = None
        if s_ == 0:
            hf = op8()
            res["hf"] = hf
        yield
        gi = (l * 2 + s_) * 8
        for c in range(KC):
            P.label = lab
            xa = XA[:, c, t0:t0 + T]
            tx = tXA[ti][c]
            on_dve = c in (1, 3, 5)
            if on_dve:
                stt(xa, xa, GA[:, gi + c:gi + c + 1], A_[:, 0:T], ALU.mult, ALU.mult, [tx, tA, tCONST], [tx])
                stt(xa, B_[:, 0:T], GA[:, gi + c:gi + c + 1], xa, ALU.mult, ALU.add, [tx, tB, tCONST], [tx])
                ts("dve", xa, xa, BA[:, gi + c:gi + c + 1], None, ALU.add, None, [tx, tCONST], [tx])
            else:
                tt("pool", xa, xa, A_[:, 0:T], ALU.mult, [tx, tA], [tx])
                tt("pool", xa, xa, B_[:, 0:T], ALU.add, [tx, tB], [tx])
                ts("pool", xa, xa, GA[:, gi + c:gi + c + 1], BA[:, gi + c:gi + c + 1], ALU.mult, ALU.add,
                   [tx, tCONST], [tx])
            if s_ == 0:
                e = "dve" if on_dve else "pool"
                ts(e, hf[0][:, c, 0:T], xa, sch_ap(l, 1, c, j), shift_ap(l, 1, c, j), ALU.mult, ALU.add,
                   [tx, tCONST], [hf[1][c]])
            if c % 2 == 1:
                yield
        frelease(msb)
        frelease(A_)
        frelease(B_)

    def ffn(l, ti, t0, T, j, res):
        hb, thb = res["hf"]
        base = W_FI0 if l == 0 else W_FI1
        for jj in range(FC):
            P.label = f"ffn_in{l}"
            w, tw = wload(1, base + jj)
            pg, tpg = psum()
            mm(pg[:, 0:T], [(w[:, kc * 128:(kc + 1) * 128], hb[:, kc, 0:T]) for kc in range(KC)], [tw] + thb, tpg)
            pu, tpu = psum()
            mm(pu[:, 0:T], [(w[:, (8 + kc) * 128:(9 + kc) * 128], hb[:, kc, 0:T]) for kc in range(KC)],
               [tw] + thb, tpu)
            sg, tsg = fpool()
            act(sg[:, 0:T], pg[:, 0:T], AF.Silu, [tpg], [tsg])
            tt("dve", HID[:, jj, 0:T], pu[:, 0:T], sg[:, 0:T], ALU.mult, [tpu, tsg], [tHID[jj]])
            if jj == 10:
                flush_deferred()
            yield
        orelease(hb)
        for oc in range(KC):
            P.label = f"ffn_out{l}"
            w, tw = wload(2, l * 8 + oc)
            po, tpo = psum()
            mm(po[:, 0:T], [(w[:, kc * 128:(kc + 1) * 128], HID[:, kc, 0:T]) for kc in range(FC)], [tw] + tHID, tpo)
            xa = XA[:, oc, t0:t0 + T]
            stt(xa, po[:, 0:T], gate_ap(l, 1, oc, j), xa, ALU.mult, ALU.add, [tpo, tXA[ti][oc], tCONST],
                [tXA[ti][oc]])
            yield

    def l0_mixer(b, ti, t0, T, is_ctx, hm):
        j = 4 if is_ctx else b
        hb, thb = hm
        ntc = T // 128
        do_halo = (ti == 0)
        P.label = "l0.v"
        wv0, twv0 = wload(1, W_MI + 8)
        wv1, twv1 = wload(1, W_MI + 9)
        VN = []
        VG = []
        sst, tst = spool()
        for tc in range(ntc):
            P.label = "l0.v"
            pv, tpv = psum()
            pairs = []
            for kc in range(KC):
                wv = wv0 if kc < 4 else wv1
                pairs.append((hb[:, kc, tc * 128:(tc + 1) * 128], wv[:, (kc % 4) * 512:(kc % 4 + 1) * 512]))
            mm(pv, pairs, [twv0, twv1] + thb, tpv)
            vg, tvg = fpool(hold=True)
            act(vg[:, 0:512], pv, AF.Gelu, [tpv], [tvg])
            s6, ts6 = spool()
            op("dve", (lambda s6=s6, vg=vg: lambda h: h.bn_stats(out=s6[:, 0:6], in_=vg[:, 0:512]))(),
               [tvg], [ts6])
            op("dve", (lambda s6=s6, tc=tc: lambda h: h.bn_aggr(out=sst[:, 2 * tc:2 * tc + 2], in_=s6[:, 0:6]))(),
               [ts6, tst], [tst])
            VG.append((vg, tvg))
            yield
        P.label = "l0.v"
        varv = sst[:, 0:2 * ntc].rearrange("p (a b) -> p a b", b=2)[:, :, 1:2]
        rsv = sst[:, 8:8 + ntc].rearrange("p (a b) -> p a b", b=1)
        act(rsv, varv, AF.Ln, [tst], [tst], bias=LN_EPS)
        act(rsv, rsv, AF.Exp, [tst], [tst], scale=-0.5)
        for tc in range(ntc):
            vg, tvg = VG[tc]
            vn, tvn = bpool(hold=True)
            ts("dve", vn, vg[:, 0:512], sst[:, 2 * tc:2 * tc + 1], sst[:, 8 + tc:9 + tc], ALU.subtract, ALU.mult,
               [tvg, tst], [tvn])
            frelease(vg)
            VN.append((vn, tvn))
        yield
        for h2 in range(2):
            P.label = "l0.mixA"
            wg, twg = wload(1, W_MI + 0 + h2)
            wh, twh = wload(1, W_MI + 2 + h2)
            for o2 in range(2):
                P.label = "l0.mixA"
                oc = h2 * 2 + o2
                pg, tpg = psum()
                mm(pg[:, 0:T], [(wg[:, (o2 * 8 + kc) * 128:(o2 * 8 + kc + 1) * 128], hb[:, kc, 0:T])
                                for kc in range(KC)], [twg] + thb, tpg)
                gt, tgt = fpool()
                act(gt[:, 0:T], pg[:, 0:T], AF.Copy, [tpg], [tgt])
                ph, tph = psum()
                mm(ph[:, 0:T], [(wh[:, (o2 * 8 + kc) * 128:(o2 * 8 + kc + 1) * 128], hb[:, kc, 0:T])
                                for kc in range(KC)], [twh] + thb, tph)
                tt("dve", Z[:, oc, 1:T + 1], ph[:, 0:T], gt[:, 0:T], ALU.mult, [tph, tgt], [tZ[oc]] + tVT)
                if do_halo:
                    pg2, tpg2 = psum()
                    mm(pg2[:, 0:6], [(wg[:, (o2 * 8 + kc) * 128:(o2 * 8 + kc + 1) * 128], HE[:, kc, :])
                                     for kc in range(KC)], [twg, tHE], tpg2)
                    act(GH[:, oc, :], pg2[:, 0:6], AF.Copy, [tpg2], [tGH])
                    ph2, tph2 = psum()
                    mm(ph2[:, 0:6], [(wh[:, (o2 * 8 + kc) * 128:(o2 * 8 + kc + 1) * 128], HE[:, kc, :])
                                     for kc in range(KC)], [twh, tHE], tph2)
                    tt("dve", ZH[:, oc, :], ph2[:, 0:6], GH[:, oc, :], ALU.mult, [tph2, tGH], [tZH])
                yield
        P.label = "l0.mixA"
        if is_ctx or ti == 0:
            op("pool", lambda h: h.memset(Z[:, :, 0:1], 0.0), [], tZ + tVT)
        else:
            op("pool", lambda h: h.tensor_copy(out=Z[:, :, 0:1], in_=ZH[:, :, 2 * (ti - 1):2 * (ti - 1) + 1]),
               [tZH], tZ + tVT)
        if is_ctx or ti == 3:
            op("pool", lambda h: h.memset(Z[:, :, T + 1:T + 2], 0.0), [], tZ + tVT)
        else:
            op("pool", lambda h: h.tensor_copy(out=Z[:, :, T + 1:T + 2], in_=ZH[:, :, 2 * ti + 1:2 * ti + 2]),
               [tZH], tZ + tVT)
        yab, tyab = op8()
        ZC = []
        for oc in range(4):
            zc, tzc = fpool(hold=True)
            cwa = lambda tap, oc=oc: SMALL[:, SM_CW + oc * 3 + tap:SM_CW + oc * 3 + tap + 1]
            ts("pool", zc[:, 0:T], Z[:, oc, 1:T + 1], cwa(1), None, ALU.mult, None, [tZ[oc], tCONST], [tzc])
            stt(zc[:, 0:T], Z[:, oc, 0:T], cwa(0), zc[:, 0:T], ALU.mult, ALU.add, [tZ[oc], tzc, tCONST], [tzc])
            stt(zc[:, 0:T], Z[:, oc, 2:T + 2], cwa(2), zc[:, 0:T], ALU.mult, ALU.add, [tZ[oc], tzc, tCONST], [tzc])
            ZC.append((zc, tzc))
        yield
        for h2 in range(2):
            P.label = "l0.gb"
            wb_, twb = wload(1, W_MI + 4 + h2)
            for o2 in range(2):
                P.label = "l0.gb"
                oc = h2 * 2 + o2
                pb, tpb = psum()
                mm(pb[:, 0:T], [(wb_[:, (o2 * 8 + kc) * 128:(o2 * 8 + kc + 1) * 128], hb[:, kc, 0:T])
                                for kc in range(KC)], [twb] + thb, tpb)
                zc, tzc = ZC[oc]
                tt("dve", yab[:, oc, 0:T], pb[:, 0:T], zc[:, 0:T], ALU.mult, [tpb, tzc], [tyab[oc]])
                frelease(zc)
                yield
        for h2 in range(2):
            P.label = "l0.u"
            wu, twu = wload(1, W_MI + 6 + h2)
            for o2 in range(2):
                P.label = "l0.u"
                jc = h2 * 2 + o2
                pu, tpu = psum()
                mm(pu[:, 0:T], [(wu[:, (o2 * 8 + kc) * 128:(o2 * 8 + kc + 1) * 128], hb[:, kc, 0:T])
                                for kc in range(KC)], [twu] + thb, tpu)
                ub, tub = fpool()
                act(ub[:, 0:T], pu[:, 0:T], AF.Gelu, [tpu], [tub])
                P.label = "l0.sgu"
                ps_, tp_ = psum()

                def fn(h, ps_=ps_, jc=jc):
                    ins = None
                    for tc in range(ntc):
                        for hh in range(2):
                            g = 2 * jc + hh
                            ins = h.matmul(ps_[64 * hh:64 * hh + 64, tc * 128:(tc + 1) * 128],
                                           lhsT=VN[tc][0][:, g * 64:(g + 1) * 64], rhs=WST[:, g, :],
                                           start=True, stop=True)
                    return ins
                op("pe", fn, [v[1] for v in VN] + [tCONST], [tp_], nmm=2 * ntc)
                tmp, ttmp = fpool()
                stt(tmp[:, 0:T].rearrange("p (a b) -> p a b", b=128),
                    ps_[:, 0:T].rearrange("p (a b) -> p a b", b=128),
                    SMALL[:, SM_GAM + jc:SM_GAM + jc + 1], BS2[:, jc:jc + 1, :].broadcast_to([128, ntc, 128]),
                    ALU.mult, ALU.add, [tp_, tCONST], [ttmp])
                tt("dve", yab[:, 4 + jc, 0:T], tmp[:, 0:T], ub[:, 0:T], ALU.mult, [ttmp, tub], [tyab[4 + jc]])
                yield
        for v_ in VN:
            brelease(v_[0])
        for pz in range(4):
            P.label = "l0.mixout"
            wo_, two = wload(1, W_MO + pz)
            for o2 in range(2):
                P.label = "l0.mixout"
                oc = pz * 2 + o2
                po, tpo = psum()
                mm(po[:, 0:T], [(wo_[:, (o2 * 8 + kc) * 128:(o2 * 8 + kc + 1) * 128], yab[:, kc, 0:T])
                                for kc in range(KC)], [two] + tyab, tpo)
                xa = XA[:, oc, t0:t0 + T]
                stt(xa, po[:, 0:T], gate_ap(0, 0, oc, j), xa, ALU.mult, ALU.add, [tpo, tXA[ti][oc], tCONST],
                    [tXA[ti][oc]])
                yield
        orelease(hb)
        orelease(yab)

    def qk_norm_rope(ps_, tp_, T, gvec, rope, dst, tdst, lab):
        P.label = lab
        sq, tsq = bpool(hold=True)
        act(sq[:, 0:T], ps_[:, 0:T], AF.Square, [tp_], [tsq])
        yield
        P.label = lab
        pm_, tpm = psum()
        mm1(pm_[:, 0:T], ONES_HD, sq[:, 0:T], True, True, [tsq, tCONST], tpm)
        brelease(sq)
        A_, tA = fpool()
        act(A_[:, 0:T], pm_[:, 0:T], AF.Ln, [tpm], [tA], bias=RMS_EPS)
        act(A_[:, 0:T], A_[:, 0:T], AF.Exp, [tA], [tA], scale=-0.5)
        if rope is None:
            stt(dst, ps_[:, 0:T], gvec, A_[:, 0:T], ALU.mult, ALU.mult, [tp_, tA, tCONST], tdst)
            prelease(ps_)
            return
        kn, tkn = bpool(hold=True)
        stt(kn[:, 0:T], ps_[:, 0:T], gvec, A_[:, 0:T], ALU.mult, ALU.mult, [tp_, tA, tCONST], [tkn])
        prelease(ps_)
        yield
        P.label = lab
        (cs, tcs), (sn, tsn) = rope
        pp, tpp = psum()
        mm1(pp[:, 0:T], PMSB, kn[:, 0:T], True, True, [tkn, tCONST], tpp)
        t1, tt1 = fpool()
        tt("pool", t1[:, 0:T], kn[:, 0:T], cs[:, 0:T], ALU.mult, [tkn, tcs], [tt1])
        t2, tt2 = fpool()
        tt("dve", t2[:, 0:T], pp[:, 0:T], sn[:, 0:T], ALU.mult, [tpp, tsn], [tt2])
        tt("dve", dst, t1[:, 0:T], t2[:, 0:T], ALU.add, [tt1, tt2], tdst)
        brelease(kn)

    def load_rope(t0, T):
        cs, tcs = fpool(hold=True)
        dma("sp", cs[:, 0:T], cosD[:, t0:t0 + T], [], [tcs], "misc")
        sn, tsn = fpool(hold=True)
        dma("sp", sn[:, 0:T], sinD[:, t0:t0 + T], [], [tsn], "misc")
        return (cs, tcs), (sn, tsn)

    def l1_kv_tile(b, ti, t0, T, is_ctx):
        P.label = "l1.kv"
        j = 4 if is_ctx else b
        hb, thb = make_h(1, ti, t0, T, j)
        rope = None if is_ctx else load_rope(t0, T)
        wk, twk = wload(1, W_K)
        wv, twv = wload(1, W_V)
        chains = []
        for kvh in range(2):
            P.label = "l1.kv"
            pk, tpk = psum(hold=True)
            mm(pk[:, 0:T], [(wk[:, (kvh * 8 + kc) * 128:(kvh * 8 + kc + 1) * 128], hb[:, kc, 0:T])
                            for kc in range(KC)], [twk] + thb, tpk)
            chains.append(qk_norm_rope(pk, tpk, T, SMALL[:, SM_KG:SM_KG + 1], rope, KT[:, kvh, t0:t0 + T],
                                       [tKT[kvh][ti]], "l1.kv"))
        vsteps = []
        for tc in range(T // 128):
            def vstep(tc=tc):
                P.label = "l1.kv"
                pv, tpv = psum()
                mm(pv[:, 0:256], [(hb[:, kc, tc * 128:(tc + 1) * 128], wv[:, kc * 256:(kc + 1) * 256])
                                  for kc in range(KC)], [twv] + thb, tpv)
                g = t0 // 128 + tc
                act(VT[:, g, :], pv[:, 0:256], AF.Copy, [tpv], [tVT[g]] + tZ)
            vsteps.append(vstep)

        def vgen():
            for f_ in vsteps:
                f_()
                yield
        for _ in kv_driver(chains, vgen()):
            yield
        orelease(hb)
        if rope is not None:
            frelease(rope[0][0])
            frelease(rope[1][0])

    def kv_driver(chains, vg):
        gens = list(chains) + [vg]
        while gens:
            for g in list(gens):
                try:
                    next(g)
                except StopIteration:
                    gens.remove(g)
            yield

    def l1_att(b, ti, t0, T, hm):
        j = b
        hb, thb = hm
        rope = load_rope(t0, T)
        at, tat = op8()
        NKC = NTOK // 128

        def qproj(pz):
            P.label = "l1.qproj"
            wq, twq = wload(1, W_Q + pz)
            for o2 in range(2):
                P.label = "l1.qproj"
                pq, tpq = psum(hold=True)
                mm(pq[:, 0:T], [(wq[:, (o2 * 8 + kc) * 128:(o2 * 8 + kc + 1) * 128], hb[:, kc, 0:T])
                                for kc in range(KC)], [twq] + thb, tpq)
                qt, tqt = bpool(hold=True)
                qres[pz].append((qt, tqt))
                for _ in qk_norm_rope(pq, tpq, T, QG[:, 0:1], rope, qt[:, 0:T], [tqt], "l1.qproj"):
                    yield
                yield

        def attend(hd_, qt, tqt):
            P.label = "l1.attn"
            kv = hd_ // 4
            po, tpo = PS[4], tPS[4]
            pd, tpd = PS[5], tPS[5]

            def smm(kc):
                ps_, tp_ = psum(hold=True)
                ktile = min(kc // 4, 4)
                mm1(ps_[:, 0:T], KT[:, kv, kc * 128:(kc + 1) * 128], qt[:, 0:T], True, True,
                    [tKT[kv][ktile], tqt], tp_)
                return ps_, tp_
            nxt = smm(0)
            pts = []

            def od(kc):
                pt, tpt = pts[kc]
                mm1(po[:, 0:T], VT[:, kc, kv * 128:(kv + 1) * 128], pt[:, 0:T], kc == 0, kc == NKC - 1,
                    [tVT[kc], tpt], tpo)
                if kc % 2 == 1:
                    ppt, tppt = pts[kc - 1]
                    p2, tp2 = bpool()
                    tt("dve", p2[:, 0:T], ppt[:, 0:T], pt[:, 0:T], ALU.add, [tppt, tpt], [tp2])
                    mm1(pd[:, 0:T], ONES_1, p2[:, 0:T], kc == 1, kc == NKC - 1, [tp2, tCONST], tpd)
                    brelease(ppt)
                    brelease(pt)
            for kc in range(NKC):
                P.label = "l1.attn"
                cur = nxt
                if kc + 1 < NKC:
                    nxt = smm(kc + 1)
                pt, tpt = bpool(hold=True)
                act(pt[:, 0:T], cur[0][:, 0:T], AF.Exp, [cur[1]], [tpt])
                prelease(cur[0])
                pts.append((pt, tpt))
                if kc >= 1:
                    od(kc - 1)
                if kc % 2 == 1:
                    yield
            P.label = "l1.attn"
            od(NKC - 1)
            P.label = "l1.attn"
            rd, trd = fpool()
            osb, tosb = fpool()
            act(rd[:, 0:T], pd[:, 0:T], AF.Ln, [tpd], [trd])
            op("dve", lambda h: h.tensor_copy(out=osb[:, 0:T], in_=po[:, 0:T]), [tpo], [tosb])
            act(rd[:, 0:T], rd[:, 0:T], AF.Exp, [trd], [trd], scale=-1.0)
            tt("pool", at[:, hd_, 0:T], osb[:, 0:T], rd[:, 0:T], ALU.mult, [tosb, trd], [tat[hd_]])
            brelease(qt)

        qres = [[] for _ in range(4)]
        for _ in qproj(0):
            yield
        for pz in range(4):
            nq = qproj(pz + 1) if pz + 1 < 4 else None
            for o2 in range(2):
                qt, tqt = qres[pz][o2]
                stepi = 0
                for _ in attend(pz * 2 + o2, qt, tqt):
                    stepi += 1
                    if nq is not None and stepi % 2 == 0:
                        try:
                            next(nq)
                        except StopIteration:
                            nq = None
                    yield
            if nq is not None:
                for _ in nq:
                    yield
        orelease(hb)
        frelease(rope[0][0])
        frelease(rope[1][0])
        for pz in range(4):
            P.label = "l1.out"
            wo_, two = wload(1, W_O + pz)
            for o2 in range(2):
                P.label = "l1.out"
                oc = pz * 2 + o2
                po, tpo = psum()
                mm(po[:, 0:T], [(wo_[:, (o2 * 8 + kc) * 128:(o2 * 8 + kc + 1) * 128], at[:, kc, 0:T])
                                for kc in range(KC)], [two] + tat, tpo)
                xa = XA[:, oc, t0:t0 + T]
                stt(xa, po[:, 0:T], gate_ap(1, 0, oc, j), xa, ALU.mult, ALU.add, [tpo, tXA[ti][oc], tCONST],
                    [tXA[ti][oc]])
                yield
        orelease(at)

    def load_tile(b, ti, t0, T, is_ctx):
        src = ctxT[b] if is_ctx else xT[b][:, :, t0:t0 + T]
        dma("sp", XA[:, :, t0:t0 + T], src, [], tXA[ti], f"xin{ti}")
        for c in range(KC):
            e = "pool" if c % 2 == 0 else "dve"
            ts(e, XA[:, c, t0:t0 + T], XA[:, c, t0:t0 + T], ALPHA, None, ALU.mult, None, [tXA[ti][c]], [tXA[ti][c]])

    def store_tile(b, ti, t0, T):
        dma("sp", outT[b][:, :, t0:t0 + T], XA[:, :, t0:t0 + T], tXA[ti], [], f"out{ti}")

    def chain(*gens):
        for g in gens:
            if g is not None:
                for _ in g:
                    yield

    def pipeline(n, front, ln1, back, ln2, extra=None):
        res = [dict() for _ in range(n)]
        run_par(front(0, res[0]), extra)
        if n > 1:
            run_par(ln1(0, res[0]), front(1, res[1]))
        else:
            run_par(ln1(0, res[0]))
        for k in range(n):
            g2 = ln2(k - 1, res[k - 1]) if k >= 1 else None
            g1 = ln1(k + 1, res[k + 1]) if k + 1 < n else None
            run_par(back(k, res[k]), chain(g2, g1))
            if k + 2 < n:
                run_par(front(k + 2, res[k + 2]))
        run_par(ln2(n - 1, res[n - 1]))

    for (ti, t0, T, is_ctx) in TILES:
        load_tile(0, ti, t0, T, is_ctx)
    ada1 = ada_gen(1)
    for b in range(nb):
        P.label = "l0.pre"
        ps_set["banks"] = [0, 1, 2, 3, 4, 5]
        dma("sp", XEB.rearrange("p a b -> p (a b)"), xeD[b], [], [tXEB], "misc")
        for c in range(KC):
            ts("dve", HE[:, c, :], XEB[:, c, :], SCHX[:, c, b:b + 1], shift_ap(0, 0, c, b), ALU.mult, ALU.add,
               [tXEB, tCONST], [tHE])

        def l0_front(k, res, b=b):
            (ti, t0, T, is_ctx) = TILES[k]
            hm = make_h(0, ti, t0, T, 4 if is_ctx else b)
            return l0_mixer(b, ti, t0, T, is_ctx, hm)

        def l0_ln1(k, res, b=b):
            (ti, t0, T, is_ctx) = TILES[k]
            return layer_norm(0, 0, ti, t0, T, 4 if is_ctx else b, res)

        def l0_back(k, res, b=b):
            (ti, t0, T, is_ctx) = TILES[k]
            return ffn(0, ti, t0, T, 4 if is_ctx else b, res)

        def l0_ln2(k, res, b=b):
            (ti, t0, T, is_ctx) = TILES[k]
            return layer_norm(0, 1, ti, t0, T, 4 if is_ctx else b, res)

        if b == 0:
            deferred.append(late_casts)
        pipeline(5, l0_front, l0_ln1, l0_back, l0_ln2, extra=(ada1 if b == 0 else None))
        if b == 0:
            for _ in ada1:
                pass
        if stop_after_l0:
            for (ti, t0, T, is_ctx) in TILES[:4]:
                store_tile(b, ti, t0, T)
            continue
        run_par(l1_kv_tile(b, *TILES[0]), l1_kv_tile(b, *TILES[1]))
        run_par(l1_kv_tile(b, *TILES[2]), l1_kv_tile(b, *TILES[3]))
        run_par(l1_kv_tile(b, *TILES[4]))
        ps_set["banks"] = [0, 1, 2, 3]
        if b + 1 < nb:
            deferred.append((lambda b=b: load_tile(b + 1, 4, 2048, 256, True)))

        def l1_front(k, res, b=b):
            (ti, t0, T, is_ctx) = TILES[k]
            hm = make_h(1, ti, t0, T, b)
            return l1_att(b, ti, t0, T, hm)

        def l1_ln1(k, res, b=b):
            (ti, t0, T, is_ctx) = TILES[k]
            return layer_norm(1, 0, ti, t0, T, b, res)

        def l1_back(k, res, b=b):
            (ti, t0, T, is_ctx) = TILES[k]
            return ffn(1, ti, t0, T, b, res)

        def l1_ln2(k, res, b=b):
            (ti, t0, T, is_ctx) = TILES[k]

            def g():
                for _ in layer_norm(1, 1, ti, t0, T, b, res):
                    yield

                def fin(b=b, ti=ti, t0=t0, T=T):
                    store_tile(b, ti, t0, T)
                    if b + 1 < nb:
                        load_tile(b + 1, ti, t0, T, False)
                deferred.append(fin)
            return g()

        pipeline(4, l1_front, l1_ln1, l1_back, l1_ln2)
    flush_deferred()
    import os
    if os.environ.get("KDBG_LABELS"):
        P.dbg = []
    P.emit()
    if os.environ.get("KDBG_LABELS"):
        import json
        json.dump(P.dbg, open(os.environ["KDBG_LABELS"], "w"))
    return P


_CACHE = {}


def kernel(**inputs):
    nb = 4
    if "P" not in _CACHE:
        _CACHE["P"] = build(nb)
    P = _CACHE["P"]
    shared = prepare_shared(inputs)
    in_maps = []
    for core in range(NCORES):
        m = dict(shared)
        m.update(prepare_core(inputs, core, nb))
        in_maps.append(m)
    res = run_bass_kernel_spmd(P.nc, in_maps, core_ids=list(range(NCORES)))
    outs = []
    for core in range(NCORES):
        o = np.asarray(res.results[core]["outT"])
        outs.append(o.transpose(0, 3, 2, 1).reshape(nb, S, D))
    return np.ascontiguousarray(np.concatenate(outs, axis=0), dtype=np.float32)
```

```python
import numpy as np
import concourse.bass as bass
import concourse.mybir as mybir
from concourse.bass_utils import run_bass_kernel_spmd

F32 = mybir.dt.float32
BF16 = mybir.dt.bfloat16
AF = mybir.ActivationFunctionType
ALU = mybir.AluOpType

NCORES = 8
D = 1024
KC = 8
S = 2048
CTX = 256
NTOK = S + CTX
FF = 2816
FC = 22
NH = 8
HD = 128
ALPHA = float((2.0 * 2) ** 0.25)
LN_EPS = 1e-5
RMS_EPS = 1e-6
GRID_W = 64
NSLOT = 4
SLOTW = 2816
NFP = 11
NBP = 12
FPW = 516


class Tok:
    __slots__ = ("w", "r", "const")

    def __init__(self, const=False):
        self.w = {}
        self.r = {}
        self.const = const


class Prog:
    def __init__(self):
        self.nc = bass.Bass("TRN2", target_bir_lowering=False)
        nc = self.nc
        self.eng = {"pe": nc.tensor, "act": nc.scalar, "dve": nc.vector, "pool": nc.gpsimd, "sp": nc.sync}
        self.ops = []
        self.last_dma = {}
        self.label = "setup"
        self.pe_labels = []
        self.dbg = None

    def op(self, eng, fn, reads=(), writes=(), dma=None, nmm=0):
        deps = {}
        if nmm:
            self.pe_labels.extend([self.label] * nmm)

        def add(d):
            for k, i in d.items():
                if deps.get(k, -1) < i:
                    deps[k] = i

        for t in reads:
            add(t.w)
        for t in writes:
            add(t.w)
            add(t.r)
        i = len(self.ops)
        key = dma if dma is not None else eng
        if dma is not None:
            if key in self.last_dma:
                add({key: self.last_dma[key]})
            self.last_dma[key] = i
        self.ops.append([eng, fn, deps, key, dma is not None, None, False, self.label])
        for t in reads:
            if not t.const:
                if t.r.get(key, -1) < i:
                    t.r[key] = i
        for t in writes:
            t.w = {key: i}
            t.r = {}
        return i

    def emit(self):
        nc = self.nc
        ops = self.ops
        needed = set()
        for o in ops:
            needed.update(o[2].values())
        keys = []
        for o in ops:
            if o[3] not in keys:
                keys.append(o[3])
        import contextlib
        with contextlib.ExitStack() as es:
            sems = {k: es.enter_context(nc.semaphore("s_" + k)) for k in keys}
            cnt = {k: 0 for k in keys}
            for i, o in enumerate(ops):
                if o[4]:
                    cnt[o[3]] += 16
                    o[5] = cnt[o[3]]
                elif i in needed:
                    cnt[o[3]] += 1
                    o[5] = cnt[o[3]]
                    o[6] = True
            seen = {e: {} for e in self.eng}
            nwait = 0
            for i, o in enumerate(ops):
                e = o[0]
                h = self.eng[e]
                for k, d in o[2].items():
                    p = ops[d]
                    if e == "pe" and p[0] == "pe" and not p[4]:
                        continue
                    v = p[5]
                    if seen[e].get(k, 0) < v:
                        h.wait_ge(sems[k], v)
                        seen[e][k] = v
                        nwait += 1
                ins = o[1](h)
                if self.dbg is not None:
                    try:
                        self.dbg.append((e, int(ins.ins.name.split("-")[1]), o[7]))
                    except Exception:
                        pass
                if o[4]:
                    ins.then_inc(sems[o[3]], 16)
                elif o[6]:
                    ins.then_inc(sems[e], 1)
            for k in keys:
                if cnt[k] > 0 and seen["sp"].get(k, 0) < cnt[k]:
                    nc.sync.wait_ge(sems[k], cnt[k])
            self.nwait = nwait
        return nc


def _piece_lhsT(W, ocs):
    K, N = W.shape
    kc, no = K // 128, N // 128
    A = W.reshape(kc, 128, no, 128).transpose(2, 1, 0, 3)
    A = A.reshape(no // ocs, ocs, 128, kc, 128).transpose(0, 2, 1, 3, 4)
    return np.ascontiguousarray(A.reshape(no // ocs, 128, ocs * kc * 128))


def _piece_moving(W, kcs):
    K, N = W.shape
    kc = K // 128
    A = W.reshape(kc, 128, N).transpose(1, 0, 2)
    A = A.reshape(128, kc // kcs, kcs * N).transpose(1, 0, 2)
    return np.ascontiguousarray(A)


def _vec_pc(v):
    v = np.asarray(v)
    lead = v.shape[:-1]
    n = v.shape[-1] // 128
    A = v.reshape(lead + (n, 128))
    A = np.moveaxis(A, -1, 0)
    return np.ascontiguousarray(A)


W_MI = 0
W_MO = 10
W_FI0 = 14
W_Q = 36
W_K = 40
W_V = 41
W_O = 42
W_FI1 = 46
NW1 = 68
NW2 = 16


def _rope_tables():
    f32 = np.float32
    t = np.arange(S)
    row = (t // GRID_W).astype(f32)
    col = (t % GRID_W).astype(f32)
    nf = HD // 4
    inv = (np.float32(10000.0) ** (-(np.arange(nf, dtype=f32) / f32(nf)))).astype(f32)
    ang_r = (row[:, None] * inv[None, :]).astype(f32)
    ang_c = (col[:, None] * inv[None, :]).astype(f32)
    cr, sr = np.cos(ang_r).astype(f32), np.sin(ang_r).astype(f32)
    cc, sc = np.cos(ang_c).astype(f32), np.sin(ang_c).astype(f32)
    COS = np.concatenate([cr, cr, cc, cc], axis=1).T
    SIN = np.concatenate([-sr, sr, -sc, sc], axis=1).T
    perm = np.arange(128)
    perm = np.where((perm // 32) % 2 == 0, perm + 32, perm - 32)
    PM = np.zeros((128, 128), f32)
    PM[perm, np.arange(128)] = 1.0
    return np.ascontiguousarray(COS, dtype=f32), np.ascontiguousarray(SIN, dtype=f32), PM


def prepare_shared(inp):
    f = lambda k: np.asarray(inp[k], dtype=np.float32)
    mix_in, mix_out = f("mix_w_in")[0], f("mix_w_out")[0]
    fin, fout = f("ffn_w_in"), f("ffn_w_out")
    qkv, wo = f("attn_w_qkv")[0], f("attn_w_out")[0]
    w1 = np.empty((NW1, 128, 2048), np.float32)
    w1[W_MI + 0:W_MI + 2] = _piece_lhsT(mix_in[:, 512:1024], 2)
    w1[W_MI + 2:W_MI + 4] = _piece_lhsT(mix_in[:, 1024:1536], 2)
    w1[W_MI + 4:W_MI + 6] = _piece_lhsT(mix_in[:, 0:512], 2)
    w1[W_MI + 6:W_MI + 8] = _piece_lhsT(mix_in[:, 1536:2048], 2)
    w1[W_MI + 8:W_MI + 10] = _piece_moving(mix_in[:, 2048:2560], 4)
    w1[W_MO:W_MO + 4] = _piece_lhsT(mix_out, 2)
    for l, base in ((0, W_FI0), (1, W_FI1)):
        g = _piece_lhsT(fin[l][:, :FF], 1)
        u = _piece_lhsT(fin[l][:, FF:], 1)
        w1[base:base + FC] = np.concatenate([g, u], axis=2)
    w1[W_Q:W_Q + 4] = _piece_lhsT(qkv[:, 0:1024], 2)
    w1[W_K:W_K + 1] = _piece_lhsT(qkv[:, 1024:1280], 2)
    w1[W_V:W_V + 1] = _piece_moving(qkv[:, 1280:1536], 8)
    w1[W_O:W_O + 4] = _piece_lhsT(wo, 2)
    w2 = np.empty((NW2, 128, SLOTW), np.float32)
    w2[0:8] = _piece_lhsT(fout[0], 1)
    w2[8:16] = _piece_lhsT(fout[1], 1)
    ada = f("ada_w")
    A = ada.reshape(2, 2, 4, 128, 48, 128)
    A = A.transpose(0, 4, 1, 3, 2, 5)
    adaw = np.ascontiguousarray(A.reshape(2 * 48 * 2, 128, 512))
    adab = _vec_pc(f("ada_b")).reshape(128, 96)
    lng = _vec_pc(f("ln_g")).reshape(128, 32)
    lnb = _vec_pc(f("ln_b")).reshape(128, 32)
    cw = np.ascontiguousarray(_vec_pc(f("conv_w")[0]).transpose(0, 2, 1)).reshape(128, 12)
    wsT = np.ascontiguousarray(f("sgu_w")[0].transpose(2, 0, 1)).reshape(128, 1024)
    sb = f("sgu_b")[0]
    bsr = np.ascontiguousarray(
        np.repeat(sb.reshape(4, 2, 1, 128), 64, axis=2).reshape(4, 128, 128).transpose(1, 0, 2)).reshape(128, 512)
    gam = _vec_pc(f("sgu_ln_g")[0]).reshape(128, 4)
    betb = np.ascontiguousarray(np.broadcast_to(f("sgu_ln_b")[0][None, :], (128, 512)))
    qg = f("q_norm_g")[0].reshape(128, 1)
    kg = f("k_norm_g")[0].reshape(128, 1)
    COS, SIN, PM = _rope_tables()
    small = np.concatenate([adab, lng, lnb, cw, gam, qg, kg], axis=1)
    return dict(w1=w1, w2=w2, adaw=adaw, small=np.ascontiguousarray(small), wsT=wsT, bsr=bsr, betb=betb,
                cosT=COS, sinT=SIN, pm=PM)


SM_ADAB, SM_LNG, SM_LNB, SM_CW, SM_GAM, SM_QG, SM_KG, SM_W = 0, 96, 128, 160, 172, 176, 177, 178


def prepare_core(inp, core, nb):
    x = np.asarray(inp["x"], dtype=np.float32)[core * nb:(core + 1) * nb]
    ctx = np.asarray(inp["ctx"], dtype=np.float32)[core * nb:(core + 1) * nb]
    c = np.asarray(inp["c"], dtype=np.float32)[core * nb:(core + 1) * nb]
    cctx = np.asarray(inp["c_ctx"], dtype=np.float32)
    xT = np.ascontiguousarray(x.reshape(nb, S, KC, 128).transpose(0, 3, 2, 1))
    ctxT = np.ascontiguousarray(ctx.reshape(nb, CTX, KC, 128).transpose(0, 3, 2, 1))
    cc = np.zeros((5, D), np.float32)
    cc[:nb] = c
    cc[4] = cctx
    ccT = np.ascontiguousarray(cc.reshape(5, KC, 128).transpose(2, 1, 0))
    xe = np.ascontiguousarray(xT[:, :, :, [511, 512, 1023, 1024, 1535, 1536]]).reshape(nb, 128, KC * 6)
    return dict(xT=xT, ctxT=ctxT, cc=ccT.reshape(128, 40), xe=xe)


def build(nb=4, stop_after_l0=False):
    P = Prog()
    nc = P.nc
    op = P.op

    def dram(name, shape, dt, kind):
        return nc.dram_tensor(name, list(shape), dt, kind=kind).ap()

    xT = dram("xT", [nb, 128, KC, S], F32, "ExternalInput")
    ctxT = dram("ctxT", [nb, 128, KC, CTX], F32, "ExternalInput")
    ccD = dram("cc", [128, 40], F32, "ExternalInput")
    xeD = dram("xe", [nb, 128, KC * 6], F32, "ExternalInput")
    w1D = dram("w1", [NW1, 128, 2048], F32, "ExternalInput")
    w2D = dram("w2", [NW2, 128, SLOTW], F32, "ExternalInput")
    adawD = dram("adaw", [192, 128, 512], F32, "ExternalInput")
    smallD = dram("small", [128, SM_W], F32, "ExternalInput")
    wsTD = dram("wsT", [128, 1024], F32, "ExternalInput")
    bsrD = dram("bsr", [128, 512], F32, "ExternalInput")
    betbD = dram("betb", [128, 512], F32, "ExternalInput")
    cosD = dram("cosT", [128, S], F32, "ExternalInput")
    sinD = dram("sinT", [128, S], F32, "ExternalInput")
    pmD = dram("pm", [128, 128], F32, "ExternalInput")
    outT = dram("outT", [nb, 128, KC, S], F32, "ExternalOutput")
    w1B = dram("w1b", [NW1, 128, 2048], BF16, "Internal")
    w2B = dram("w2b", [NW2, 128, SLOTW], BF16, "Internal")

    def sb(name, shape, dt):
        return nc.alloc_sbuf_tensor(name, list(shape), dt).ap()

    XA = sb("XA", [128, KC, NTOK], F32)
    OP8 = [sb(f"OP8_{i}", [128, KC, 512], BF16) for i in range(3)]
    WR = [sb(f"WR{i}", [128, SLOTW], BF16) for i in range(NSLOT)]
    HID = sb("HID", [128, FC, 512], BF16)
    FPB = [sb(f"FP{i}", [128, FPW], F32) for i in range(NFP)]
    BPB = [sb(f"BP{i}", [128, 512], BF16) for i in range(NBP)]
    ZH = sb("ZH", [128, 4, 6], F32)
    GH = sb("GH", [128, 4, 6], F32)
    HE = sb("HE", [128, KC, 6], BF16)
    XEB = sb("XEB", [128, KC, 6], F32)
    KT = sb("KT", [128, 2, NTOK], BF16)
    RG = sb("RG", [128, (NTOK // 128) * 256], BF16)
    VT = RG.rearrange("p (a b) -> p a b", b=256)
    Z = RG[:, 0:4 * 514 * 2].bitcast(F32).rearrange("p (a b) -> p a b", b=514)
    SMALL = sb("SMALL", [128, SM_W], F32)
    CC = sb("CC", [128, KC, 5], F32)
    SIL = sb("SIL", [128, KC, 5], F32)
    MOD = sb("MOD", [128, 96, 5], F32)
    SCH = sb("SCH", [128, 4, KC, 5], F32)
    GA = sb("GA", [128, 32], F32)
    BA = sb("BA", [128, 32], F32)
    QG = sb("QG", [128, 1], F32)
    WST = sb("WST", [128, 8, 128], BF16)
    WSTF = FPB
    BS2 = sb("BS2", [128, 4, 128], F32)
    ONES_LN = sb("ONES_LN", [128, 128], BF16)
    ONES_HD = sb("ONES_HD", [128, 128], BF16)
    ONES_1 = sb("ONES_1", [128, 128], BF16)
    PMS = sb("PMS", [128, 128], F32)
    PMSB = sb("PMSB", [128, 128], BF16)
    STAT = [sb(f"STAT{i}", [128, 16], F32) for i in range(6)]
    PS = [nc.alloc_psum_tensor(f"PS{i}", [128, 512], F32).ap() for i in range(8)]

    tXA = [[Tok() for _ in range(KC)] for _ in range(5)]
    tOP8 = [[Tok() for _ in range(KC)] for _ in range(3)]
    tWR = [Tok() for _ in range(NSLOT)]
    tHID = [Tok() for _ in range(FC)]
    tFP = [Tok() for _ in range(NFP)]
    tBP = [Tok() for _ in range(NBP)]
    tZ = [Tok() for _ in range(4)]
    tZH, tGH, tHE, tXEB = Tok(), Tok(), Tok(), Tok()
    tKT = [[Tok() for _ in range(5)] for _ in range(2)]
    tVT = [Tok() for _ in range(NTOK // 128)]
    tSTAT = [Tok() for _ in range(6)]
    tPS = [Tok() for _ in range(8)]
    tCONST = Tok(const=True)
    tW1 = [Tok(const=True) for _ in range(NW1)]
    tW2 = [Tok(const=True) for _ in range(NW2)]

    st = dict(wr=0, fp=0, bp=0, ps=0, stat=0, op8=0, ndma=0, ncast=0)

    held = {"fp": set(), "bp": set()}

    def _pool(kind, n, hold):
        i = st[kind]
        k = 0
        while i in held[kind]:
            i = (i + 1) % n
            k += 1
            assert k <= n, "pool exhausted " + kind
        st[kind] = (i + 1) % n
        if hold:
            held[kind].add(i)
        return i

    def fpool(hold=False):
        i = _pool("fp", NFP, hold)
        return FPB[i], tFP[i]

    def bpool(hold=False):
        i = _pool("bp", NBP, hold)
        return BPB[i], tBP[i]

    def frelease(buf):
        held["fp"].discard([k for k in range(NFP) if FPB[k] is buf][0])

    def brelease(buf):
        held["bp"].discard([k for k in range(NBP) if BPB[k] is buf][0])

    def spool():
        i = st["stat"]
        st["stat"] = (i + 1) % 6
        return STAT[i], tSTAT[i]

    ps_set = {"banks": [0, 1, 2, 3, 4, 5]}
    ps_held = set()

    def psum(hold=False):
        banks = ps_set["banks"]
        k = 0
        while True:
            i = banks[st["ps"] % len(banks)]
            st["ps"] += 1
            if i not in ps_held:
                break
            k += 1
            assert k <= len(banks), "psum exhausted"
        if hold:
            ps_held.add(i)
        return PS[i], tPS[i]

    def prelease(ps_):
        ps_held.discard([k for k in range(8) if PS[k] is ps_][0])

    op8_held = set()

    def op8():
        i = st["op8"]
        k = 0
        while i in op8_held:
            i = (i + 1) % 3
            k += 1
            assert k <= 3, "op8 exhausted"
        st["op8"] = (i + 1) % 3
        op8_held.add(i)
        return OP8[i], tOP8[i]

    def orelease(buf):
        op8_held.discard([k for k in range(3) if OP8[k] is buf][0])

    def dma(eng, out, in_, reads, writes, stream):
        if stream == "misc":
            k = st["ndma"]
            st["ndma"] = k + 1
            stream = stream + str(k % 4)
        elif stream == "cast":
            k = st["ncast"]
            st["ncast"] = k + 1
            stream = stream + str(k % 12)
        return op(eng, lambda h: h.dma_start(out=out, in_=in_), reads, writes, dma=stream)

    def wload(which, idx):
        s = st["wr"]
        st["wr"] = (s + 1) % NSLOT
        if which == 1:
            src, tk, n = w1B[idx], tW1[idx], 2048
        else:
            src, tk, n = w2B[idx], tW2[idx], SLOTW
        dma("sp", WR[s][:, 0:n], src, [tk], [tWR[s]], f"w{s}")
        return WR[s], tWR[s]

    def mm(out, pairs, reads, wtok):
        def fn(h):
            n = len(pairs)
            ins = None
            for i, (l, r) in enumerate(pairs):
                ins = h.matmul(out, lhsT=l, rhs=r, start=(i == 0), stop=(i == n - 1))
            return ins
        return op("pe", fn, reads, [wtok], nmm=len(pairs))

    def mm1(out, l, r, start, stop, reads, wtok):
        return op("pe", lambda h: h.matmul(out, lhsT=l, rhs=r, start=start, stop=stop), reads, [wtok], nmm=1)

    def tt(e, out, a, b, o, reads, writes):
        return op(e, lambda h: h.tensor_tensor(out=out, in0=a, in1=b, op=o), reads, writes)

    def ts(e, out, a, s1, s2, o1, o2, reads, writes):
        if s2 is None:
            s2 = 1.0 if o1 == ALU.add else 0.0
            o2 = ALU.mult if o1 == ALU.add else ALU.add
        return op(e, lambda h: h.tensor_scalar(out=out, in0=a, scalar1=s1, scalar2=s2, op0=o1, op1=o2), reads, writes)

    def stt(out, a, s, b, o1, o2, reads, writes):
        return op("dve", lambda h: h.scalar_tensor_tensor(out=out, in0=a, scalar=s, in1=b, op0=o1, op1=o2),
                  reads, writes)

    def act(out, in_, func, reads, writes, bias=None, scale=None):
        kw = {}
        if bias is not None:
            kw["bias"] = bias
        if scale is not None:
            kw["scale"] = scale
        return op("act", lambda h: h.activation(out=out, in_=in_, func=func, **kw), reads, writes)

    order1 = list(range(W_MI, W_MI + 10)) + list(range(W_MO, W_MO + 4)) + list(range(W_FI0, W_FI0 + FC))
    order1b = list(range(W_Q, NW1))

    def cast1(lo, hi):
        dma("pool", w1B[lo:hi], w1D[lo:hi], [], [tW1[i] for i in range(lo, hi)], "cast")

    def cast2(lo, hi):
        dma("pool", w2B[lo:hi], w2D[lo:hi], [], [tW2[i] for i in range(lo, hi)], "cast")

    dma("sp", SMALL, smallD, [], [tCONST], "misc")
    dma("sp", CC.rearrange("p a b -> p (a b)"), ccD, [], [tCONST], "misc")
    dma("sp", PMS, pmD, [], [tCONST], "misc")
    dma("sp", BS2.rearrange("p a b -> p (a b)"), bsrD, [], [tCONST], "misc")
    cast1(8, 12)
    cast1(0, 4)
    cast1(4, 8)
    for lo in range(12, 36, 4):
        cast1(lo, lo + 4)
    cast2(0, 4)
    cast2(4, 8)

    def late_casts():
        for lo in range(36, NW1, 4):
            cast1(lo, lo + 4)
        cast2(8, 12)
        cast2(12, 16)

    op("dve", lambda h: h.memset(ONES_LN, 1.0 / D), [], [tCONST])
    op("dve", lambda h: h.memset(ONES_HD, 1.0 / HD), [], [tCONST])
    op("dve", lambda h: h.memset(ONES_1, 1.0), [], [tCONST])
    act(SIL, CC, AF.Silu, [tCONST], [tCONST])
    op("dve", lambda h: h.tensor_copy(out=PMSB, in_=PMS), [tCONST], [tCONST])
    ts("dve", GA[:, 0:24], SMALL[:, SM_LNG:SM_LNG + 24], ALPHA, None, ALU.mult, None, [tCONST], [tCONST])
    ts("dve", GA[:, 24:32], SMALL[:, SM_LNG + 24:SM_LNG + 32], 1.0, None, ALU.mult, None, [tCONST], [tCONST])
    ts("dve", BA[:, 0:24], SMALL[:, SM_LNB:SM_LNB + 24], ALPHA, None, ALU.mult, None, [tCONST], [tCONST])
    ts("dve", BA[:, 24:32], SMALL[:, SM_LNB + 24:SM_LNB + 32], 1.0, None, ALU.mult, None, [tCONST], [tCONST])
    ts("dve", QG, SMALL[:, SM_QG:SM_QG + 1], float(HD ** -0.5), None, ALU.mult, None, [tCONST], [tCONST])

    wf = []
    for hlf in range(2):
        b_, t_ = fpool(hold=True)
        dma("sp", b_[:, 0:512], wsTD[:, hlf * 512:(hlf + 1) * 512], [], [t_], "misc")
        wf.append((b_, t_))
        op("dve", (lambda b_=b_, hlf=hlf: lambda h: h.tensor_copy(
            out=WST[:, hlf * 4:(hlf + 1) * 4, :].rearrange("p a b -> p (a b)"), in_=b_[:, 0:512]))(),
           [t_], [tCONST])
    bb, tbb = fpool(hold=True)
    dma("sp", bb[:, 0:512], betbD, [], [tbb], "misc")
    for jc in range(4):
        ps_, tp_ = psum()
        for hh in range(2):
            g = 2 * jc + hh
            wsrc, twsrc = wf[g // 4]
            mm1(ps_[64 * hh:64 * hh + 64, 0:128], bb[:, jc * 128 + 64 * hh:jc * 128 + 64 * hh + 64],
                wsrc[:, (g % 4) * 128:(g % 4) * 128 + 128], True, True, [tbb, twsrc], tp_)
        tt("dve", BS2[:, jc, :], ps_[:, 0:128], BS2[:, jc, :], ALU.add, [tp_, tCONST], [tCONST])
    frelease(bb)
    frelease(wf[0][0])
    frelease(wf[1][0])

    SCHX = sb("SCHX", [128, KC, 5], F32)

    def ada_gen(l):
        for oc in range(48):
            P.label = "setup"
            bufs = []
            for hlf in range(2):
                b_, t_ = fpool()
                dma("sp", b_[:, 0:512], adawD[(l * 48 + oc) * 2 + hlf], [], [t_], "misc")
                bufs.append((b_, t_))
            ps_, tp_ = psum()
            pairs = []
            for kc in range(KC):
                b_, t_ = bufs[kc // 4]
                pairs.append((b_[:, (kc % 4) * 128:(kc % 4) * 128 + 128], SIL[:, kc, :]))
            mm(ps_[:, 0:5], pairs, [bufs[0][1], bufs[1][1], tCONST], tp_)
            ts("dve", MOD[:, l * 48 + oc, :], ps_[:, 0:5],
               SMALL[:, SM_ADAB + l * 48 + oc:SM_ADAB + l * 48 + oc + 1], None, ALU.add, None, [tp_, tCONST],
               [tCONST])
            yield
        P.label = "setup"
        for s_ in range(2):
            base = l * 48 + (3 * s_ + 1) * 8
            ts("dve", SCH[:, l * 2 + s_].rearrange("p a b -> p (a b)"),
               MOD[:, base:base + 8, :].rearrange("p a b -> p (a b)"), 1.0, 1.0 / ALPHA, ALU.add, ALU.mult,
               [tCONST], [tCONST])
        if l == 0:
            ts("dve", SCHX.rearrange("p a b -> p (a b)"), MOD[:, 8:16, :].rearrange("p a b -> p (a b)"),
               1.0, 1.0, ALU.add, ALU.mult, [tCONST], [tCONST])

    for _ in ada_gen(0):
        pass

    def shift_ap(l, s_, c, j):
        return MOD[:, l * 48 + (3 * s_) * 8 + c, j:j + 1]

    def gate_ap(l, s_, c, j):
        return MOD[:, l * 48 + (3 * s_ + 2) * 8 + c, j:j + 1]

    def sch_ap(l, s_, c, j):
        return SCH[:, l * 2 + s_, c, j:j + 1]

    TILES = [(0, 0, 512, False), (1, 512, 512, False), (2, 1024, 512, False), (3, 1536, 512, False),
             (4, 2048, 256, True)]
    deferred = []

    def flush_deferred():
        for f_ in deferred:
            f_()
        deferred.clear()

    def make_h(l, ti, t0, T, j):
        buf, tk = op8()
        for c in range(KC):
            e = "dve" if c % 2 == 0 else "pool"
            ts(e, buf[:, c, 0:T], XA[:, c, t0:t0 + T], sch_ap(l, 0, c, j), shift_ap(l, 0, c, j), ALU.mult, ALU.add,
               [tXA[ti][c], tCONST], [tk[c]])
        return buf, tk

    def run_par(*gens):
        gens = [g for g in gens if g is not None]
        while gens:
            for g in list(gens):
                try:
                    next(g)
                except StopIteration:
                    gens.remove(g)

    def layer_norm(l, s_, ti, t0, T, j, res):
        lab = f"ln{l}{s_}"
        P.label = lab
        mean_ps, tm = PS[6], tPS[6]
        ex2_ps, te = PS[7], tPS[7]
        pend = []

        def stats_mm():
            for (c, rb, trb, sq, tsq) in pend:
                mm1(mean_ps[:, 0:T], ONES_LN, rb[:, 0:T], c == 0, c == KC - 1, [trb, tCONST], tm)
                mm1(ex2_ps[:, 0:T], ONES_LN, sq[:, 0:T], c == 0, c == KC - 1, [tsq, tCONST], te)
                brelease(rb)
                brelease(sq)
            pend.clear()
        for c in range(KC):
            P.label = lab
            if c % 2 == 0:
                stats_mm()
            rb, trb = bpool(hold=True)
            op("pool", (lambda rb=rb, c=c: lambda h: h.tensor_copy(out=rb[:, 0:T], in_=XA[:, c, t0:t0 + T]))(),
               [tXA[ti][c]], [trb])
            sq, tsq = bpool(hold=True)
            e = "pool" if c % 4 != 3 else "dve"
            tt(e, sq[:, 0:T], XA[:, c, t0:t0 + T], XA[:, c, t0:t0 + T], ALU.mult, [tXA[ti][c]], [tsq])
            pend.append((c, rb, trb, sq, tsq))
            if c % 2 == 1:
                yield
        P.label = lab
        stats_mm()
        P.label = lab
        msb, tmsb = fpool(hold=True)
        op("dve", lambda h: h.tensor_copy(out=msb[:, 0:T], in_=mean_ps[:, 0:T]), [tm], [tmsb])
        var, tvar = fpool(hold=True)
        tt("pool", var[:, 0:T], msb[:, 0:T], msb[:, 0:T], ALU.mult, [tmsb], [tvar])
        tt("dve", var[:, 0:T], ex2_ps[:, 0:T], var[:, 0:T], ALU.subtract, [te, tvar], [tvar])
        A_, tA = fpool(hold=True)
        act(A_[:, 0:T], var[:, 0:T], AF.Ln, [tvar], [tA], bias=LN_EPS)
        act(A_[:, 0:T], A_[:, 0:T], AF.Exp, [tA], [tA], scale=-0.5)
        frelease(var)
        B_, tB = fpool(hold=True)
        stt(B_[:, 0:T], msb[:, 0:T], -1.0, A_[:, 0:T], ALU.mult, ALU.mult, [tmsb, tA], [tB])
        hf = None
        if s_ == 0:
            hf = op8()
            res["hf"] = hf
        yield
        gi = (l * 2 + s_) * 8
        for c in range(KC):
            P.label = lab
            xa = XA[:, c, t0:t0 + T]
            tx = tXA[ti][c]
            on_dve = c in (1, 3, 5)
            if on_dve:
                stt(xa, xa, GA[:, gi + c:gi + c + 1], A_[:, 0:T], ALU.mult, ALU.mult, [tx, tA, tCONST], [tx])
                stt(xa, B_[:, 0:T], GA[:, gi + c:gi + c + 1], xa, ALU.mult, ALU.add, [tx, tB, tCONST], [tx])
                ts("dve", xa, xa, BA[:, gi + c:gi + c + 1], None, ALU.add, None, [tx, tCONST], [tx])
            else:
                tt("pool", xa, xa, A_[:, 0:T], ALU.mult, [tx, tA], [tx])
                tt("pool", xa, xa, B_[:, 0:T], ALU.add, [tx, tB], [tx])
                ts("pool", xa, xa, GA[:, gi + c:gi + c + 1], BA[:, gi + c:gi + c + 1], ALU.mult, ALU.add,
                   [tx, tCONST], [tx])
            if s_ == 0:
                e = "dve" if on_dve else "pool"
                ts(e, hf[0][:, c, 0:T], xa, sch_ap(l, 1, c, j), shift_ap(l, 1, c, j), ALU.mult, ALU.add,
                   [tx, tCONST], [hf[1][c]])
            if c % 2 == 1:
                yield
        frelease(msb)
        frelease(A_)
        frelease(B_)

    def ffn(l, ti, t0, T, j, res):
        hb, thb = res["hf"]
        base = W_FI0 if l == 0 else W_FI1
        for jj in range(FC):
            P.label = f"ffn_in{l}"
            w, tw = wload(1, base + jj)
            pg, tpg = psum()
            mm(pg[:, 0:T], [(w[:, kc * 128:(kc + 1) * 128], hb[:, kc, 0:T]) for kc in range(KC)], [tw] + thb, tpg)
            pu, tpu = psum()
            mm(pu[:, 0:T], [(w[:, (8 + kc) * 128:(9 + kc) * 128], hb[:, kc, 0:T]) for kc in range(KC)],
               [tw] + thb, tpu)
            sg, tsg = fpool()
            act(sg[:, 0:T], pg[:, 0:T], AF.Silu, [tpg], [tsg])
            tt("dve", HID[:, jj, 0:T], pu[:, 0:T], sg[:, 0:T], ALU.mult, [tpu, tsg], [tHID[jj]])
            if jj == 10:
                flush_deferred()
            yield
        orelease(hb)
        for oc in range(KC):
            P.label = f"ffn_out{l}"
            w, tw = wload(2, l * 8 + oc)
            po, tpo = psum()
            mm(po[:, 0:T], [(w[:, kc * 128:(kc + 1) * 128], HID[:, kc, 0:T]) for kc in range(FC)], [tw] + tHID, tpo)
            xa = XA[:, oc, t0:t0 + T]
            stt(xa, po[:, 0:T], gate_ap(l, 1, oc, j), xa, ALU.mult, ALU.add, [tpo, tXA[ti][oc], tCONST],
                [tXA[ti][oc]])
            yield

    def l0_mixer(b, ti, t0, T, is_ctx, hm):
        j = 4 if is_ctx else b
        hb, thb = hm
        ntc = T // 128
        do_halo = (ti == 0)
        P.label = "l0.v"
        wv0, twv0 = wload(1, W_MI + 8)
        wv1, twv1 = wload(1, W_MI + 9)
        VN = []
        VG = []
        sst, tst = spool()
        for tc in range(ntc):
            P.label = "l0.v"
            pv, tpv = psum()
            pairs = []
            for kc in range(KC):
                wv = wv0 if kc < 4 else wv1
                pairs.append((hb[:, kc, tc * 128:(tc + 1) * 128], wv[:, (kc % 4) * 512:(kc % 4 + 1) * 512]))
            mm(pv, pairs, [twv0, twv1] + thb, tpv)
            vg, tvg = fpool(hold=True)
            act(vg[:, 0:512], pv, AF.Gelu, [tpv], [tvg])
            s6, ts6 = spool()
            op("dve", (lambda s6=s6, vg=vg: lambda h: h.bn_stats(out=s6[:, 0:6], in_=vg[:, 0:512]))(),
               [tvg], [ts6])
            op("dve", (lambda s6=s6, tc=tc: lambda h: h.bn_aggr(out=sst[:, 2 * tc:2 * tc + 2], in_=s6[:, 0:6]))(),
               [ts6, tst], [tst])
            VG.append((vg, tvg))
            yield
        P.label = "l0.v"
        varv = sst[:, 0:2 * ntc].rearrange("p (a b) -> p a b", b=2)[:, :, 1:2]
        rsv = sst[:, 8:8 + ntc].rearrange("p (a b) -> p a b", b=1)
        act(rsv, varv, AF.Ln, [tst], [tst], bias=LN_EPS)
        act(rsv, rsv, AF.Exp, [tst], [tst], scale=-0.5)
        for tc in range(ntc):
            vg, tvg = VG[tc]
            vn, tvn = bpool(hold=True)
            ts("dve", vn, vg[:, 0:512], sst[:, 2 * tc:2 * tc + 1], sst[:, 8 + tc:9 + tc], ALU.subtract, ALU.mult,
               [tvg, tst], [tvn])
            frelease(vg)
            VN.append((vn, tvn))
        yield
        for h2 in range(2):
            P.label = "l0.mixA"
            wg, twg = wload(1, W_MI + 0 + h2)
            wh, twh = wload(1, W_MI + 2 + h2)
            for o2 in range(2):
                P.label = "l0.mixA"
                oc = h2 * 2 + o2
                pg, tpg = psum()
                mm(pg[:, 0:T], [(wg[:, (o2 * 8 + kc) * 128:(o2 * 8 + kc + 1) * 128], hb[:, kc, 0:T])
                                for kc in range(KC)], [twg] + thb, tpg)
                gt, tgt = fpool()
                act(gt[:, 0:T], pg[:, 0:T], AF.Copy, [tpg], [tgt])
                ph, tph = psum()
                mm(ph[:, 0:T], [(wh[:, (o2 * 8 + kc) * 128:(o2 * 8 + kc + 1) * 128], hb[:, kc, 0:T])
                                for kc in range(KC)], [twh] + thb, tph)
                tt("dve", Z[:, oc, 1:T + 1], ph[:, 0:T], gt[:, 0:T], ALU.mult, [tph, tgt], [tZ[oc]] + tVT)
                if do_halo:
                    pg2, tpg2 = psum()
                    mm(pg2[:, 0:6], [(wg[:, (o2 * 8 + kc) * 128:(o2 * 8 + kc + 1) * 128], HE[:, kc, :])
                                     for kc in range(KC)], [twg, tHE], tpg2)
                    act(GH[:, oc, :], pg2[:, 0:6], AF.Copy, [tpg2], [tGH])
                    ph2, tph2 = psum()
                    mm(ph2[:, 0:6], [(wh[:, (o2 * 8 + kc) * 128:(o2 * 8 + kc + 1) * 128], HE[:, kc, :])
                                     for kc in range(KC)], [twh, tHE], tph2)
                    tt("dve", ZH[:, oc, :], ph2[:, 0:6], GH[:, oc, :], ALU.mult, [tph2, tGH], [tZH])
                yield
        P.label = "l0.mixA"
        if is_ctx or ti == 0:
            op("pool", lambda h: h.memset(Z[:, :, 0:1], 0.0), [], tZ + tVT)
        else:
            op("pool", lambda h: h.tensor_copy(out=Z[:, :, 0:1], in_=ZH[:, :, 2 * (ti - 1):2 * (ti - 1) + 1]),
               [tZH], tZ + tVT)
        if is_ctx or ti == 3:
            op("pool", lambda h: h.memset(Z[:, :, T + 1:T + 2], 0.0), [], tZ + tVT)
        else:
            op("pool", lambda h: h.tensor_copy(out=Z[:, :, T + 1:T + 2], in_=ZH[:, :, 2 * ti + 1:2 * ti + 2]),
               [tZH], tZ + tVT)
        yab, tyab = op8()
        ZC = []
        for oc in range(4):
            zc, tzc = fpool(hold=True)
            cwa = lambda tap, oc=oc: SMALL[:, SM_CW + oc * 3 + tap:SM_CW + oc * 3 + tap + 1]
            ts("pool", zc[:, 0:T], Z[:, oc, 1:T + 1], cwa(1), None, ALU.mult, None, [tZ[oc], tCONST], [tzc])
            stt(zc[:, 0:T], Z[:, oc, 0:T], cwa(0), zc[:, 0:T], ALU.mult, ALU.add, [tZ[oc], tzc, tCONST], [tzc])
            stt(zc[:, 0:T], Z[:, oc, 2:T + 2], cwa(2), zc[:, 0:T], ALU.mult, ALU.add, [tZ[oc], tzc, tCONST], [tzc])
            ZC.append((zc, tzc))
        yield
        for h2 in range(2):
            P.label = "l0.gb"
            wb_, twb = wload(1, W_MI + 4 + h2)
            for o2 in range(2):
                P.label = "l0.gb"
                oc = h2 * 2 + o2
                pb, tpb = psum()
                mm(pb[:, 0:T], [(wb_[:, (o2 * 8 + kc) * 128:(o2 * 8 + kc + 1) * 128], hb[:, kc, 0:T])
                                for kc in range(KC)], [twb] + thb, tpb)
                zc, tzc = ZC[oc]
                tt("dve", yab[:, oc, 0:T], pb[:, 0:T], zc[:, 0:T], ALU.mult, [tpb, tzc], [tyab[oc]])
                frelease(zc)
                yield
        for h2 in range(2):
            P.label = "l0.u"
            wu, twu = wload(1, W_MI + 6 + h2)
            for o2 in range(2):
                P.label = "l0.u"
                jc = h2 * 2 + o2
                pu, tpu = psum()
                mm(pu[:, 0:T], [(wu[:, (o2 * 8 + kc) * 128:(o2 * 8 + kc + 1) * 128], hb[:, kc, 0:T])
                                for kc in range(KC)], [twu] + thb, tpu)
                ub, tub = fpool()
                act(ub[:, 0:T], pu[:, 0:T], AF.Gelu, [tpu], [tub])
                P.label = "l0.sgu"
                ps_, tp_ = psum()

                def fn(h, ps_=ps_, jc=jc):
                    ins = None
                    for tc in range(ntc):
                        for hh in range(2):
                            g = 2 * jc + hh
                            ins = h.matmul(ps_[64 * hh:64 * hh + 64, tc * 128:(tc + 1) * 128],
                                           lhsT=VN[tc][0][:, g * 64:(g + 1) * 64], rhs=WST[:, g, :],
                                           start=True, stop=True)
                    return ins
                op("pe", fn, [v[1] for v in VN] + [tCONST], [tp_], nmm=2 * ntc)
                tmp, ttmp = fpool()
                stt(tmp[:, 0:T].rearrange("p (a b) -> p a b", b=128),
                    ps_[:, 0:T].rearrange("p (a b) -> p a b", b=128),
                    SMALL[:, SM_GAM + jc:SM_GAM + jc + 1], BS2[:, jc:jc + 1, :].broadcast_to([128, ntc, 128]),
                    ALU.mult, ALU.add, [tp_, tCONST], [ttmp])
                tt("dve", yab[:, 4 + jc, 0:T], tmp[:, 0:T], ub[:, 0:T], ALU.mult, [ttmp, tub], [tyab[4 + jc]])
                yield
        for v_ in VN:
            brelease(v_[0])
        for pz in range(4):
            P.label = "l0.mixout"
            wo_, two = wload(1, W_MO + pz)
            for o2 in range(2):
                P.label = "l0.mixout"
                oc = pz * 2 + o2
                po, tpo = psum()
                mm(po[:, 0:T], [(wo_[:, (o2 * 8 + kc) * 128:(o2 * 8 + kc + 1) * 128], yab[:, kc, 0:T])
                                for kc in range(KC)], [two] + tyab, tpo)
                xa = XA[:, oc, t0:t0 + T]
                stt(xa, po[:, 0:T], gate_ap(0, 0, oc, j), xa, ALU.mult, ALU.add, [tpo, tXA[ti][oc], tCONST],
                    [tXA[ti][oc]])
                yield
        orelease(hb)
        orelease(yab)

    def qk_norm_rope(ps_, tp_, T, gvec, rope, dst, tdst, lab):
        P.label = lab
        sq, tsq = bpool(hold=True)
        act(sq[:, 0:T], ps_[:, 0:T], AF.Square, [tp_], [tsq])
        yield
        P.label = lab
        pm_, tpm = psum()
        mm1(pm_[:, 0:T], ONES_HD, sq[:, 0:T], True, True, [tsq, tCONST], tpm)
        brelease(sq)
        A_, tA = fpool()
        act(A_[:, 0:T], pm_[:, 0:T], AF.Ln, [tpm], [tA], bias=RMS_EPS)
        act(A_[:, 0:T], A_[:, 0:T], AF.Exp, [tA], [tA], scale=-0.5)
        if rope is None:
            stt(dst, ps_[:, 0:T], gvec, A_[:, 0:T], ALU.mult, ALU.mult, [tp_, tA, tCONST], tdst)
            prelease(ps_)
            return
        kn, tkn = bpool(hold=True)
        stt(kn[:, 0:T], ps_[:, 0:T], gvec, A_[:, 0:T], ALU.mult, ALU.mult, [tp_, tA, tCONST], [tkn])
        prelease(ps_)
        yield
        P.label = lab
        (cs, tcs), (sn, tsn) = rope
        pp, tpp = psum()
        mm1(pp[:, 0:T], PMSB, kn[:, 0:T], True, True, [tkn, tCONST], tpp)
        t1, tt1 = fpool()
        tt("pool", t1[:, 0:T], kn[:, 0:T], cs[:, 0:T], ALU.mult, [tkn, tcs], [tt1])
        t2, tt2 = fpool()
        tt("dve", t2[:, 0:T], pp[:, 0:T], sn[:, 0:T], ALU.mult, [tpp, tsn], [tt2])
        tt("dve", dst, t1[:, 0:T], t2[:, 0:T], ALU.add, [tt1, tt2], tdst)
        brelease(kn)

    def load_rope(t0, T):
        cs, tcs = fpool(hold=True)
        dma("sp", cs[:, 0:T], cosD[:, t0:t0 + T], [], [tcs], "misc")
        sn, tsn = fpool(hold=True)
        dma("sp", sn[:, 0:T], sinD[:, t0:t0 + T], [], [tsn], "misc")
        return (cs, tcs), (sn, tsn)

    def l1_kv_tile(b, ti, t0, T, is_ctx):
        P.label = "l1.kv"
        j = 4 if is_ctx else b
        hb, thb = make_h(1, ti, t0, T, j)
        rope = None if is_ctx else load_rope(t0, T)
        wk, twk = wload(1, W_K)
        wv, twv = wload(1, W_V)
        chains = []
        for kvh in range(2):
            P.label = "l1.kv"
            pk, tpk = psum(hold=True)
            mm(pk[:, 0:T], [(wk[:, (kvh * 8 + kc) * 128:(kvh * 8 + kc + 1) * 128], hb[:, kc, 0:T])
                            for kc in range(KC)], [twk] + thb, tpk)
            chains.append(qk_norm_rope(pk, tpk, T, SMALL[:, SM_KG:SM_KG + 1], rope, KT[:, kvh, t0:t0 + T],
                                       [tKT[kvh][ti]], "l1.kv"))
        vsteps = []
        for tc in range(T // 128):
            def vstep(tc=tc):
                P.label = "l1.kv"
                pv, tpv = psum()
                mm(pv[:, 0:256], [(hb[:, kc, tc * 128:(tc + 1) * 128], wv[:, kc * 256:(kc + 1) * 256])
                                  for kc in range(KC)], [twv] + thb, tpv)
                g = t0 // 128 + tc
                act(VT[:, g, :], pv[:, 0:256], AF.Copy, [tpv], [tVT[g]] + tZ)
            vsteps.append(vstep)

        def vgen():
            for f_ in vsteps:
                f_()
                yield
        for _ in kv_driver(chains, vgen()):
            yield
        orelease(hb)
        if rope is not None:
            frelease(rope[0][0])
            frelease(rope[1][0])

    def kv_driver(chains, vg):
        gens = list(chains) + [vg]
        while gens:
            for g in list(gens):
                try:
                    next(g)
                except StopIteration:
                    gens.remove(g)
            yield

    def l1_att(b, ti, t0, T, hm):
        j = b
        hb, thb = hm
        rope = load_rope(t0, T)
        at, tat = op8()
        NKC = NTOK // 128

        def qproj(pz):
            P.label = "l1.qproj"
            wq, twq = wload(1, W_Q + pz)
            for o2 in range(2):
                P.label = "l1.qproj"
                pq, tpq = psum(hold=True)
                mm(pq[:, 0:T], [(wq[:, (o2 * 8 + kc) * 128:(o2 * 8 + kc + 1) * 128], hb[:, kc, 0:T])
                                for kc in range(KC)], [twq] + thb, tpq)
                qt, tqt = bpool(hold=True)
                qres[pz].append((qt, tqt))
                for _ in qk_norm_rope(pq, tpq, T, QG[:, 0:1], rope, qt[:, 0:T], [tqt], "l1.qproj"):
                    yield
                yield

        def attend(hd_, qt, tqt):
            P.label = "l1.attn"
            kv = hd_ // 4
            po, tpo = PS[4], tPS[4]
            pd, tpd = PS[5], tPS[5]

            def smm(kc):
                ps_, tp_ = psum(hold=True)
                ktile = min(kc // 4, 4)
                mm1(ps_[:, 0:T], KT[:, kv, kc * 128:(kc + 1) * 128], qt[:, 0:T], True, True,
                    [tKT[kv][ktile], tqt], tp_)
                return ps_, tp_
            nxt = smm(0)
            pts = []

            pairs = []
            dq = []
            dcount = [0]
            ND = (NKC // 2 + 1) // 2

            def issue_d():
                while dq:
                    buf, tk = dq.pop(0)
                    dcount[0] += 1
                    mm1(pd[:, 0:T], ONES_1, buf[:, 0:T], dcount[0] == 1, dcount[0] == ND, [tk, tCONST], tpd)
                    brelease(buf)

            def od(kc):
                pt, tpt = pts[kc]
                issue_d()
                mm1(po[:, 0:T], VT[:, kc, kv * 128:(kv + 1) * 128], pt[:, 0:T], kc == 0, kc == NKC - 1,
                    [tVT[kc], tpt], tpo)
                if kc % 2 == 1:
                    ppt, tppt = pts[kc - 1]
                    p2, tp2 = bpool(hold=True)
                    tt("dve", p2[:, 0:T], ppt[:, 0:T], pt[:, 0:T], ALU.add, [tppt, tpt], [tp2])
                    brelease(ppt)
                    brelease(pt)
                    pairs.append((p2, tp2))
                    if len(pairs) == 2:
                        (a0, ta0), (a1, ta1) = pairs
                        p4, tp4 = bpool(hold=True)
                        tt("dve", p4[:, 0:T], a0[:, 0:T], a1[:, 0:T], ALU.add, [ta0, ta1], [tp4])
                        brelease(a0)
                        brelease(a1)
                        pairs.clear()
                        dq.append((p4, tp4))
                    elif kc == NKC - 1:
                        dq.append(pairs.pop())
                if kc == NKC - 1:
                    issue_d()
            for kc in range(NKC):
                P.label = "l1.attn"
                cur = nxt
                if kc + 1 < NKC:
                    nxt = smm(kc + 1)
                pt, tpt = bpool(hold=True)
                act(pt[:, 0:T], cur[0][:, 0:T], AF.Exp, [cur[1]], [tpt])
                prelease(cur[0])
                pts.append((pt, tpt))
                if kc >= 1:
                    od(kc - 1)
                if kc % 2 == 1:
                    yield
            P.label = "l1.attn"
            od(NKC - 1)
            P.label = "l1.attn"
            rd, trd = fpool()
            osb, tosb = fpool()
            act(rd[:, 0:T], pd[:, 0:T], AF.Ln, [tpd], [trd])
            op("dve", lambda h: h.tensor_copy(out=osb[:, 0:T], in_=po[:, 0:T]), [tpo], [tosb])
            act(rd[:, 0:T], rd[:, 0:T], AF.Exp, [trd], [trd], scale=-1.0)
            tt("pool", at[:, hd_, 0:T], osb[:, 0:T], rd[:, 0:T], ALU.mult, [tosb, trd], [tat[hd_]])
            brelease(qt)

        qres = [[] for _ in range(4)]
        for _ in qproj(0):
            yield
        for pz in range(4):
            nq = qproj(pz + 1) if pz + 1 < 4 else None
            for o2 in range(2):
                qt, tqt = qres[pz][o2]
                stepi = 0
                for _ in attend(pz * 2 + o2, qt, tqt):
                    stepi += 1
                    if nq is not None and stepi % 2 == 0:
                        try:
                            next(nq)
                        except StopIteration:
                            nq = None
                    yield
            if nq is not None:
                for _ in nq:
                    yield
        orelease(hb)
        frelease(rope[0][0])
        frelease(rope[1][0])
        for pz in range(4):
            P.label = "l1.out"
            wo_, two = wload(1, W_O + pz)
            for o2 in range(2):
                P.label = "l1.out"
                oc = pz * 2 + o2
                po, tpo = psum()
                mm(po[:, 0:T], [(wo_[:, (o2 * 8 + kc) * 128:(o2 * 8 + kc + 1) * 128], at[:, kc, 0:T])
                                for kc in range(KC)], [two] + tat, tpo)
                xa = XA[:, oc, t0:t0 + T]
                stt(xa, po[:, 0:T], gate_ap(1, 0, oc, j), xa, ALU.mult, ALU.add, [tpo, tXA[ti][oc], tCONST],
                    [tXA[ti][oc]])
                yield
        orelease(at)

    def load_tile(b, ti, t0, T, is_ctx):
        src = ctxT[b] if is_ctx else xT[b][:, :, t0:t0 + T]
        dma("sp", XA[:, :, t0:t0 + T], src, [], tXA[ti], f"xin{ti}")
        for c in range(KC):
            e = "pool" if c % 2 == 0 else "dve"
            ts(e, XA[:, c, t0:t0 + T], XA[:, c, t0:t0 + T], ALPHA, None, ALU.mult, None, [tXA[ti][c]], [tXA[ti][c]])

    def store_tile(b, ti, t0, T):
        dma("sp", outT[b][:, :, t0:t0 + T], XA[:, :, t0:t0 + T], tXA[ti], [], f"out{ti}")

    def chain(*gens):
        for g in gens:
            if g is not None:
                for _ in g:
                    yield

    def pipeline(n, front, ln1, back, ln2, extra=None):
        res = [dict() for _ in range(n)]
        run_par(front(0, res[0]), extra)
        if n > 1:
            run_par(ln1(0, res[0]), front(1, res[1]))
        else:
            run_par(ln1(0, res[0]))
        for k in range(n):
            g2 = ln2(k - 1, res[k - 1]) if k >= 1 else None
            g1 = ln1(k + 1, res[k + 1]) if k + 1 < n else None
            run_par(back(k, res[k]), chain(g2, g1))
            if k + 2 < n:
                run_par(front(k + 2, res[k + 2]))
        run_par(ln2(n - 1, res[n - 1]))

    for (ti, t0, T, is_ctx) in TILES:
        load_tile(0, ti, t0, T, is_ctx)
    ada1 = ada_gen(1)
    for b in range(nb):
        P.label = "l0.pre"
        ps_set["banks"] = [0, 1, 2, 3, 4, 5]
        dma("sp", XEB.rearrange("p a b -> p (a b)"), xeD[b], [], [tXEB], "misc")
        for c in range(KC):
            ts("dve", HE[:, c, :], XEB[:, c, :], SCHX[:, c, b:b + 1], shift_ap(0, 0, c, b), ALU.mult, ALU.add,
               [tXEB, tCONST], [tHE])

        def l0_front(k, res, b=b):
            (ti, t0, T, is_ctx) = TILES[k]
            hm = make_h(0, ti, t0, T, 4 if is_ctx else b)
            return l0_mixer(b, ti, t0, T, is_ctx, hm)

        def l0_ln1(k, res, b=b):
            (ti, t0, T, is_ctx) = TILES[k]
            return layer_norm(0, 0, ti, t0, T, 4 if is_ctx else b, res)

        def l0_back(k, res, b=b):
            (ti, t0, T, is_ctx) = TILES[k]
            return ffn(0, ti, t0, T, 4 if is_ctx else b, res)

        def l0_ln2(k, res, b=b):
            (ti, t0, T, is_ctx) = TILES[k]
            return layer_norm(0, 1, ti, t0, T, 4 if is_ctx else b, res)

        if b == 0:
            deferred.append(late_casts)
        pipeline(5, l0_front, l0_ln1, l0_back, l0_ln2, extra=(ada1 if b == 0 else None))
        if b == 0:
            for _ in ada1:
                pass
        if stop_after_l0:
            for (ti, t0, T, is_ctx) in TILES[:4]:
                store_tile(b, ti, t0, T)
            continue
        run_par(l1_kv_tile(b, *TILES[0]), l1_kv_tile(b, *TILES[1]))
        run_par(l1_kv_tile(b, *TILES[2]), l1_kv_tile(b, *TILES[3]))
        run_par(l1_kv_tile(b, *TILES[4]))
        ps_set["banks"] = [0, 1, 2, 3]
        if b + 1 < nb:
            deferred.append((lambda b=b: load_tile(b + 1, 4, 2048, 256, True)))

        def l1_front(k, res, b=b):
            (ti, t0, T, is_ctx) = TILES[k]
            hm = make_h(1, ti, t0, T, b)
            return l1_att(b, ti, t0, T, hm)

        def l1_ln1(k, res, b=b):
            (ti, t0, T, is_ctx) = TILES[k]
            return layer_norm(1, 0, ti, t0, T, b, res)

        def l1_back(k, res, b=b):
            (ti, t0, T, is_ctx) = TILES[k]
            return ffn(1, ti, t0, T, b, res)

        def l1_ln2(k, res, b=b):
            (ti, t0, T, is_ctx) = TILES[k]

            def g():
                for _ in layer_norm(1, 1, ti, t0, T, b, res):
                    yield

                def fin(b=b, ti=ti, t0=t0, T=T):
                    store_tile(b, ti, t0, T)
                    if b + 1 < nb:
                        load_tile(b + 1, ti, t0, T, False)
                deferred.append(fin)
            return g()

        pipeline(4, l1_front, l1_ln1, l1_back, l1_ln2)
    flush_deferred()
    import os
    if os.environ.get("KDBG_LABELS"):
        P.dbg = []
    P.emit()
    if os.environ.get("KDBG_LABELS"):
        import json
        json.dump(P.dbg, open(os.environ["KDBG_LABELS"], "w"))
    return P


_CACHE = {}


def kernel(**inputs):
    nb = 4
    if "P" not in _CACHE:
        _CACHE["P"] = build(nb)
    P = _CACHE["P"]
    shared = prepare_shared(inputs)
    in_maps = []
    for core in range(NCORES):
        m = dict(shared)
        m.update(prepare_core(inputs, core, nb))
        in_maps.append(m)
    res = run_bass_kernel_spmd(P.nc, in_maps, core_ids=list(range(NCORES)))
    outs = []
    for core in range(NCORES):
        o = np.asarray(res.results[core]["outT"])
        outs.append(o.transpose(0, 3, 2, 1).reshape(nb, S, D))
    return np.ascontiguousarray(np.concatenate(outs, axis=0), dtype=np.float32)
```

```python
import numpy as np
import concourse.bass as bass
import concourse.mybir as mybir
from concourse.bass_utils import run_bass_kernel_spmd

F32 = mybir.dt.float32
BF16 = mybir.dt.bfloat16
AF = mybir.ActivationFunctionType
ALU = mybir.AluOpType

NCORES = 8
D = 1024
KC = 8
S = 2048
CTX = 256
NTOK = S + CTX
FF = 2816
FC = 22
NH = 8
HD = 128
ALPHA = float((2.0 * 2) ** 0.25)
LN_EPS = 1e-5
RMS_EPS = 1e-6
GRID_W = 64
NSLOT = 4
SLOTW = 2816
NFP = 11
NBP = 12
FPW = 516


class Tok:
    __slots__ = ("w", "r", "const")

    def __init__(self, const=False):
        self.w = {}
        self.r = {}
        self.const = const


class Prog:
    def __init__(self):
        self.nc = bass.Bass("TRN2", target_bir_lowering=False)
        nc = self.nc
        self.eng = {"pe": nc.tensor, "act": nc.scalar, "dve": nc.vector, "pool": nc.gpsimd, "sp": nc.sync}
        self.ops = []
        self.last_dma = {}
        self.label = "setup"
        self.pe_labels = []
        self.dbg = None

    def op(self, eng, fn, reads=(), writes=(), dma=None, nmm=0):
        deps = {}
        if nmm:
            self.pe_labels.extend([self.label] * nmm)

        def add(d):
            for k, i in d.items():
                if deps.get(k, -1) < i:
                    deps[k] = i

        for t in reads:
            add(t.w)
        for t in writes:
            add(t.w)
            add(t.r)
        i = len(self.ops)
        key = dma if dma is not None else eng
        if dma is not None:
            if key in self.last_dma:
                add({key: self.last_dma[key]})
            self.last_dma[key] = i
        self.ops.append([eng, fn, deps, key, dma is not None, None, False, self.label])
        for t in reads:
            if not t.const:
                if t.r.get(key, -1) < i:
                    t.r[key] = i
        for t in writes:
            t.w = {key: i}
            t.r = {}
        return i

    def emit(self):
        nc = self.nc
        ops = self.ops
        needed = set()
        for o in ops:
            needed.update(o[2].values())
        keys = []
        for o in ops:
            if o[3] not in keys:
                keys.append(o[3])
        import contextlib
        with contextlib.ExitStack() as es:
            sems = {k: es.enter_context(nc.semaphore("s_" + k)) for k in keys}
            cnt = {k: 0 for k in keys}
            for i, o in enumerate(ops):
                if o[4]:
                    cnt[o[3]] += 16
                    o[5] = cnt[o[3]]
                elif i in needed:
                    cnt[o[3]] += 1
                    o[5] = cnt[o[3]]
                    o[6] = True
            seen = {e: {} for e in self.eng}
            nwait = 0
            for i, o in enumerate(ops):
                e = o[0]
                h = self.eng[e]
                for k, d in o[2].items():
                    p = ops[d]
                    if e == "pe" and p[0] == "pe" and not p[4]:
                        continue
                    v = p[5]
                    if seen[e].get(k, 0) < v:
                        h.wait_ge(sems[k], v)
                        seen[e][k] = v
                        nwait += 1
                ins = o[1](h)
                if self.dbg is not None:
                    try:
                        self.dbg.append((e, int(ins.ins.name.split("-")[1]), o[7]))
                    except Exception:
                        pass
                if o[4]:
                    ins.then_inc(sems[o[3]], 16)
                elif o[6]:
                    ins.then_inc(sems[e], 1)
            for k in keys:
                if cnt[k] > 0 and seen["sp"].get(k, 0) < cnt[k]:
                    nc.sync.wait_ge(sems[k], cnt[k])
            self.nwait = nwait
        return nc


def _piece_lhsT(W, ocs):
    K, N = W.shape
    kc, no = K // 128, N // 128
    A = W.reshape(kc, 128, no, 128).transpose(2, 1, 0, 3)
    A = A.reshape(no // ocs, ocs, 128, kc, 128).transpose(0, 2, 1, 3, 4)
    return np.ascontiguousarray(A.reshape(no // ocs, 128, ocs * kc * 128))


def _piece_moving(W, kcs):
    K, N = W.shape
    kc = K // 128
    A = W.reshape(kc, 128, N).transpose(1, 0, 2)
    A = A.reshape(128, kc // kcs, kcs * N).transpose(1, 0, 2)
    return np.ascontiguousarray(A)


def _vec_pc(v):
    v = np.asarray(v)
    lead = v.shape[:-1]
    n = v.shape[-1] // 128
    A = v.reshape(lead + (n, 128))
    A = np.moveaxis(A, -1, 0)
    return np.ascontiguousarray(A)


W_MI = 0
W_MO = 10
W_FI0 = 14
W_Q = 36
W_K = 40
W_V = 41
W_O = 42
W_FI1 = 46
NW1 = 68
NW2 = 16


def _rope_tables():
    f32 = np.float32
    t = np.arange(S)
    row = (t // GRID_W).astype(f32)
    col = (t % GRID_W).astype(f32)
    nf = HD // 4
    inv = (np.float32(10000.0) ** (-(np.arange(nf, dtype=f32) / f32(nf)))).astype(f32)
    ang_r = (row[:, None] * inv[None, :]).astype(f32)
    ang_c = (col[:, None] * inv[None, :]).astype(f32)
    cr, sr = np.cos(ang_r).astype(f32), np.sin(ang_r).astype(f32)
    cc, sc = np.cos(ang_c).astype(f32), np.sin(ang_c).astype(f32)
    COS = np.concatenate([cr, cr, cc, cc], axis=1).T
    SIN = np.concatenate([-sr, sr, -sc, sc], axis=1).T
    perm = np.arange(128)
    perm = np.where((perm // 32) % 2 == 0, perm + 32, perm - 32)
    PM = np.zeros((128, 128), f32)
    PM[perm, np.arange(128)] = 1.0
    return np.ascontiguousarray(COS, dtype=f32), np.ascontiguousarray(SIN, dtype=f32), PM


def prepare_shared(inp):
    f = lambda k: np.asarray(inp[k], dtype=np.float32)
    mix_in, mix_out = f("mix_w_in")[0], f("mix_w_out")[0]
    fin, fout = f("ffn_w_in"), f("ffn_w_out")
    qkv, wo = f("attn_w_qkv")[0], f("attn_w_out")[0]
    w1 = np.empty((NW1, 128, 2048), np.float32)
    w1[W_MI + 0:W_MI + 2] = _piece_lhsT(mix_in[:, 512:1024], 2)
    w1[W_MI + 2:W_MI + 4] = _piece_lhsT(mix_in[:, 1024:1536], 2)
    w1[W_MI + 4:W_MI + 6] = _piece_lhsT(mix_in[:, 0:512], 2)
    w1[W_MI + 6:W_MI + 8] = _piece_lhsT(mix_in[:, 1536:2048], 2)
    w1[W_MI + 8:W_MI + 10] = _piece_moving(mix_in[:, 2048:2560], 4)
    w1[W_MO:W_MO + 4] = _piece_lhsT(mix_out, 2)
    for l, base in ((0, W_FI0), (1, W_FI1)):
        g = _piece_lhsT(fin[l][:, :FF], 1)
        u = _piece_lhsT(fin[l][:, FF:], 1)
        w1[base:base + FC] = np.concatenate([g, u], axis=2)
    w1[W_Q:W_Q + 4] = _piece_lhsT(qkv[:, 0:1024], 2)
    w1[W_K:W_K + 1] = _piece_lhsT(qkv[:, 1024:1280], 2)
    w1[W_V:W_V + 1] = _piece_moving(qkv[:, 1280:1536], 8)
    w1[W_O:W_O + 4] = _piece_lhsT(wo, 2)
    w2 = np.empty((NW2, 128, SLOTW), np.float32)
    w2[0:8] = _piece_lhsT(fout[0], 1)
    w2[8:16] = _piece_lhsT(fout[1], 1)
    ada = f("ada_w")
    A = ada.reshape(2, 2, 4, 128, 48, 128)
    A = A.transpose(0, 4, 1, 3, 2, 5)
    adaw = np.ascontiguousarray(A.reshape(2 * 48 * 2, 128, 512))
    adab = _vec_pc(f("ada_b")).reshape(128, 96)
    lng = _vec_pc(f("ln_g")).reshape(128, 32)
    lnb = _vec_pc(f("ln_b")).reshape(128, 32)
    cw = np.ascontiguousarray(_vec_pc(f("conv_w")[0]).transpose(0, 2, 1)).reshape(128, 12)
    wsT = np.ascontiguousarray(f("sgu_w")[0].transpose(2, 0, 1)).reshape(128, 1024)
    sb = f("sgu_b")[0]
    bsr = np.ascontiguousarray(
        np.repeat(sb.reshape(4, 2, 1, 128), 64, axis=2).reshape(4, 128, 128).transpose(1, 0, 2)).reshape(128, 512)
    gam = _vec_pc(f("sgu_ln_g")[0]).reshape(128, 4)
    betb = np.ascontiguousarray(np.broadcast_to(f("sgu_ln_b")[0][None, :], (128, 512)))
    qg = f("q_norm_g")[0].reshape(128, 1)
    kg = f("k_norm_g")[0].reshape(128, 1)
    COS, SIN, PM = _rope_tables()
    small = np.concatenate([adab, lng, lnb, cw, gam, qg, kg], axis=1)
    return dict(w1=w1, w2=w2, adaw=adaw, small=np.ascontiguousarray(small), wsT=wsT, bsr=bsr, betb=betb,
                cosT=COS, sinT=SIN, pm=PM)


SM_ADAB, SM_LNG, SM_LNB, SM_CW, SM_GAM, SM_QG, SM_KG, SM_W = 0, 96, 128, 160, 172, 176, 177, 178


def prepare_core(inp, core, nb):
    x = np.asarray(inp["x"], dtype=np.float32)[core * nb:(core + 1) * nb]
    ctx = np.asarray(inp["ctx"], dtype=np.float32)[core * nb:(core + 1) * nb]
    c = np.asarray(inp["c"], dtype=np.float32)[core * nb:(core + 1) * nb]
    cctx = np.asarray(inp["c_ctx"], dtype=np.float32)
    xT = np.ascontiguousarray(x.reshape(nb, S, KC, 128).transpose(0, 3, 2, 1))
    ctxT = np.ascontiguousarray(ctx.reshape(nb, CTX, KC, 128).transpose(0, 3, 2, 1))
    cc = np.zeros((5, D), np.float32)
    cc[:nb] = c
    cc[4] = cctx
    ccT = np.ascontiguousarray(cc.reshape(5, KC, 128).transpose(2, 1, 0))
    xe = np.ascontiguousarray(xT[:, :, :, [511, 512, 1023, 1024, 1535, 1536]]).reshape(nb, 128, KC * 6)
    return dict(xT=xT, ctxT=ctxT, cc=ccT.reshape(128, 40), xe=xe)


def build(nb=4, stop_after_l0=False):
    P = Prog()
    nc = P.nc
    op = P.op

    def dram(name, shape, dt, kind):
        return nc.dram_tensor(name, list(shape), dt, kind=kind).ap()

    xT = dram("xT", [nb, 128, KC, S], F32, "ExternalInput")
    ctxT = dram("ctxT", [nb, 128, KC, CTX], F32, "ExternalInput")
    ccD = dram("cc", [128, 40], F32, "ExternalInput")
    xeD = dram("xe", [nb, 128, KC * 6], F32, "ExternalInput")
    w1D = dram("w1", [NW1, 128, 2048], F32, "ExternalInput")
    w2D = dram("w2", [NW2, 128, SLOTW], F32, "ExternalInput")
    adawD = dram("adaw", [192, 128, 512], F32, "ExternalInput")
    smallD = dram("small", [128, SM_W], F32, "ExternalInput")
    wsTD = dram("wsT", [128, 1024], F32, "ExternalInput")
    bsrD = dram("bsr", [128, 512], F32, "ExternalInput")
    betbD = dram("betb", [128, 512], F32, "ExternalInput")
    cosD = dram("cosT", [128, S], F32, "ExternalInput")
    sinD = dram("sinT", [128, S], F32, "ExternalInput")
    pmD = dram("pm", [128, 128], F32, "ExternalInput")
    outT = dram("outT", [nb, 128, KC, S], F32, "ExternalOutput")
    w1B = dram("w1b", [NW1, 128, 2048], BF16, "Internal")
    w2B = dram("w2b", [NW2, 128, SLOTW], BF16, "Internal")

    def sb(name, shape, dt):
        return nc.alloc_sbuf_tensor(name, list(shape), dt).ap()

    XA = sb("XA", [128, KC, NTOK], F32)
    OP8 = [sb(f"OP8_{i}", [128, KC, 512], BF16) for i in range(3)]
    WR = [sb(f"WR{i}", [128, SLOTW], BF16) for i in range(NSLOT)]
    HID = sb("HID", [128, FC, 512], BF16)
    FPB = [sb(f"FP{i}", [128, FPW], F32) for i in range(NFP)]
    BPB = [sb(f"BP{i}", [128, 512], BF16) for i in range(NBP)]
    ZH = sb("ZH", [128, 4, 6], F32)
    GH = sb("GH", [128, 4, 6], F32)
    HE = sb("HE", [128, KC, 6], BF16)
    XEB = sb("XEB", [128, KC, 6], F32)
    KT = sb("KT", [128, 2, NTOK], BF16)
    RG = sb("RG", [128, (NTOK // 128) * 256], BF16)
    VT = RG.rearrange("p (a b) -> p a b", b=256)
    Z = RG[:, 0:4 * 514 * 2].bitcast(F32).rearrange("p (a b) -> p a b", b=514)
    SMALL = sb("SMALL", [128, SM_W], F32)
    CC = sb("CC", [128, KC, 5], F32)
    SIL = sb("SIL", [128, KC, 5], F32)
    MOD = sb("MOD", [128, 96, 5], F32)
    SCH = sb("SCH", [128, 4, KC, 5], F32)
    GA = sb("GA", [128, 32], F32)
    BA = sb("BA", [128, 32], F32)
    QG = sb("QG", [128, 1], F32)
    WST = sb("WST", [128, 8, 128], BF16)
    WSTF = FPB
    BS2 = sb("BS2", [128, 4, 128], F32)
    ONES_LN = sb("ONES_LN", [128, 128], BF16)
    ONES_HD = sb("ONES_HD", [128, 128], BF16)
    ONES_1 = sb("ONES_1", [128, 128], BF16)
    PMS = sb("PMS", [128, 128], F32)
    PMSB = sb("PMSB", [128, 128], BF16)
    STAT = [sb(f"STAT{i}", [128, 16], F32) for i in range(6)]
    PS = [nc.alloc_psum_tensor(f"PS{i}", [128, 512], F32).ap() for i in range(8)]

    tXA = [[Tok() for _ in range(KC)] for _ in range(5)]
    tOP8 = [[Tok() for _ in range(KC)] for _ in range(3)]
    tWR = [Tok() for _ in range(NSLOT)]
    tHID = [Tok() for _ in range(FC)]
    tFP = [Tok() for _ in range(NFP)]
    tBP = [Tok() for _ in range(NBP)]
    tZ = [Tok() for _ in range(4)]
    tZH, tGH, tHE, tXEB = Tok(), Tok(), Tok(), Tok()
    tKT = [[Tok() for _ in range(5)] for _ in range(2)]
    tVT = [Tok() for _ in range(NTOK // 128)]
    tSTAT = [Tok() for _ in range(6)]
    tPS = [Tok() for _ in range(8)]
    tCONST = Tok(const=True)
    tW1 = [Tok(const=True) for _ in range(NW1)]
    tW2 = [Tok(const=True) for _ in range(NW2)]

    st = dict(wr=0, fp=0, bp=0, ps=0, stat=0, op8=0, ndma=0, ncast=0)

    held = {"fp": set(), "bp": set()}

    def _pool(kind, n, hold):
        i = st[kind]
        k = 0
        while i in held[kind]:
            i = (i + 1) % n
            k += 1
            assert k <= n, "pool exhausted " + kind
        st[kind] = (i + 1) % n
        if hold:
            held[kind].add(i)
        return i

    def fpool(hold=False):
        i = _pool("fp", NFP, hold)
        return FPB[i], tFP[i]

    def bpool(hold=False):
        i = _pool("bp", NBP, hold)
        return BPB[i], tBP[i]

    def frelease(buf):
        held["fp"].discard([k for k in range(NFP) if FPB[k] is buf][0])

    def brelease(buf):
        held["bp"].discard([k for k in range(NBP) if BPB[k] is buf][0])

    def spool():
        i = st["stat"]
        st["stat"] = (i + 1) % 6
        return STAT[i], tSTAT[i]

    ps_set = {"banks": [0, 1, 2, 3, 4, 5]}
    ps_held = set()

    def psum(hold=False):
        banks = ps_set["banks"]
        k = 0
        while True:
            i = banks[st["ps"] % len(banks)]
            st["ps"] += 1
            if i not in ps_held:
                break
            k += 1
            assert k <= len(banks), "psum exhausted"
        if hold:
            ps_held.add(i)
        return PS[i], tPS[i]

    def prelease(ps_):
        ps_held.discard([k for k in range(8) if PS[k] is ps_][0])

    op8_held = set()

    def op8():
        i = st["op8"]
        k = 0
        while i in op8_held:
            i = (i + 1) % 3
            k += 1
            assert k <= 3, "op8 exhausted"
        st["op8"] = (i + 1) % 3
        op8_held.add(i)
        return OP8[i], tOP8[i]

    def orelease(buf):
        op8_held.discard([k for k in range(3) if OP8[k] is buf][0])

    def dma(eng, out, in_, reads, writes, stream):
        if stream == "misc":
            k = st["ndma"]
            st["ndma"] = k + 1
            stream = stream + str(k % 4)
        elif stream == "cast":
            k = st["ncast"]
            st["ncast"] = k + 1
            stream = stream + str(k % 12)
        return op(eng, lambda h: h.dma_start(out=out, in_=in_), reads, writes, dma=stream)

    def wload(which, idx):
        s = st["wr"]
        st["wr"] = (s + 1) % NSLOT
        if which == 1:
            src, tk, n = w1B[idx], tW1[idx], 2048
        else:
            src, tk, n = w2B[idx], tW2[idx], SLOTW
        dma("sp", WR[s][:, 0:n], src, [tk], [tWR[s]], f"w{s}")
        return WR[s], tWR[s]

    def mm(out, pairs, reads, wtok):
        def fn(h):
            n = len(pairs)
            ins = None
            for i, (l, r) in enumerate(pairs):
                ins = h.matmul(out, lhsT=l, rhs=r, start=(i == 0), stop=(i == n - 1))
            return ins
        return op("pe", fn, reads, [wtok], nmm=len(pairs))

    def mm1(out, l, r, start, stop, reads, wtok):
        return op("pe", lambda h: h.matmul(out, lhsT=l, rhs=r, start=start, stop=stop), reads, [wtok], nmm=1)

    def tt(e, out, a, b, o, reads, writes):
        return op(e, lambda h: h.tensor_tensor(out=out, in0=a, in1=b, op=o), reads, writes)

    def ts(e, out, a, s1, s2, o1, o2, reads, writes):
        if s2 is None:
            s2 = 1.0 if o1 == ALU.add else 0.0
            o2 = ALU.mult if o1 == ALU.add else ALU.add
        return op(e, lambda h: h.tensor_scalar(out=out, in0=a, scalar1=s1, scalar2=s2, op0=o1, op1=o2), reads, writes)

    def stt(out, a, s, b, o1, o2, reads, writes):
        return op("dve", lambda h: h.scalar_tensor_tensor(out=out, in0=a, scalar=s, in1=b, op0=o1, op1=o2),
                  reads, writes)

    def act(out, in_, func, reads, writes, bias=None, scale=None):
        kw = {}
        if bias is not None:
            kw["bias"] = bias
        if scale is not None:
            kw["scale"] = scale
        return op("act", lambda h: h.activation(out=out, in_=in_, func=func, **kw), reads, writes)

    order1 = list(range(W_MI, W_MI + 10)) + list(range(W_MO, W_MO + 4)) + list(range(W_FI0, W_FI0 + FC))
    order1b = list(range(W_Q, NW1))

    def cast1(lo, hi):
        dma("pool", w1B[lo:hi], w1D[lo:hi], [], [tW1[i] for i in range(lo, hi)], "cast")

    def cast2(lo, hi):
        dma("pool", w2B[lo:hi], w2D[lo:hi], [], [tW2[i] for i in range(lo, hi)], "cast")

    dma("sp", SMALL, smallD, [], [tCONST], "misc")
    dma("sp", CC.rearrange("p a b -> p (a b)"), ccD, [], [tCONST], "misc")
    dma("sp", PMS, pmD, [], [tCONST], "misc")
    dma("sp", BS2.rearrange("p a b -> p (a b)"), bsrD, [], [tCONST], "misc")
    cast1(8, 12)
    cast1(0, 4)
    cast1(4, 8)
    for lo in range(12, 36, 4):
        cast1(lo, lo + 4)
    cast2(0, 4)
    cast2(4, 8)

    def late_casts():
        for lo in range(36, NW1, 4):
            cast1(lo, lo + 4)
        cast2(8, 12)
        cast2(12, 16)

    op("dve", lambda h: h.memset(ONES_LN, 1.0 / D), [], [tCONST])
    op("dve", lambda h: h.memset(ONES_HD, 1.0 / HD), [], [tCONST])
    op("dve", lambda h: h.memset(ONES_1, 1.0), [], [tCONST])
    act(SIL, CC, AF.Silu, [tCONST], [tCONST])
    op("dve", lambda h: h.tensor_copy(out=PMSB, in_=PMS), [tCONST], [tCONST])
    ts("dve", GA[:, 0:24], SMALL[:, SM_LNG:SM_LNG + 24], ALPHA, None, ALU.mult, None, [tCONST], [tCONST])
    ts("dve", GA[:, 24:32], SMALL[:, SM_LNG + 24:SM_LNG + 32], 1.0, None, ALU.mult, None, [tCONST], [tCONST])
    ts("dve", BA[:, 0:24], SMALL[:, SM_LNB:SM_LNB + 24], ALPHA, None, ALU.mult, None, [tCONST], [tCONST])
    ts("dve", BA[:, 24:32], SMALL[:, SM_LNB + 24:SM_LNB + 32], 1.0, None, ALU.mult, None, [tCONST], [tCONST])
    ts("dve", QG, SMALL[:, SM_QG:SM_QG + 1], float(HD ** -0.5), None, ALU.mult, None, [tCONST], [tCONST])

    wf = []
    for hlf in range(2):
        b_, t_ = fpool(hold=True)
        dma("sp", b_[:, 0:512], wsTD[:, hlf * 512:(hlf + 1) * 512], [], [t_], "misc")
        wf.append((b_, t_))
        op("dve", (lambda b_=b_, hlf=hlf: lambda h: h.tensor_copy(
            out=WST[:, hlf * 4:(hlf + 1) * 4, :].rearrange("p a b -> p (a b)"), in_=b_[:, 0:512]))(),
           [t_], [tCONST])
    bb, tbb = fpool(hold=True)
    dma("sp", bb[:, 0:512], betbD, [], [tbb], "misc")
    for jc in range(4):
        ps_, tp_ = psum()
        for hh in range(2):
            g = 2 * jc + hh
            wsrc, twsrc = wf[g // 4]
            mm1(ps_[64 * hh:64 * hh + 64, 0:128], bb[:, jc * 128 + 64 * hh:jc * 128 + 64 * hh + 64],
                wsrc[:, (g % 4) * 128:(g % 4) * 128 + 128], True, True, [tbb, twsrc], tp_)
        tt("dve", BS2[:, jc, :], ps_[:, 0:128], BS2[:, jc, :], ALU.add, [tp_, tCONST], [tCONST])
    frelease(bb)
    frelease(wf[0][0])
    frelease(wf[1][0])

    SCHX = sb("SCHX", [128, KC, 5], F32)

    def ada_gen(l):
        for oc in range(48):
            P.label = "setup"
            bufs = []
            for hlf in range(2):
                b_, t_ = fpool()
                dma("sp", b_[:, 0:512], adawD[(l * 48 + oc) * 2 + hlf], [], [t_], "misc")
                bufs.append((b_, t_))
            ps_, tp_ = psum()
            pairs = []
            for kc in range(KC):
                b_, t_ = bufs[kc // 4]
                pairs.append((b_[:, (kc % 4) * 128:(kc % 4) * 128 + 128], SIL[:, kc, :]))
            mm(ps_[:, 0:5], pairs, [bufs[0][1], bufs[1][1], tCONST], tp_)
            ts("dve", MOD[:, l * 48 + oc, :], ps_[:, 0:5],
               SMALL[:, SM_ADAB + l * 48 + oc:SM_ADAB + l * 48 + oc + 1], None, ALU.add, None, [tp_, tCONST],
               [tCONST])
            yield
        P.label = "setup"
        for s_ in range(2):
            base = l * 48 + (3 * s_ + 1) * 8
            ts("dve", SCH[:, l * 2 + s_].rearrange("p a b -> p (a b)"),
               MOD[:, base:base + 8, :].rearrange("p a b -> p (a b)"), 1.0, 1.0 / ALPHA, ALU.add, ALU.mult,
               [tCONST], [tCONST])
        if l == 0:
            ts("dve", SCHX.rearrange("p a b -> p (a b)"), MOD[:, 8:16, :].rearrange("p a b -> p (a b)"),
               1.0, 1.0, ALU.add, ALU.mult, [tCONST], [tCONST])

    for _ in ada_gen(0):
        pass

    def shift_ap(l, s_, c, j):
        return MOD[:, l * 48 + (3 * s_) * 8 + c, j:j + 1]

    def gate_ap(l, s_, c, j):
        return MOD[:, l * 48 + (3 * s_ + 2) * 8 + c, j:j + 1]

    def sch_ap(l, s_, c, j):
        return SCH[:, l * 2 + s_, c, j:j + 1]

    TILES = [(0, 0, 512, False), (1, 512, 512, False), (2, 1024, 512, False), (3, 1536, 512, False),
             (4, 2048, 256, True)]
    deferred = []

    def flush_deferred():
        for f_ in deferred:
            f_()
        deferred.clear()

    def make_h(l, ti, t0, T, j):
        buf, tk = op8()
        for c in range(KC):
            e = "dve" if c % 2 == 0 else "pool"
            ts(e, buf[:, c, 0:T], XA[:, c, t0:t0 + T], sch_ap(l, 0, c, j), shift_ap(l, 0, c, j), ALU.mult, ALU.add,
               [tXA[ti][c], tCONST], [tk[c]])
        return buf, tk

    def run_par(*gens):
        gens = [g for g in gens if g is not None]
        while gens:
            for g in list(gens):
                try:
                    next(g)
                except StopIteration:
                    gens.remove(g)

    def layer_norm(l, s_, ti, t0, T, j, res):
        lab = f"ln{l}{s_}"
        P.label = lab
        mean_ps, tm = PS[6], tPS[6]
        ex2_ps, te = PS[7], tPS[7]
        pend = []

        def stats_mm():
            for (c, rb, trb, sq, tsq) in pend:
                mm1(mean_ps[:, 0:T], ONES_LN, rb[:, 0:T], c == 0, c == KC - 1, [trb, tCONST], tm)
                mm1(ex2_ps[:, 0:T], ONES_LN, sq[:, 0:T], c == 0, c == KC - 1, [tsq, tCONST], te)
                brelease(rb)
                brelease(sq)
            pend.clear()
        for c in range(KC):
            P.label = lab
            if c % 2 == 0:
                stats_mm()
            rb, trb = bpool(hold=True)
            op("pool", (lambda rb=rb, c=c: lambda h: h.tensor_copy(out=rb[:, 0:T], in_=XA[:, c, t0:t0 + T]))(),
               [tXA[ti][c]], [trb])
            sq, tsq = bpool(hold=True)
            e = "pool" if c % 4 != 3 else "dve"
            tt(e, sq[:, 0:T], XA[:, c, t0:t0 + T], XA[:, c, t0:t0 + T], ALU.mult, [tXA[ti][c]], [tsq])
            pend.append((c, rb, trb, sq, tsq))
            if c % 2 == 1:
                yield
        P.label = lab
        stats_mm()
        P.label = lab
        msb, tmsb = fpool(hold=True)
        op("dve", lambda h: h.tensor_copy(out=msb[:, 0:T], in_=mean_ps[:, 0:T]), [tm], [tmsb])
        var, tvar = fpool(hold=True)
        tt("pool", var[:, 0:T], msb[:, 0:T], msb[:, 0:T], ALU.mult, [tmsb], [tvar])
        tt("dve", var[:, 0:T], ex2_ps[:, 0:T], var[:, 0:T], ALU.subtract, [te, tvar], [tvar])
        A_, tA = fpool(hold=True)
        act(A_[:, 0:T], var[:, 0:T], AF.Ln, [tvar], [tA], bias=LN_EPS)
        act(A_[:, 0:T], A_[:, 0:T], AF.Exp, [tA], [tA], scale=-0.5)
        frelease(var)
        B_, tB = fpool(hold=True)
        stt(B_[:, 0:T], msb[:, 0:T], -1.0, A_[:, 0:T], ALU.mult, ALU.mult, [tmsb, tA], [tB])
        hf = None
        if s_ == 0:
            hf = op8()
            res["hf"] = hf
        yield
        gi = (l * 2 + s_) * 8
        for c in range(KC):
            P.label = lab
            xa = XA[:, c, t0:t0 + T]
            tx = tXA[ti][c]
            on_dve = c in (1, 3, 5)
            if on_dve:
                stt(xa, xa, GA[:, gi + c:gi + c + 1], A_[:, 0:T], ALU.mult, ALU.mult, [tx, tA, tCONST], [tx])
                stt(xa, B_[:, 0:T], GA[:, gi + c:gi + c + 1], xa, ALU.mult, ALU.add, [tx, tB, tCONST], [tx])
                ts("dve", xa, xa, BA[:, gi + c:gi + c + 1], None, ALU.add, None, [tx, tCONST], [tx])
            else:
                tt("pool", xa, xa, A_[:, 0:T], ALU.mult, [tx, tA], [tx])
                tt("pool", xa, xa, B_[:, 0:T], ALU.add, [tx, tB], [tx])
                ts("pool", xa, xa, GA[:, gi + c:gi + c + 1], BA[:, gi + c:gi + c + 1], ALU.mult, ALU.add,
                   [tx, tCONST], [tx])
            if s_ == 0:
                e = "dve" if on_dve else "pool"
                ts(e, hf[0][:, c, 0:T], xa, sch_ap(l, 1, c, j), shift_ap(l, 1, c, j), ALU.mult, ALU.add,
                   [tx, tCONST], [hf[1][c]])
            if c % 2 == 1:
                yield
        frelease(msb)
        frelease(A_)
        frelease(B_)

    def ffn(l, ti, t0, T, j, res):
        ps_set["banks"] = [0, 1, 2, 3, 4, 5]
        hb, thb = res["hf"]
        base = W_FI0 if l == 0 else W_FI1
        for jj in range(FC):
            P.label = f"ffn_in{l}"
            w, tw = wload(1, base + jj)
            pg, tpg = psum()
            mm(pg[:, 0:T], [(w[:, kc * 128:(kc + 1) * 128], hb[:, kc, 0:T]) for kc in range(KC)], [tw] + thb, tpg)
            pu, tpu = psum()
            mm(pu[:, 0:T], [(w[:, (8 + kc) * 128:(9 + kc) * 128], hb[:, kc, 0:T]) for kc in range(KC)],
               [tw] + thb, tpu)
            sg, tsg = fpool()
            act(sg[:, 0:T], pg[:, 0:T], AF.Silu, [tpg], [tsg])
            tt("dve", HID[:, jj, 0:T], pu[:, 0:T], sg[:, 0:T], ALU.mult, [tpu, tsg], [tHID[jj]])
            if jj == 10:
                flush_deferred()
            yield
        orelease(hb)
        for oc in range(KC):
            P.label = f"ffn_out{l}"
            w, tw = wload(2, l * 8 + oc)
            po, tpo = psum()
            mm(po[:, 0:T], [(w[:, kc * 128:(kc + 1) * 128], HID[:, kc, 0:T]) for kc in range(FC)], [tw] + tHID, tpo)
            xa = XA[:, oc, t0:t0 + T]
            stt(xa, po[:, 0:T], gate_ap(l, 1, oc, j), xa, ALU.mult, ALU.add, [tpo, tXA[ti][oc], tCONST],
                [tXA[ti][oc]])
            yield

    def l0_mixer(b, ti, t0, T, is_ctx, hm):
        j = 4 if is_ctx else b
        hb, thb = hm
        ntc = T // 128
        do_halo = (ti == 0)
        P.label = "l0.v"
        wv0, twv0 = wload(1, W_MI + 8)
        wv1, twv1 = wload(1, W_MI + 9)
        VN = []
        VG = []
        sst, tst = spool()
        for tc in range(ntc):
            P.label = "l0.v"
            pv, tpv = psum()
            pairs = []
            for kc in range(KC):
                wv = wv0 if kc < 4 else wv1
                pairs.append((hb[:, kc, tc * 128:(tc + 1) * 128], wv[:, (kc % 4) * 512:(kc % 4 + 1) * 512]))
            mm(pv, pairs, [twv0, twv1] + thb, tpv)
            vg, tvg = fpool(hold=True)
            act(vg[:, 0:512], pv, AF.Gelu, [tpv], [tvg])
            s6, ts6 = spool()
            op("dve", (lambda s6=s6, vg=vg: lambda h: h.bn_stats(out=s6[:, 0:6], in_=vg[:, 0:512]))(),
               [tvg], [ts6])
            op("dve", (lambda s6=s6, tc=tc: lambda h: h.bn_aggr(out=sst[:, 2 * tc:2 * tc + 2], in_=s6[:, 0:6]))(),
               [ts6, tst], [tst])
            VG.append((vg, tvg))
            yield
        P.label = "l0.v"
        varv = sst[:, 0:2 * ntc].rearrange("p (a b) -> p a b", b=2)[:, :, 1:2]
        rsv = sst[:, 8:8 + ntc].rearrange("p (a b) -> p a b", b=1)
        act(rsv, varv, AF.Ln, [tst], [tst], bias=LN_EPS)
        act(rsv, rsv, AF.Exp, [tst], [tst], scale=-0.5)
        for tc in range(ntc):
            vg, tvg = VG[tc]
            vn, tvn = bpool(hold=True)
            ts("dve", vn, vg[:, 0:512], sst[:, 2 * tc:2 * tc + 1], sst[:, 8 + tc:9 + tc], ALU.subtract, ALU.mult,
               [tvg, tst], [tvn])
            frelease(vg)
            VN.append((vn, tvn))
        yield
        for h2 in range(2):
            P.label = "l0.mixA"
            wg, twg = wload(1, W_MI + 0 + h2)
            wh, twh = wload(1, W_MI + 2 + h2)
            for o2 in range(2):
                P.label = "l0.mixA"
                oc = h2 * 2 + o2
                pg, tpg = psum()
                mm(pg[:, 0:T], [(wg[:, (o2 * 8 + kc) * 128:(o2 * 8 + kc + 1) * 128], hb[:, kc, 0:T])
                                for kc in range(KC)], [twg] + thb, tpg)
                gt, tgt = fpool()
                act(gt[:, 0:T], pg[:, 0:T], AF.Copy, [tpg], [tgt])
                ph, tph = psum()
                mm(ph[:, 0:T], [(wh[:, (o2 * 8 + kc) * 128:(o2 * 8 + kc + 1) * 128], hb[:, kc, 0:T])
                                for kc in range(KC)], [twh] + thb, tph)
                tt("dve", Z[:, oc, 1:T + 1], ph[:, 0:T], gt[:, 0:T], ALU.mult, [tph, tgt], [tZ[oc]] + tVT)
                if do_halo:
                    pg2, tpg2 = psum()
                    mm(pg2[:, 0:6], [(wg[:, (o2 * 8 + kc) * 128:(o2 * 8 + kc + 1) * 128], HE[:, kc, :])
                                     for kc in range(KC)], [twg, tHE], tpg2)
                    act(GH[:, oc, :], pg2[:, 0:6], AF.Copy, [tpg2], [tGH])
                    ph2, tph2 = psum()
                    mm(ph2[:, 0:6], [(wh[:, (o2 * 8 + kc) * 128:(o2 * 8 + kc + 1) * 128], HE[:, kc, :])
                                     for kc in range(KC)], [twh, tHE], tph2)
                    tt("dve", ZH[:, oc, :], ph2[:, 0:6], GH[:, oc, :], ALU.mult, [tph2, tGH], [tZH])
                yield
        P.label = "l0.mixA"
        if is_ctx or ti == 0:
            op("pool", lambda h: h.memset(Z[:, :, 0:1], 0.0), [], tZ + tVT)
        else:
            op("pool", lambda h: h.tensor_copy(out=Z[:, :, 0:1], in_=ZH[:, :, 2 * (ti - 1):2 * (ti - 1) + 1]),
               [tZH], tZ + tVT)
        if is_ctx or ti == 3:
            op("pool", lambda h: h.memset(Z[:, :, T + 1:T + 2], 0.0), [], tZ + tVT)
        else:
            op("pool", lambda h: h.tensor_copy(out=Z[:, :, T + 1:T + 2], in_=ZH[:, :, 2 * ti + 1:2 * ti + 2]),
               [tZH], tZ + tVT)
        yab, tyab = op8()
        ZC = []
        for oc in range(4):
            zc, tzc = fpool(hold=True)
            cwa = lambda tap, oc=oc: SMALL[:, SM_CW + oc * 3 + tap:SM_CW + oc * 3 + tap + 1]
            ts("pool", zc[:, 0:T], Z[:, oc, 1:T + 1], cwa(1), None, ALU.mult, None, [tZ[oc], tCONST], [tzc])
            stt(zc[:, 0:T], Z[:, oc, 0:T], cwa(0), zc[:, 0:T], ALU.mult, ALU.add, [tZ[oc], tzc, tCONST], [tzc])
            stt(zc[:, 0:T], Z[:, oc, 2:T + 2], cwa(2), zc[:, 0:T], ALU.mult, ALU.add, [tZ[oc], tzc, tCONST], [tzc])
            ZC.append((zc, tzc))
        yield
        for h2 in range(2):
            P.label = "l0.gb"
            wb_, twb = wload(1, W_MI + 4 + h2)
            for o2 in range(2):
                P.label = "l0.gb"
                oc = h2 * 2 + o2
                pb, tpb = psum()
                mm(pb[:, 0:T], [(wb_[:, (o2 * 8 + kc) * 128:(o2 * 8 + kc + 1) * 128], hb[:, kc, 0:T])
                                for kc in range(KC)], [twb] + thb, tpb)
                zc, tzc = ZC[oc]
                tt("dve", yab[:, oc, 0:T], pb[:, 0:T], zc[:, 0:T], ALU.mult, [tpb, tzc], [tyab[oc]])
                frelease(zc)
                yield
        for h2 in range(2):
            P.label = "l0.u"
            wu, twu = wload(1, W_MI + 6 + h2)
            for o2 in range(2):
                P.label = "l0.u"
                jc = h2 * 2 + o2
                pu, tpu = psum()
                mm(pu[:, 0:T], [(wu[:, (o2 * 8 + kc) * 128:(o2 * 8 + kc + 1) * 128], hb[:, kc, 0:T])
                                for kc in range(KC)], [twu] + thb, tpu)
                ub, tub = fpool()
                act(ub[:, 0:T], pu[:, 0:T], AF.Gelu, [tpu], [tub])
                P.label = "l0.sgu"
                ps_, tp_ = psum()

                def fn(h, ps_=ps_, jc=jc):
                    ins = None
                    for tc in range(ntc):
                        for hh in range(2):
                            g = 2 * jc + hh
                            ins = h.matmul(ps_[64 * hh:64 * hh + 64, tc * 128:(tc + 1) * 128],
                                           lhsT=VN[tc][0][:, g * 64:(g + 1) * 64], rhs=WST[:, g, :],
                                           start=True, stop=True)
                    return ins
                op("pe", fn, [v[1] for v in VN] + [tCONST], [tp_], nmm=2 * ntc)
                tmp, ttmp = fpool()
                stt(tmp[:, 0:T].rearrange("p (a b) -> p a b", b=128),
                    ps_[:, 0:T].rearrange("p (a b) -> p a b", b=128),
                    SMALL[:, SM_GAM + jc:SM_GAM + jc + 1], BS2[:, jc:jc + 1, :].broadcast_to([128, ntc, 128]),
                    ALU.mult, ALU.add, [tp_, tCONST], [ttmp])
                tt("dve", yab[:, 4 + jc, 0:T], tmp[:, 0:T], ub[:, 0:T], ALU.mult, [ttmp, tub], [tyab[4 + jc]])
                yield
        for v_ in VN:
            brelease(v_[0])
        for pz in range(4):
            P.label = "l0.mixout"
            wo_, two = wload(1, W_MO + pz)
            for o2 in range(2):
                P.label = "l0.mixout"
                oc = pz * 2 + o2
                po, tpo = psum()
                mm(po[:, 0:T], [(wo_[:, (o2 * 8 + kc) * 128:(o2 * 8 + kc + 1) * 128], yab[:, kc, 0:T])
                                for kc in range(KC)], [two] + tyab, tpo)
                xa = XA[:, oc, t0:t0 + T]
                stt(xa, po[:, 0:T], gate_ap(0, 0, oc, j), xa, ALU.mult, ALU.add, [tpo, tXA[ti][oc], tCONST],
                    [tXA[ti][oc]])
                yield
        orelease(hb)
        orelease(yab)

    def qk_norm_rope(ps_, tp_, T, gvec, rope, dst, tdst, lab):
        P.label = lab
        sq, tsq = bpool(hold=True)
        act(sq[:, 0:T], ps_[:, 0:T], AF.Square, [tp_], [tsq])
        yield
        P.label = lab
        pm_, tpm = psum()
        mm1(pm_[:, 0:T], ONES_HD, sq[:, 0:T], True, True, [tsq, tCONST], tpm)
        brelease(sq)
        A_, tA = fpool()
        act(A_[:, 0:T], pm_[:, 0:T], AF.Ln, [tpm], [tA], bias=RMS_EPS)
        act(A_[:, 0:T], A_[:, 0:T], AF.Exp, [tA], [tA], scale=-0.5)
        if rope is None:
            stt(dst, ps_[:, 0:T], gvec, A_[:, 0:T], ALU.mult, ALU.mult, [tp_, tA, tCONST], tdst)
            prelease(ps_)
            return
        kn, tkn = bpool(hold=True)
        stt(kn[:, 0:T], ps_[:, 0:T], gvec, A_[:, 0:T], ALU.mult, ALU.mult, [tp_, tA, tCONST], [tkn])
        prelease(ps_)
        yield
        P.label = lab
        (cs, tcs), (sn, tsn) = rope
        pp, tpp = psum()
        mm1(pp[:, 0:T], PMSB, kn[:, 0:T], True, True, [tkn, tCONST], tpp)
        t1, tt1 = fpool()
        tt("pool", t1[:, 0:T], kn[:, 0:T], cs[:, 0:T], ALU.mult, [tkn, tcs], [tt1])
        t2, tt2 = fpool()
        tt("dve", t2[:, 0:T], pp[:, 0:T], sn[:, 0:T], ALU.mult, [tpp, tsn], [tt2])
        tt("dve", dst, t1[:, 0:T], t2[:, 0:T], ALU.add, [tt1, tt2], tdst)
        brelease(kn)

    def load_rope(t0, T):
        cs, tcs = fpool(hold=True)
        dma("sp", cs[:, 0:T], cosD[:, t0:t0 + T], [], [tcs], "misc")
        sn, tsn = fpool(hold=True)
        dma("sp", sn[:, 0:T], sinD[:, t0:t0 + T], [], [tsn], "misc")
        return (cs, tcs), (sn, tsn)

    def l1_kv_tile(b, ti, t0, T, is_ctx):
        P.label = "l1.kv"
        j = 4 if is_ctx else b
        hb, thb = make_h(1, ti, t0, T, j)
        rope = None if is_ctx else load_rope(t0, T)
        wk, twk = wload(1, W_K)
        wv, twv = wload(1, W_V)
        chains = []
        for kvh in range(2):
            P.label = "l1.kv"
            pk, tpk = psum(hold=True)
            mm(pk[:, 0:T], [(wk[:, (kvh * 8 + kc) * 128:(kvh * 8 + kc + 1) * 128], hb[:, kc, 0:T])
                            for kc in range(KC)], [twk] + thb, tpk)
            chains.append(qk_norm_rope(pk, tpk, T, SMALL[:, SM_KG:SM_KG + 1], rope, KT[:, kvh, t0:t0 + T],
                                       [tKT[kvh][ti]], "l1.kv"))
        vsteps = []
        for tc in range(T // 128):
            def vstep(tc=tc):
                P.label = "l1.kv"
                pv, tpv = psum()
                mm(pv[:, 0:256], [(hb[:, kc, tc * 128:(tc + 1) * 128], wv[:, kc * 256:(kc + 1) * 256])
                                  for kc in range(KC)], [twv] + thb, tpv)
                g = t0 // 128 + tc
                act(VT[:, g, :], pv[:, 0:256], AF.Copy, [tpv], [tVT[g]] + tZ)
            vsteps.append(vstep)

        def vgen():
            for f_ in vsteps:
                f_()
                yield
        for _ in kv_driver(chains, vgen()):
            yield
        orelease(hb)
        if rope is not None:
            frelease(rope[0][0])
            frelease(rope[1][0])

    def kv_driver(chains, vg):
        gens = list(chains) + [vg]
        while gens:
            for g in list(gens):
                try:
                    next(g)
                except StopIteration:
                    gens.remove(g)
            yield

    def l1_att(b, ti, t0, T, hm):
        j = b
        hb, thb = hm
        ps_set["banks"] = [0, 1, 2, 3]
        rope = load_rope(t0, T)
        at, tat = op8()
        NKC = NTOK // 128

        def qproj(pz):
            P.label = "l1.qproj"
            wq, twq = wload(1, W_Q + pz)
            for o2 in range(2):
                P.label = "l1.qproj"
                pq, tpq = psum(hold=True)
                mm(pq[:, 0:T], [(wq[:, (o2 * 8 + kc) * 128:(o2 * 8 + kc + 1) * 128], hb[:, kc, 0:T])
                                for kc in range(KC)], [twq] + thb, tpq)
                qt, tqt = bpool(hold=True)
                qres[pz].append((qt, tqt))
                for _ in qk_norm_rope(pq, tpq, T, QG[:, 0:1], rope, qt[:, 0:T], [tqt], "l1.qproj"):
                    yield
                yield

        def attend(hd_, qt, tqt):
            P.label = "l1.attn"
            kv = hd_ // 4
            po, tpo = PS[4], tPS[4]
            pd, tpd = PS[5], tPS[5]

            def smm(kc):
                ps_, tp_ = psum(hold=True)
                ktile = min(kc // 4, 4)
                mm1(ps_[:, 0:T], KT[:, kv, kc * 128:(kc + 1) * 128], qt[:, 0:T], True, True,
                    [tKT[kv][ktile], tqt], tp_)
                return ps_, tp_
            nxt = smm(0)
            pts = []

            pairs = []
            dq = []
            dcount = [0]
            ND = (NKC // 2 + 1) // 2

            def issue_d():
                while dq:
                    buf, tk = dq.pop(0)
                    dcount[0] += 1
                    mm1(pd[:, 0:T], ONES_1, buf[:, 0:T], dcount[0] == 1, dcount[0] == ND, [tk, tCONST], tpd)
                    brelease(buf)

            def od(kc):
                pt, tpt = pts[kc]
                issue_d()
                mm1(po[:, 0:T], VT[:, kc, kv * 128:(kv + 1) * 128], pt[:, 0:T], kc == 0, kc == NKC - 1,
                    [tVT[kc], tpt], tpo)
                if kc % 2 == 1:
                    ppt, tppt = pts[kc - 1]
                    p2, tp2 = bpool(hold=True)
                    tt("dve", p2[:, 0:T], ppt[:, 0:T], pt[:, 0:T], ALU.add, [tppt, tpt], [tp2])
                    brelease(ppt)
                    brelease(pt)
                    pairs.append((p2, tp2))
                    if len(pairs) == 2:
                        (a0, ta0), (a1, ta1) = pairs
                        p4, tp4 = bpool(hold=True)
                        tt("dve", p4[:, 0:T], a0[:, 0:T], a1[:, 0:T], ALU.add, [ta0, ta1], [tp4])
                        brelease(a0)
                        brelease(a1)
                        pairs.clear()
                        dq.append((p4, tp4))
                    elif kc == NKC - 1:
                        dq.append(pairs.pop())
                if kc == NKC - 1:
                    issue_d()
            for kc in range(NKC):
                P.label = "l1.attn"
                cur = nxt
                if kc + 1 < NKC:
                    nxt = smm(kc + 1)
                pt, tpt = bpool(hold=True)
                act(pt[:, 0:T], cur[0][:, 0:T], AF.Exp, [cur[1]], [tpt])
                prelease(cur[0])
                pts.append((pt, tpt))
                if kc >= 1:
                    od(kc - 1)
                if kc % 2 == 1:
                    yield
            P.label = "l1.attn"
            od(NKC - 1)
            P.label = "l1.attn"
            rd, trd = fpool()
            osb, tosb = fpool()
            act(rd[:, 0:T], pd[:, 0:T], AF.Ln, [tpd], [trd])
            op("dve", lambda h: h.tensor_copy(out=osb[:, 0:T], in_=po[:, 0:T]), [tpo], [tosb])
            act(rd[:, 0:T], rd[:, 0:T], AF.Exp, [trd], [trd], scale=-1.0)
            tt("pool", at[:, hd_, 0:T], osb[:, 0:T], rd[:, 0:T], ALU.mult, [tosb, trd], [tat[hd_]])
            brelease(qt)

        qres = [[] for _ in range(4)]
        for _ in qproj(0):
            yield
        for pz in range(4):
            nq = qproj(pz + 1) if pz + 1 < 4 else None
            for o2 in range(2):
                qt, tqt = qres[pz][o2]
                stepi = 0
                for _ in attend(pz * 2 + o2, qt, tqt):
                    stepi += 1
                    if nq is not None and stepi % 2 == 0:
                        try:
                            next(nq)
                        except StopIteration:
                            nq = None
                    yield
            if nq is not None:
                for _ in nq:
                    yield
        orelease(hb)
        frelease(rope[0][0])
        frelease(rope[1][0])
        for pz in range(4):
            P.label = "l1.out"
            wo_, two = wload(1, W_O + pz)
            for o2 in range(2):
                P.label = "l1.out"
                oc = pz * 2 + o2
                po, tpo = psum()
                mm(po[:, 0:T], [(wo_[:, (o2 * 8 + kc) * 128:(o2 * 8 + kc + 1) * 128], at[:, kc, 0:T])
                                for kc in range(KC)], [two] + tat, tpo)
                xa = XA[:, oc, t0:t0 + T]
                stt(xa, po[:, 0:T], gate_ap(1, 0, oc, j), xa, ALU.mult, ALU.add, [tpo, tXA[ti][oc], tCONST],
                    [tXA[ti][oc]])
                yield
        orelease(at)

    def load_tile(b, ti, t0, T, is_ctx):
        src = ctxT[b] if is_ctx else xT[b][:, :, t0:t0 + T]
        dma("sp", XA[:, :, t0:t0 + T], src, [], tXA[ti], f"xin{ti}")
        for c in range(KC):
            e = "pool" if c % 2 == 0 else "dve"
            ts(e, XA[:, c, t0:t0 + T], XA[:, c, t0:t0 + T], ALPHA, None, ALU.mult, None, [tXA[ti][c]], [tXA[ti][c]])

    def store_tile(b, ti, t0, T):
        dma("sp", outT[b][:, :, t0:t0 + T], XA[:, :, t0:t0 + T], tXA[ti], [], f"out{ti}")

    def chain(*gens):
        for g in gens:
            if g is not None:
                for _ in g:
                    yield

    def pipeline(n, front, ln1, back, ln2, extra=None):
        res = [dict() for _ in range(n)]
        run_par(front(0, res[0]), extra)
        if n > 1:
            run_par(ln1(0, res[0]), front(1, res[1]))
        else:
            run_par(ln1(0, res[0]))
        for k in range(n):
            g2 = ln2(k - 1, res[k - 1]) if k >= 1 else None
            g1 = ln1(k + 1, res[k + 1]) if k + 1 < n else None
            run_par(back(k, res[k]), chain(g2, g1))
            if k + 2 < n:
                run_par(front(k + 2, res[k + 2]))
        run_par(ln2(n - 1, res[n - 1]))

    for (ti, t0, T, is_ctx) in TILES:
        load_tile(0, ti, t0, T, is_ctx)
    ada1 = ada_gen(1)
    for b in range(nb):
        P.label = "l0.pre"
        ps_set["banks"] = [0, 1, 2, 3, 4, 5]
        dma("sp", XEB.rearrange("p a b -> p (a b)"), xeD[b], [], [tXEB], "misc")
        for c in range(KC):
            ts("dve", HE[:, c, :], XEB[:, c, :], SCHX[:, c, b:b + 1], shift_ap(0, 0, c, b), ALU.mult, ALU.add,
               [tXEB, tCONST], [tHE])

        def l0_front(k, res, b=b):
            (ti, t0, T, is_ctx) = TILES[k]
            hm = make_h(0, ti, t0, T, 4 if is_ctx else b)
            return l0_mixer(b, ti, t0, T, is_ctx, hm)

        def l0_ln1(k, res, b=b):
            (ti, t0, T, is_ctx) = TILES[k]
            return layer_norm(0, 0, ti, t0, T, 4 if is_ctx else b, res)

        def l0_back(k, res, b=b):
            (ti, t0, T, is_ctx) = TILES[k]
            return ffn(0, ti, t0, T, 4 if is_ctx else b, res)

        def l0_ln2(k, res, b=b):
            (ti, t0, T, is_ctx) = TILES[k]
            return layer_norm(0, 1, ti, t0, T, 4 if is_ctx else b, res)

        if b == 0:
            deferred.append(late_casts)
        pipeline(5, l0_front, l0_ln1, l0_back, l0_ln2, extra=(ada1 if b == 0 else None))
        if b == 0:
            for _ in ada1:
                pass
        if stop_after_l0:
            for (ti, t0, T, is_ctx) in TILES[:4]:
                store_tile(b, ti, t0, T)
            continue
        run_par(l1_kv_tile(b, *TILES[0]), l1_kv_tile(b, *TILES[1]))
        run_par(l1_kv_tile(b, *TILES[2]), l1_kv_tile(b, *TILES[3]))
        run_par(l1_kv_tile(b, *TILES[4]))
        ps_set["banks"] = [0, 1, 2, 3]
        if b + 1 < nb:
            deferred.append((lambda b=b: load_tile(b + 1, 4, 2048, 256, True)))

        def l1_front(k, res, b=b):
            (ti, t0, T, is_ctx) = TILES[k]
            hm = make_h(1, ti, t0, T, b)
            return l1_att(b, ti, t0, T, hm)

        def l1_ln1(k, res, b=b):
            (ti, t0, T, is_ctx) = TILES[k]
            return layer_norm(1, 0, ti, t0, T, b, res)

        def l1_back(k, res, b=b):
            (ti, t0, T, is_ctx) = TILES[k]
            return ffn(1, ti, t0, T, b, res)

        def l1_ln2(k, res, b=b):
            (ti, t0, T, is_ctx) = TILES[k]

            def g():
                for _ in layer_norm(1, 1, ti, t0, T, b, res):
                    yield

                def fin(b=b, ti=ti, t0=t0, T=T):
                    store_tile(b, ti, t0, T)
                    if b + 1 < nb:
                        load_tile(b + 1, ti, t0, T, False)
                deferred.append(fin)
            return g()

        pipeline(4, l1_front, l1_ln1, l1_back, l1_ln2)
    flush_deferred()
    import os
    if os.environ.get("KDBG_LABELS"):
        P.dbg = []
    P.emit()
    if os.environ.get("KDBG_LABELS"):
        import json
        json.dump(P.dbg, open(os.environ["KDBG_LABELS"], "w"))
    return P


_CACHE = {}


def kernel(**inputs):
    nb = 4
    if "P" not in _CACHE:
        _CACHE["P"] = build(nb)
    P = _CACHE["P"]
    shared = prepare_shared(inputs)
    in_maps = []
    for core in range(NCORES):
        m = dict(shared)
        m.update(prepare_core(inputs, core, nb))
        in_maps.append(m)
    res = run_bass_kernel_spmd(P.nc, in_maps, core_ids=list(range(NCORES)))
    outs = []
    for core in range(NCORES):
        o = np.asarray(res.results[core]["outT"])
        outs.append(o.transpose(0, 3, 2, 1).reshape(nb, S, D))
    return np.ascontiguousarray(np.concatenate(outs, axis=0), dtype=np.float32)
```
